# Optimizing a Trainium2 kernel written in Bass

```python
import math
import jax, jax.numpy as jnp
from jax import lax
import numpy as np

D_MODEL = 1024
BATCH = 32
SEQ = 2048
DEPTH = 4
DEC_BATCH = 8
DEC_SEQ = 2048
PAST_LEN = 128

N_MIXERS = 4
GRID_W = 64
Q_BLOCK = 128
NORM_EPS = 1e-6
A_HEADS = 16
A_KV_HEADS = 4
A_HEAD_DIM = D_MODEL // A_HEADS
A_GROUP = A_HEADS // A_KV_HEADS
ROPE_THETA = 10000.0
POOL_WINDOWS = (2, 4, 8, 16)
POOL_GROUP = D_MODEL // len(POOL_WINDOWS)
C_HEADS = 8
C_HEAD_DIM = D_MODEL // (2 * C_HEADS)
F_GROUPS = 4
F_GROUP_DIM = D_MODEL // F_GROUPS
D_FF = -(-8 * D_MODEL // (3 * 256)) * 256
N_A = len(range(0, DEPTH, N_MIXERS))
N_B = len(range(1, DEPTH, N_MIXERS))
N_C = len(range(2, DEPTH, N_MIXERS))
N_D = len(range(3, DEPTH, N_MIXERS))

kernel_name = "hybrid_interleaved_bidir_encoder"

F32 = jnp.float32


def rmsnorm(x, gain):
    xf = x.astype(F32)
    y = xf * lax.rsqrt(jnp.mean(xf * xf, axis=-1, keepdims=True) + NORM_EPS)
    return (y * gain.astype(F32)).astype(x.dtype)


def axial_rope_angles(T):
    rows = T // GRID_W
    r, c = jnp.meshgrid(jnp.arange(rows), jnp.arange(GRID_W), indexing="ij")
    r = r.reshape(-1).astype(F32)
    c = c.reshape(-1).astype(F32)
    n = A_HEAD_DIM // 4
    freqs = ROPE_THETA ** (-jnp.arange(n, dtype=F32) / n)
    ang = jnp.concatenate([r[:, None] * freqs, c[:, None] * freqs], axis=-1)
    return jnp.cos(ang), jnp.sin(ang)


def apply_rope(x, cos, sin):
    xp = x.astype(F32).reshape(x.shape[:-1] + (x.shape[-1] // 2, 2))
    xe, xo = xp[..., 0], xp[..., 1]
    out = jnp.stack([xe * cos - xo * sin, xe * sin + xo * cos], axis=-1)
    return out.reshape(x.shape).astype(x.dtype)


def mixer_gqa_axial(h, w_qkv, q_gain, k_gain, w_o):
    B, T, _ = h.shape
    nblk = T // Q_BLOCK
    qkv = h @ w_qkv
    q, k, v = jnp.split(qkv, [A_HEADS * A_HEAD_DIM, (A_HEADS + A_KV_HEADS) * A_HEAD_DIM], axis=-1)
    q = rmsnorm(q.reshape(B, T, A_KV_HEADS, A_GROUP, A_HEAD_DIM), q_gain)
    k = rmsnorm(k.reshape(B, T, A_KV_HEADS, A_HEAD_DIM), k_gain)
    v = v.reshape(B, T, A_KV_HEADS, A_HEAD_DIM)
    cos, sin = axial_rope_angles(T)
    q = apply_rope(q, cos[:, None, None], sin[:, None, None]) * (A_HEAD_DIM ** -0.5)
    k = apply_rope(k, cos[:, None], sin[:, None])
    qb = q.reshape(B, nblk, Q_BLOCK, A_KV_HEADS, A_GROUP, A_HEAD_DIM).swapaxes(0, 1)

    def attend(qi):
        s = jnp.einsum("bqkgd,bskd->bkgqs", qi, k, preferred_element_type=F32)
        p = jax.nn.softmax(s, axis=-1).astype(v.dtype)
        return jnp.einsum("bkgqs,bskd->bqkgd", p, v)

    o = lax.map(attend, qb).swapaxes(0, 1).reshape(B, T, A_HEADS * A_HEAD_DIM)
    return o @ w_o


def mixer_pool(h, w_pool, scale):
    B, T, D = h.shape
    hf = h.astype(F32)
    csum = jnp.concatenate([jnp.zeros((B, 1, D), F32), lax.cumsum(hf, axis=1)], axis=1)
    t = jnp.arange(T)
    parts = []
    for g, win in enumerate(POOL_WINDOWS):
        sl = slice(g * POOL_GROUP, (g + 1) * POOL_GROUP)
        lo = jnp.clip(t - win // 2, 0, T - 1)
        hi = jnp.clip(t + win // 2 - 1, 0, T - 1)
        cg = csum[..., sl]
        cnt = (hi - lo + 1).astype(F32)[:, None]
        mean = (jnp.take(cg, hi + 1, axis=1) - jnp.take(cg, lo, axis=1)) / cnt
        parts.append(mean - hf[..., sl])
    p = jnp.stack(parts, axis=2).astype(h.dtype)
    y = jnp.einsum("btgc,gce->btge", p, w_pool).reshape(B, T, D)
    return y * scale


def mixer_diff_attn(h, w_qkv, lam_params, sub_gain, w_o, lam_init):
    B, T, _ = h.shape
    nblk = T // Q_BLOCK
    width = C_HEADS * 2 * C_HEAD_DIM
    q, k, v = jnp.split(h @ w_qkv, [width, 2 * width], axis=-1)
    q = q.reshape(B, T, C_HEADS, 2, C_HEAD_DIM) * (C_HEAD_DIM ** -0.5)
    k = k.reshape(B, T, C_HEADS, 2, C_HEAD_DIM)
    v = v.reshape(B, T, C_HEADS, 2 * C_HEAD_DIM)
    lp = lam_params.astype(F32)
    lam = jnp.exp(jnp.sum(lp[0] * lp[1])) - jnp.exp(jnp.sum(lp[2] * lp[3])) + lam_init
    slopes = 2.0 ** (-(8.0 / C_HEADS) * jnp.arange(1, C_HEADS + 1, dtype=F32))
    pos = jnp.arange(T)
    qb = q.reshape(B, nblk, Q_BLOCK, C_HEADS, 2, C_HEAD_DIM).swapaxes(0, 1)
    tb = pos.reshape(nblk, Q_BLOCK)

    def attend(args):
        qi, tq = args
        s = jnp.einsum("bqhcd,bshcd->bhcqs", qi, k, preferred_element_type=F32)
        dist = jnp.abs(tq[:, None] - pos[None, :]).astype(F32)
        s = s - (slopes[:, None, None, None] * dist)[None]
        p = jax.nn.softmax(s, axis=-1)
        pd = (p[:, :, 0] - lam * p[:, :, 1]).astype(v.dtype)
        return jnp.einsum("bhqs,bshe->bqhe", pd, v)

    o = lax.map(attend, (qb, tb)).swapaxes(0, 1).reshape(B, T, C_HEADS, 2 * C_HEAD_DIM)
    o = rmsnorm(o, sub_gain) * (1.0 - lam_init)
    return o.reshape(B, T, C_HEADS * 2 * C_HEAD_DIM) @ w_o


def mixer_fourier(h, w_o):
    B, T, D = h.shape
    hg = h.astype(F32).reshape(B, T, F_GROUPS, F_GROUP_DIM)
    f = jnp.fft.fft2(hg, axes=(1, 3), norm="ortho").real
    return f.reshape(B, T, D).astype(h.dtype) @ w_o


def swiglu(h, w_gate, w_up, w_down):
    return (jax.nn.silu(h @ w_gate) * (h @ w_up)) @ w_down


def encoder_trunk(x, attn_norm, ffn_norm, final_norm, a_w_qkv, a_q_gain, a_k_gain, a_w_o,
                  b_w_pool, b_scale, c_w_qkv, c_lambda, c_sub_gain, c_w_o, d_w_o,
                  w_gate, w_up, w_down):
    for i in range(DEPTH):
        kind, j = i % N_MIXERS, i // N_MIXERS
        h = rmsnorm(x, attn_norm[i])
        if kind == 0:
            y = mixer_gqa_axial(h, a_w_qkv[j], a_q_gain[j], a_k_gain[j], a_w_o[j])
        elif kind == 1:
            y = mixer_pool(h, b_w_pool[j], b_scale[j])
        elif kind == 2:
            lam_init = 0.8 - 0.6 * math.exp(-0.3 * i)
            y = mixer_diff_attn(h, c_w_qkv[j], c_lambda[j], c_sub_gain[j], c_w_o[j], lam_init)
        else:
            y = mixer_fourier(h, d_w_o[j])
        x = x + y
        x = x + swiglu(rmsnorm(x, ffn_norm[i]), w_gate[i], w_up[i], w_down[i])
    return rmsnorm(x, final_norm)


def setup_inputs(seed: int = 0) -> dict:
    key = jax.random.key(seed)
    ks = jax.random.split(key, 20)

    def nrm(k, shape, fan_in):
        return jax.random.normal(k, shape, F32) * (fan_in ** -0.5)

    def gain(k, shape):
        return 1.0 + 0.1 * jax.random.normal(k, shape, F32)

    a_qkv_w = (A_HEADS + 2 * A_KV_HEADS) * A_HEAD_DIM
    c_w = C_HEADS * 2 * C_HEAD_DIM
    return {
        "x_prompt": jax.random.normal(ks[0], (BATCH, SEQ, D_MODEL), F32),
        "x_sample": jax.random.normal(ks[1], (DEC_BATCH, DEC_SEQ, D_MODEL), F32),
        "attn_norm": gain(ks[2], (DEPTH, D_MODEL)),
        "ffn_norm": gain(ks[3], (DEPTH, D_MODEL)),
        "final_norm": gain(ks[4], (D_MODEL,)),
        "a_w_qkv": nrm(ks[5], (N_A, D_MODEL, a_qkv_w), D_MODEL),
        "a_q_gain": gain(ks[6], (N_A, A_HEAD_DIM)),
        "a_k_gain": gain(ks[7], (N_A, A_HEAD_DIM)),
        "a_w_o": nrm(ks[8], (N_A, A_HEADS * A_HEAD_DIM, D_MODEL), A_HEADS * A_HEAD_DIM),
        "b_w_pool": nrm(ks[9], (N_B, len(POOL_WINDOWS), POOL_GROUP, POOL_GROUP), POOL_GROUP),
        "b_scale": gain(ks[10], (N_B, D_MODEL)),
        "c_w_qkv": nrm(ks[11], (N_C, D_MODEL, 3 * c_w), D_MODEL),
        "c_lambda": 0.1 * jax.random.normal(ks[12], (N_C, 4, C_HEAD_DIM), F32),
        "c_sub_gain": gain(ks[13], (N_C, 2 * C_HEAD_DIM)),
        "c_w_o": nrm(ks[14], (N_C, c_w, D_MODEL), c_w),
        "d_w_o": nrm(ks[15], (N_D, D_MODEL, D_MODEL), D_MODEL),
        "w_gate": nrm(ks[16], (DEPTH, D_MODEL, D_FF), D_MODEL),
        "w_up": nrm(ks[17], (DEPTH, D_MODEL, D_FF), D_MODEL),
        "w_down": nrm(ks[18], (DEPTH, D_FF, D_MODEL), D_FF),
    }


def reference(x_prompt, x_sample, attn_norm, ffn_norm, final_norm, a_w_qkv, a_q_gain, a_k_gain,
              a_w_o, b_w_pool, b_scale, c_w_qkv, c_lambda, c_sub_gain, c_w_o, d_w_o,
              w_gate, w_up, w_down):
    y_prompt = encoder_trunk(x_prompt, attn_norm, ffn_norm, final_norm, a_w_qkv, a_q_gain,
                             a_k_gain, a_w_o, b_w_pool, b_scale, c_w_qkv, c_lambda, c_sub_gain,
                             c_w_o, d_w_o, w_gate, w_up, w_down)
    y_sample = encoder_trunk(x_sample, attn_norm, ffn_norm, final_norm, a_w_qkv, a_q_gain,
                             a_k_gain, a_w_o, b_w_pool, b_scale, c_w_qkv, c_lambda, c_sub_gain,
                             c_w_o, d_w_o, w_gate, w_up, w_down)
    return (y_prompt, y_sample)
```

```python
import bisect
import math
from contextlib import ExitStack

import ml_dtypes
import numpy as np

import concourse.bass as bass
import concourse.mybir as mybir
from concourse.bass_utils import run_bass_kernel_spmd

F32 = mybir.dt.float32
BF16 = mybir.dt.bfloat16
ALU = mybir.AluOpType
AF = mybir.ActivationFunctionType
AX = mybir.AxisListType

T = 2048
D = 1024
NCH = 8
TB = 512
NB = 4
DFF = 2816
EPS = 1e-6
NCORES = 8
SEQ_PER_CORE = 5
LAM_INIT = 0.8 - 0.6 * math.exp(-0.3 * 2)
WSLOT = 1408
SKEW = 5
ALIBI_SKIP = 40.0
NWSLOT = 6
NTSLOT = 6
ARENA_W = 20480

C_ATTN = 0
C_FFN = 32
C_FINAL = 64
C_BSCALE = 72
C_GQ, C_GQS, C_GK, C_GKS, C_SUBG = 80, 81, 82, 83, 84
NCOL = 88


class Buf:
    __slots__ = ("lw", "rd", "const", "psum")

    def __init__(self, const=False, psum=False):
        self.lw = None
        self.rd = []
        self.const = const
        self.psum = psum


class Sched:
    ENGS = ("pe", "act", "dve", "pool", "sp")

    def __init__(self, nc):
        self.nc = nc
        self.ops = {e: [] for e in self.ENGS}
        self.seq = {e: 0 for e in self.ENGS}
        self.needed = {e: set() for e in self.ENGS}
        self.known = {e: {} for e in self.ENGS}
        self.dma_cnt = {}

    def _reduce(self, eng, toks, same_toks=()):
        best = {}
        for t in toks:
            kind, key, val = t
            if kind == "e" and key == eng:
                continue
            k = (kind, key)
            if val > best.get(k, -1):
                best[k] = val
        if eng in ("act", "dve", "pool"):
            for t in same_toks:
                if t is not None and t[0] == "e" and t[1] == eng:
                    k = ("e", eng)
                    if t[2] > best.get(k, -1):
                        best[k] = t[2]
        waits = []
        kn = self.known[eng]
        for k, val in best.items():
            if kn.get(k, -1) >= val:
                continue
            kn[k] = val
            waits.append((k[0], k[1], val))
            if k[0] == "e":
                self.needed[k[1]].add(val)
        return waits

    def _collect(self, eng, reads, writes):
        toks = []
        for b in reads:
            if b.lw is not None:
                toks.append(b.lw)
            if b.psum:
                toks.extend(b.rd)
        for b in writes:
            if b.lw is not None:
                toks.append(b.lw)
            toks.extend(b.rd)
        return self._reduce(eng, toks, toks)

    def _mark(self, tok, reads, writes):
        for b in writes:
            b.lw = tok
            b.rd = []
        for b in reads:
            if not b.const:
                b.rd.append(tok)

    def op(self, eng, fn, reads=(), writes=()):
        waits = self._collect(eng, reads, writes)
        self.seq[eng] += 1
        s = self.seq[eng]
        tok = ("e", eng, s)
        self.ops[eng].append((waits, fn, s, None))
        self._mark(tok, reads, writes)
        return tok

    def dma(self, eng, sem, fn, reads=(), writes=()):
        waits = self._collect(eng, reads, writes)
        self.dma_cnt[sem] = self.dma_cnt.get(sem, 0) + 16
        tok = ("d", sem, self.dma_cnt[sem])
        self.seq[eng] += 1
        self.ops[eng].append((waits, fn, self.seq[eng], sem))
        self._mark(tok, reads, writes)
        return tok

    def wait(self, eng, toks):
        waits = self._reduce(eng, [t for t in toks if t is not None])
        if waits:
            self.ops[eng].append((waits, None, None, None))

    def last_tok(self, eng):
        return ("e", eng, self.seq[eng]) if self.seq[eng] > 0 else None

    def barrier(self):
        toks = [self.last_tok(e) for e in ("pe", "act", "dve")]
        for e in ("pe", "act", "dve", "sp"):
            self.wait(e, toks)

    def emit(self, stack):
        nc = self.nc
        esem = {e: stack.enter_context(nc.semaphore("s_" + e)) for e in self.ENGS}
        dsem = {n: stack.enter_context(nc.semaphore("d_" + n)) for n in self.dma_cnt}
        ranks = {e: sorted(self.needed[e]) for e in self.ENGS}
        block = stack.enter_context(nc.Block())

        def run(e, handle):
            needed = self.needed[e]
            for waits, fn, s, dma in self.ops[e]:
                for kind, key, val in waits:
                    if kind == "e":
                        handle.wait_ge(esem[key], bisect.bisect_right(ranks[key], val))
                    else:
                        handle.wait_ge(dsem[key], val)
                if fn is None:
                    continue
                ins = fn(handle)
                if dma is not None:
                    ins.then_inc(dsem[dma], 16)
                elif s in needed:
                    ins.then_inc(esem[e], 1)

        if self.ops["pe"]:
            @block.tensor
            def _(h):
                run("pe", h)
        if self.ops["act"]:
            @block.scalar
            def _(h):
                run("act", h)
        if self.ops["dve"]:
            @block.vector
            def _(h):
                run("dve", h)
        if self.ops["pool"]:
            @block.gpsimd
            def _(h):
                run("pool", h)
        if self.ops["sp"]:
            @block.sync
            def _(h):
                run("sp", h)


def MM(out, lhsT, rhs, start, stop):
    return lambda e: e.matmul(out, lhsT, rhs, start=bool(start), stop=bool(stop))


def ACT(out, in_, func, bias=0.0, scale=1.0):
    return lambda e: e.activation(out, in_, func, bias=bias, scale=scale)


def TT(out, in0, in1, op):
    return lambda e: e.tensor_tensor(out, in0, in1, op)


def STT(out, in0, scalar, in1, op0, op1):
    return lambda e: e.scalar_tensor_tensor(out, in0, scalar, in1, op0, op1)


def TS(out, in0, s1, s2, op0, op1=None):
    if op1 is None:
        return lambda e: e.tensor_scalar(out, in0, s1, None, op0)
    return lambda e: e.tensor_scalar(out, in0, s1, s2, op0, op1)


def CP(out, in_):
    return lambda e: e.tensor_copy(out, in_)


def SMUL(out, in_, c):
    return lambda e: e.mul(out, in_, c)


def SCOPY(out, in_):
    return lambda e: e.copy(out, in_)


def RCP(out, in_):
    return lambda e: e.reciprocal(out, in_)


def MSET(ap, v):
    return lambda e: e.memset(ap, v)


def DMA(out, in_):
    return lambda e: e.dma_start(out=out, in_=in_)


class Ring:
    def __init__(self, S, eng, prefix, slot_ap_fn, nslots, stream):
        self.S, self.eng, self.prefix = S, eng, prefix
        self.slot_ap = slot_ap_fn
        self.n = nslots
        self.stream = stream
        self.pos = 0
        self.loaded = 0
        self.bufs = [Buf() for _ in range(nslots)]

    def get(self, key):
        i = self.pos
        k, _, ncols = self.stream[i]
        assert k == key, (k, key)
        lim = min(len(self.stream), i + self.n - 1)
        while self.loaded < lim:
            j = self.loaded
            _, src, nc_ = self.stream[j]
            slot = j % self.n
            self.S.dma(self.eng, "%s%d" % (self.prefix, slot), DMA(self.slot_ap(slot, nc_), src),
                       writes=[self.bufs[slot]])
            self.loaded += 1
        self.pos += 1
        slot = i % self.n
        return self.slot_ap(slot, ncols), self.bufs[slot]


def ffn_catalog(l):
    L = []
    for G in range(2):
        for j in range(11):
            jj = G * 11 + j
            L += [("gate", l, jj), ("up", l, jj)]
        for m in range(8):
            L.append(("down", l, m, G))
    return L


def catalog(layers=(0, 1, 2, 3)):
    L = []
    for l in layers:
        if l == 0:
            for g in range(4):
                L += [("aq", g, 0, 0), ("aq", g, 1, 0), ("ak", g, 0), ("av", g)]
                L += [("awo", g, m) for m in range(8)]
        elif l == 1:
            L += [("bp", g) for g in range(4)]
        elif l == 2:
            for hp in range(4):
                for hl in range(2):
                    h = 2 * hp + hl
                    L += [("cq", h), ("ck", h), ("cv", h)]
                L += [("cwo", hp, m) for m in range(8)]
        else:
            L += [("dwo", m) for m in range(8)]
        L += ffn_catalog(l)
    return L


def tile_cols(key):
    k = key[0]
    if k in ("gate", "up", "aq", "ak", "cq", "ck", "cv", "dwo"):
        return 1024
    if k == "down":
        return 1408
    if k in ("av", "bp"):
        return 512
    if k in ("awo", "cwo"):
        return 256
    raise KeyError(key)


def _kc_tile(w):
    K, M = w.shape
    return np.ascontiguousarray(w.reshape(K // 128, 128, M).transpose(1, 0, 2)).reshape(128, -1)


_SWAP64 = np.arange(64) ^ 1


def extract_tile(key, W):
    k = key[0]
    if k == "gate":
        return _kc_tile(W["w_gate"][key[1]][:, key[2] * 128:(key[2] + 1) * 128])
    if k == "up":
        return _kc_tile(W["w_up"][key[1]][:, key[2] * 128:(key[2] + 1) * 128])
    if k == "down":
        _, l, m, G = key
        return _kc_tile(W["w_down"][l][G * 1408:(G + 1) * 1408, m * 128:(m + 1) * 128])
    if k == "aq":
        _, g, cc, sw = key
        cols = (2 * g + cc) * 128 + np.arange(128)
        if sw:
            cols = cols ^ 1
        return _kc_tile(W["a_w_qkv"][0][:, cols])
    if k == "ak":
        _, g, sw = key
        c64 = np.arange(64)
        if sw:
            c64 = c64 ^ 1
        cols = 1024 + g * 64 + np.concatenate([c64, c64])
        return _kc_tile(W["a_w_qkv"][0][:, cols])
    if k == "av":
        g = key[1]
        cols = 1280 + g * 64 + np.arange(64)
        return _kc_tile(W["a_w_qkv"][0][:, cols])
    if k == "awo":
        _, g, m = key
        return _kc_tile(W["a_w_o"][0][g * 256:(g + 1) * 256, m * 128:(m + 1) * 128])
    if k == "bp":
        g = key[1]
        return _kc_tile(W["b_w_pool"][0][g])
    if k == "cq":
        h = key[1]
        return _kc_tile(W["c_w_qkv"][0][:, h * 128:(h + 1) * 128])
    if k == "ck":
        h = key[1]
        return _kc_tile(W["c_w_qkv"][0][:, 1024 + h * 128:1024 + (h + 1) * 128])
    if k == "cv":
        h = key[1]
        return _kc_tile(W["c_w_qkv"][0][:, 2048 + h * 128:2048 + (h + 1) * 128])
    if k == "cwo":
        _, hp, m = key
        return _kc_tile(W["c_w_o"][0][hp * 256:(hp + 1) * 256, m * 128:(m + 1) * 128])
    if k == "dwo":
        m = key[1]
        return _kc_tile(W["d_w_o"][0][:, m * 128:(m + 1) * 128])
    raise KeyError(key)


def weight_offsets(layers):
    offs = {}
    o = 0
    for key in catalog(layers):
        if key not in offs:
            offs[key] = o
            o += tile_cols(key)
    return offs, o


def const_tables():
    tabs = {}
    p = np.arange(128)
    d = p % 64
    i = d // 2
    t = np.arange(T)
    r = (t // 64).astype(np.float32)
    c = (t % 64).astype(np.float32)
    n = 16
    freqs = (np.float32(10000.0) ** (-np.arange(n, dtype=np.float32) / np.float32(n))).astype(np.float32)
    ang = np.where((i < 16)[:, None], r[None, :] * freqs[np.minimum(i, 15)][:, None],
                   c[None, :] * freqs[np.maximum(i - 16, 0)][:, None]).astype(np.float32)
    cosv = np.cos(ang.astype(np.float64))
    sinv = np.sin(ang.astype(np.float64))
    sgn = np.where(d % 2 == 0, -1.0, 1.0)[:, None]
    tabs["rope"] = np.concatenate([cosv, sinv * sgn], axis=1).astype(np.float32)
    cc = np.arange(4096)
    dist = np.abs(cc[None, :] - 2048 - p[:, None]).astype(np.float64)
    tabs["dec"] = np.concatenate([np.exp(-(2.0 ** (-(h + 1))) * dist) for h in range(8)], axis=1).astype(ml_dtypes.bfloat16)
    ic = np.zeros((4, T), np.float64)
    for gi, win in enumerate((2, 4, 8, 16)):
        lo = np.clip(t - win // 2, 0, T - 1)
        hi = np.clip(t + win // 2 - 1, 0, T - 1)
        ic[gi] = 1.0 / (hi - lo + 1)
    tabs["icnt"] = np.ascontiguousarray(np.broadcast_to(ic.reshape(1, 4 * T), (128, 4 * T))).astype(np.float32)
    cidx = (np.arange(2)[None, :, None] * 128 + p[:, None, None])
    e = np.arange(256)[None, None, :]
    a = 2.0 * np.pi * ((cidx * e) % 256) / 256.0
    csc = np.concatenate([np.cos(a), np.sin(a)], axis=2) / 16.0
    tabs["csc"] = csc.reshape(128, 1024).astype(ml_dtypes.bfloat16)
    pm = np.zeros((128, 128), np.float32)
    pm[np.arange(128) ^ 1, np.arange(128)] = 1.0
    tabs["perm"] = pm.astype(ml_dtypes.bfloat16)
    tt = np.zeros((128, NB, 16, 2, TB), np.float32)
    sc_ = 1.0 / math.sqrt(T)
    for tc in range(16):
        trow = (tc * 128 + p)[:, None]
        ang2 = 2.0 * np.pi * ((trow * t[None, :]) % T) / T
        cm = (np.cos(ang2) * sc_).astype(np.float32).reshape(128, NB, TB)
        sm = (-np.sin(ang2) * sc_).astype(np.float32).reshape(128, NB, TB)
        tt[:, :, tc, 0, :] = cm
        tt[:, :, tc, 1, :] = sm
    tabs["ttab"] = tt.reshape(128, NB * 16 * 2 * TB).astype(ml_dtypes.bfloat16)
    return tabs


def build_cols(inp):
    cols = np.zeros((128, NCOL), np.float32)

    def colify(v):
        return np.asarray(v, np.float32).reshape(8, 128).T

    for l in range(4):
        cols[:, C_ATTN + 8 * l:C_ATTN + 8 * l + 8] = colify(inp["attn_norm"][l])
        cols[:, C_FFN + 8 * l:C_FFN + 8 * l + 8] = colify(inp["ffn_norm"][l])
    cols[:, C_FINAL:C_FINAL + 8] = colify(inp["final_norm"])
    cols[:, C_BSCALE:C_BSCALE + 8] = colify(inp["b_scale"][0])
    d = np.arange(128) % 64
    cols[:, C_GQ] = inp["a_q_gain"][0][d]
    cols[:, C_GQS] = inp["a_q_gain"][0][d ^ 1]
    cols[:, C_GK] = inp["a_k_gain"][0][d]
    cols[:, C_GKS] = inp["a_k_gain"][0][d ^ 1]
    cols[:, C_SUBG] = inp["c_sub_gain"][0]
    return cols


def build_program(nseq, layers=(0, 1, 2, 3), do_ffn=True, final_norm=True, debug=False):
    nc = bass.Bass("TRN2", target_bir_lowering=False)
    dbg_n = [0]
    offs, wcols = weight_offsets(layers)
    xin = nc.dram_tensor("xin", [nseq, D, T], F32, kind="ExternalInput").ap()
    yout = nc.dram_tensor("yout", [nseq, D, T], F32, kind="ExternalOutput").ap()
    wts = nc.dram_tensor("wts", [128, wcols], F32, kind="ExternalInput").ap()
    d_cols = nc.dram_tensor("cols", [128, NCOL], F32, kind="ExternalInput").ap()
    d_lamb = nc.dram_tensor("lamb", [128, 256], F32, kind="ExternalInput").ap()
    d_rope = nc.dram_tensor("rope", [128, 4096], F32, kind="ExternalInput").ap()
    d_dec = nc.dram_tensor("dec", [128, 8 * 4096], BF16, kind="ExternalInput").ap()
    d_icnt = nc.dram_tensor("icnt", [128, 4 * T], F32, kind="ExternalInput").ap()
    d_csc = nc.dram_tensor("csc", [128, 1024], BF16, kind="ExternalInput").ap()
    d_perm = nc.dram_tensor("perm", [128, 128], BF16, kind="ExternalInput").ap()
    d_ttab = nc.dram_tensor("ttab", [128, NB * 16 * 2 * TB], BF16, kind="ExternalInput").ap()

    st = ExitStack()
    with st:
        xT = st.enter_context(nc.sbuf_tensor("xT", [128, NCH * T], F32))
        hT = st.enter_context(nc.sbuf_tensor("hT", [128, NCH * T], BF16))
        wring = st.enter_context(nc.sbuf_tensor("wring", [128, NWSLOT * WSLOT], BF16))
        tring = st.enter_context(nc.sbuf_tensor("tring", [128, NTSLOT * TB], BF16))
        cols = st.enter_context(nc.sbuf_tensor("colst", [128, NCOL], F32))
        dyn = st.enter_context(nc.sbuf_tensor("dyn", [128, 8], F32))
        lamt = st.enter_context(nc.sbuf_tensor("lamt", [128, 256], F32))
        lamp = st.enter_context(nc.sbuf_tensor("lamp", [128, 128], F32))
        ones = st.enter_context(nc.sbuf_tensor("ones", [128, 128], BF16))
        bdiag = st.enter_context(nc.sbuf_tensor("bdiag", [128, 128], BF16))
        permt = st.enter_context(nc.sbuf_tensor("permt", [128, 128], BF16))
        arena = st.enter_context(nc.sbuf_tensor("arena", [128, ARENA_W], F32))
        P = [st.enter_context(nc.psum_tensor("ps%d" % i, [128, TB], F32)) for i in range(8)]

        S = Sched(nc)
        pb = [Buf(psum=True) for _ in range(8)]
        xb = [[Buf() for _ in range(NB)] for _ in range(NCH)]
        hb = [[Buf() for _ in range(NB)] for _ in range(NCH)]
        colsB = Buf(const=True)
        constB = Buf(const=True)
        lamB = Buf()

        def x_ap(c, n):
            return xT[:, c * T + n * TB:c * T + (n + 1) * TB]

        def h_ap(c, n):
            return hT[:, c * T + n * TB:c * T + (n + 1) * TB]

        def h_tok(c, tc):
            return hT[:, c * T + tc * 128:c * T + (tc + 1) * 128]

        def col(i):
            return cols[:, i:i + 1]

        def af(o, n):
            return arena[:, o:o + n]

        def ab(o, n):
            return arena[:, o:o + n].bitcast(BF16)

        sqb = [ab(256 * i, 256) for i in range(4)]
        sqB = [Buf() for _ in range(4)]
        tsq = [af(1024 + 512 * i, 512) for i in range(2)]
        tsqB = [Buf() for _ in range(2)]
        rstd = [af(2048 + 512 * i, 512) for i in range(2)]
        rstdB = [Buf() for _ in range(2)]
        A0 = 3072

        cat = catalog(layers) if do_ffn else [k for k in catalog(layers) if k[0] not in ("gate", "up", "down")]
        wstream = []
        for _ in range(nseq):
            for key in cat:
                ncl = tile_cols(key)
                wstream.append((key, wts[:, offs[key]:offs[key] + ncl], ncl))
        WR = Ring(S, "pool", "w", lambda s, n_: wring[:, s * WSLOT:s * WSLOT + n_], NWSLOT, wstream)
        tstream = []
        if 3 in layers:
            for _ in range(nseq):
                for n in range(NB):
                    for tc in range(16):
                        for cs in range(2):
                            o = ((n * 16 + tc) * 2 + cs) * TB
                            tstream.append(((n, tc, cs), d_ttab[:, o:o + TB], TB))
        TR = Ring(S, "sp", "t", lambda s, n_: tring[:, s * TB:s * TB + n_], NTSLOT, tstream)

        ctr = {"st": 0, "e": 0, "sb": 0, "nb": 0}
        deferred = []

        S.dma("sp", "c0", DMA(cols[:], d_cols[:, :]), writes=[colsB])
        S.dma("sp", "c1", DMA(lamt[:], d_lamb[:, :]), writes=[lamB])
        S.dma("sp", "c2", DMA(permt[:], d_perm[:, :]), writes=[constB])
        S.op("dve", MSET(ones[:], 1.0), writes=[constB])
        S.op("dve", MSET(bdiag[:], 0.0), writes=[constB])
        S.op("dve", MSET(bdiag[0:64, 0:64], 1.0), writes=[constB])
        S.op("dve", MSET(bdiag[64:128, 64:128], 1.0), writes=[constB])
        lpB = Buf()
        S.op("dve", TT(lamp[:, 0:64], lamt[:, 0:64], lamt[:, 64:128], ALU.mult), reads=[lamB], writes=[lpB])
        S.op("dve", TT(lamp[:, 64:128], lamt[:, 128:192], lamt[:, 192:256], ALU.mult), reads=[lamB], writes=[lpB])
        dynB = Buf()
        S.op("dve", lambda e: e.reduce_sum(dyn[:, 2:3], lamp[:, 0:64], AX.X), reads=[lpB], writes=[dynB])
        S.op("dve", lambda e: e.reduce_sum(dyn[:, 3:4], lamp[:, 64:128], AX.X), reads=[lpB], writes=[dynB])
        S.op("act", ACT(dyn[:, 4:6], dyn[:, 2:4], AF.Exp), reads=[dynB], writes=[dynB])
        S.op("dve", TT(dyn[:, 6:7], dyn[:, 4:5], dyn[:, 5:6], ALU.subtract), reads=[dynB], writes=[dynB])
        S.op("dve", TS(dyn[:, 0:1], dyn[:, 6:7], LAM_INIT, -1.0, ALU.add, ALU.mult), reads=[dynB], writes=[dynB])
        S.op("dve", TS(dyn[:, 1:2], col(C_SUBG), 1.0 - LAM_INIT, None, ALU.mult), reads=[dynB, colsB], writes=[dynB])
        dynB.const = True

        def dbg(name, ap, bufs, shape, dt):
            if not debug or dbg_n[0] > 12:
                return
            dbg_n[0] += 1
            dten = nc.dram_tensor("dbg_" + name, shape, dt, kind="ExternalOutput").ap()
            S.dma("sp", "dbg%d" % dbg_n[0], DMA(dten[:, :], ap), reads=bufs)

        def stats_block(n, dst, dstB, inv_n=1.0 / D, src=None):
            bank = 6 + (ctr["nb"] % 2)
            ctr["nb"] += 1
            for c in range(NCH):
                q = c % 4
                S.op("act", ACT(sqb[q], x_ap(c, n), AF.Square), reads=[xb[c][n]], writes=[sqB[q]])
                S.op("pe", MM(P[bank][:, :], ones[:, :], sqb[q], c == 0, c == NCH - 1),
                     reads=[sqB[q], constB], writes=[pb[bank]])
            tq = ctr["nb"] % 2
            S.op("act", ACT(tsq[tq], P[bank][:, :], AF.Ln, bias=EPS, scale=inv_n), reads=[pb[bank]], writes=[tsqB[tq]])
            S.op("act", ACT(dst, tsq[tq], AF.Exp, scale=-0.5), reads=[tsqB[tq]], writes=[dstB])

        def norm_to_h(gcol0):
            for n in range(NB):
                r = n % 2
                stats_block(n, rstd[r], rstdB[r])
                for c in range(NCH):
                    S.op("dve", STT(h_ap(c, n), x_ap(c, n), col(gcol0 + c), rstd[r], ALU.mult, ALU.mult),
                         reads=[xb[c][n], rstdB[r], colsB], writes=[hb[c][n]])

        def add_to_x(m, base, scale_col=None):
            for n in range(NB):
                if scale_col is None:
                    S.op("dve", TT(x_ap(m, n), x_ap(m, n), P[base + n][:, :], ALU.add),
                         reads=[pb[base + n], xb[m][n]], writes=[xb[m][n]])
                else:
                    S.op("dve", STT(x_ap(m, n), P[base + n][:, :], scale_col, x_ap(m, n), ALU.mult, ALU.add),
                         reads=[pb[base + n], xb[m][n], colsB], writes=[xb[m][n]])

        def wo_partial(keyfn, srcs, srcBs):
            nk = len(srcs)
            for m in range(NCH):
                wt, wB = WR.get(keyfn(m))
                base = 0 if m % 2 == 0 else 4
                for kc in range(nk):
                    for n in range(NB):
                        S.op("pe", MM(P[base + n][:, :], wt[:, kc * 128:(kc + 1) * 128], srcs[kc](n), kc == 0, kc == nk - 1),
                             reads=[wB, srcBs[kc][n]], writes=[pb[base + n]])
                add_to_x(m, base)

        def flat_pipeline(tiles, s_fn, pv_fn, skew=SKEW):
            hs = {}
            nt = len(tiles)
            for t in range(nt + skew):
                if t < nt:
                    hs[t] = s_fn(tiles[t])
                if t >= skew:
                    pv_fn(tiles[t - skew], hs.pop(t - skew))
                    for d_ in list(deferred):
                        d_[0] -= 1
                        if d_[0] <= 0:
                            deferred.remove(d_)
                            d_[1]()
            while deferred:
                d_ = deferred.pop(0)
                d_[1]()

        def ffn(l):
            S.barrier()
            norm_to_h(C_FFN + 8 * l)
            act_o = A0
            actv = ab(act_o, 11264)
            actB = [[Buf() for _ in range(NB)] for _ in range(11)]
            sg_o = A0 + 11264
            sg = [af(sg_o + 512 * i, 512) for i in range(4)]
            sgB = [Buf() for _ in range(4)]

            def act_ap(j, n):
                return actv[:, j * T + n * TB:j * T + (n + 1) * TB]

            for G in range(2):
                for j in range(11):
                    jj = G * 11 + j
                    gt, gB = WR.get(("gate", l, jj))
                    ut, uB = WR.get(("up", l, jj))
                    for kc in range(NCH):
                        for n in range(NB):
                            S.op("pe", MM(P[n][:, :], gt[:, kc * 128:(kc + 1) * 128], h_ap(kc, n), kc == 0, kc == NCH - 1),
                                 reads=[gB, hb[kc][n]], writes=[pb[n]])
                    for kc in range(NCH):
                        for n in range(NB):
                            S.op("pe", MM(P[4 + n][:, :], ut[:, kc * 128:(kc + 1) * 128], h_ap(kc, n), kc == 0, kc == NCH - 1),
                                 reads=[uB, hb[kc][n]], writes=[pb[4 + n]])
                    for n in range(NB):
                        S.op("act", ACT(sg[n], P[n][:, :], AF.Silu), reads=[pb[n]], writes=[sgB[n]])
                        S.op("dve", TT(act_ap(j, n), sg[n], P[4 + n][:, :], ALU.mult),
                             reads=[sgB[n], pb[4 + n]], writes=[actB[j][n]])
                for m in range(NCH):
                    dt_, dB = WR.get(("down", l, m, G))
                    base = 0 if m % 2 == 0 else 4
                    for j in range(11):
                        for n in range(NB):
                            S.op("pe", MM(P[base + n][:, :], dt_[:, j * 128:(j + 1) * 128], act_ap(j, n), j == 0, j == 10),
                                 reads=[dB, actB[j][n]], writes=[pb[base + n]])
                    add_to_x(m, base)

        def mixer_a():
            S.barrier()
            o = A0
            ropeC = af(o, 2048); o += 2048
            ropeS = af(o, 2048); o += 2048
            ropeB = Buf(const=True)
            qv = ab(o, 2048); o += 2048
            kv = ab(o, 1024); o += 1024
            vv = ab(o, 1024); o += 1024
            ov = ab(o, 2048); o += 2048
            NEA = 8
            Ev = [ab(o + 256 * i, 256) for i in range(NEA)]; o += 256 * NEA
            EB = [Buf() for _ in range(NEA)]
            sq2 = [ab(o + 256 * i, 256) for i in range(2)]; o += 512
            sq2B = [Buf() for _ in range(2)]
            abf = [ab(o + 256 * i, 256) for i in range(2)]; o += 512
            abfB = [Buf() for _ in range(2)]
            f32t = [af(o + 512 * i, 512) for i in range(7)]; o += 3584
            assert o <= ARENA_W, o
            rs2, t1v, rcv = f32t[0:2], f32t[2:4], f32t[4:6]
            t2v = [f32t[6], f32t[6]]
            rs2B, t1B, rcB = ([Buf() for _ in range(2)] for _ in range(3))
            t2B_ = Buf()
            t2B = [t2B_, t2B_]
            qB = [[Buf() for _ in range(NB)] for _ in range(2)]
            kB = [Buf() for _ in range(NB)]
            vB = Buf()
            oB = [[Buf() for _ in range(NB)] for _ in range(2)]
            S.dma("sp", "tab", DMA(arena[:, A0:A0 + 4096], d_rope[:, :]), writes=[ropeB])
            vv3 = vv.rearrange("p (t e) -> p t e", e=128)
            S.op("dve", MSET(vv3[:, :, 64:128], 1.0), writes=[vB])
            norm_to_h(C_ATTN + 0)

            def q_ap(cc, n):
                return qv[:, cc * T + n * TB:cc * T + (n + 1) * TB]

            def o_ap(cc, n):
                return ov[:, cc * T + n * TB:cc * T + (n + 1) * TB]

            for g in range(4):
                ptiles = [(kind, cc, n) for kind, cc in (("q", 0), ("q", 1), ("k", 0)) for n in range(NB)]
                wcur = {}

                def stage1(i):
                    kind, cc, n = ptiles[i]
                    r = i % 2
                    PA = 0 if r == 0 else 3
                    if n == 0:
                        wcur[0] = WR.get(("aq", g, cc, 0)) if kind == "q" else WR.get(("ak", g, 0))
                    wa, waB = wcur[0]
                    for kc in range(NCH):
                        S.op("pe", MM(P[PA][:, :], wa[:, kc * 128:(kc + 1) * 128], h_ap(kc, n), kc == 0, kc == NCH - 1),
                             reads=[waB, hb[kc][n]], writes=[pb[PA]])
                    S.op("act", ACT(sq2[r], P[PA][:, :], AF.Square), reads=[pb[PA]], writes=[sq2B[r]])
                    S.op("act", SCOPY(abf[r], P[PA][:, :]), reads=[pb[PA]], writes=[abfB[r]])

                def stage2(i):
                    kind, cc, n = ptiles[i]
                    r = i % 2
                    PA, PB_, PR = (0, 1, 2) if r == 0 else (3, 4, 5)
                    if kind == "q":
                        gc, gsc, sc_ = col(C_GQ), col(C_GQS), 0.125
                    else:
                        gc, gsc, sc_ = col(C_GK), col(C_GKS), 1.0
                    S.op("pe", MM(P[PB_][:, :], permt[:, :], abf[r], True, True), reads=[abfB[r], constB], writes=[pb[PB_]])
                    S.op("pe", MM(P[PR][:, :], bdiag[:, :], sq2[r], True, True), reads=[sq2B[r], constB], writes=[pb[PR]])
                    S.op("act", ACT(rs2[r], P[PR][:, :], AF.Ln, bias=EPS, scale=1.0 / 64), reads=[pb[PR]], writes=[rs2B[r]])
                    S.op("act", ACT(rs2[r], rs2[r], AF.Exp, scale=-0.5), reads=[rs2B[r]], writes=[rs2B[r]])
                    S.op("dve", STT(t1v[r], P[PA][:, :], gc, ropeC[:, n * TB:(n + 1) * TB], ALU.mult, ALU.mult),
                         reads=[pb[PA], ropeB, colsB], writes=[t1B[r]])
                    S.op("dve", STT(t2v[r], P[PB_][:, :], gsc, ropeS[:, n * TB:(n + 1) * TB], ALU.mult, ALU.mult),
                         reads=[pb[PB_], ropeB, colsB], writes=[t2B[r]])
                    S.op("dve", TT(t1v[r], t1v[r], t2v[r], ALU.add), reads=[t1B[r], t2B[r]], writes=[t1B[r]])
                    if kind == "q":
                        S.op("dve", STT(q_ap(cc, n), t1v[r], sc_, rs2[r], ALU.mult, ALU.mult), reads=[t1B[r], rs2B[r]], writes=[qB[cc][n]])
                    else:
                        S.op("dve", STT(kv[:, n * TB:(n + 1) * TB], t1v[r], sc_, rs2[r], ALU.mult, ALU.mult),
                             reads=[t1B[r], rs2B[r]], writes=[kB[n]])

                for i in range(len(ptiles) + 1):
                    if i < len(ptiles):
                        stage1(i)
                    if i >= 1:
                        stage2(i - 1)
                wv, wvB = WR.get(("av", g))
                for half in range(2):
                    bank = 6 + half
                    for j in range(8):
                        tc = half * 8 + j
                        for kc in range(NCH):
                            S.op("pe", MM(P[bank][:, j * 64:(j + 1) * 64], h_tok(kc, tc), wv[:, kc * 64:(kc + 1) * 64], kc == 0, kc == NCH - 1),
                                 reads=[wvB, hb[kc][tc // 4]], writes=[pb[bank]])
                    S.op("act", SCOPY(vv3[:, half * 8:(half + 1) * 8, 0:64], P[bank][:, :].rearrange("p (t e) -> p t e", e=64)),
                         reads=[pb[bank]], writes=[vB])
                if g == 0:
                    dbg("q", qv, [qB[0][0], qB[0][1], qB[0][2], qB[0][3], qB[1][0], qB[1][1], qB[1][2], qB[1][3]], [128, 2 * T], BF16)
                    dbg("k", kv, kB, [128, T], BF16)
                    dbg("v", vv, [vB], [128, T], BF16)
                    dbg("h", hT[:, :], [hb[c_][n_] for c_ in range(NCH) for n_ in range(NB)], [128, NCH * T], BF16)
                tiles = []
                for cc in range(2):
                    for n in range(NB):
                        ub = 4 + 2 * (ctr["sb"] % 2)
                        ctr["sb"] += 1
                        for sc in range(16):
                            tiles.append((cc, n, sc, ub))

                def s_fn(tl):
                    cc, n, sc, ub = tl
                    hs = []
                    banks = []
                    for ph in range(2):
                        banks.append(ctr["st"] % 4)
                        ctr["st"] += 1
                    for ph in range(2):
                        psl = slice(ph * 64, (ph + 1) * 64)
                        S.op("pe", MM(P[banks[ph]][:, :], kv[psl, sc * 128:(sc + 1) * 128], qv[psl, cc * T + n * TB:cc * T + (n + 1) * TB], True, True),
                             reads=[kB[sc // 4], qB[cc][n]], writes=[pb[banks[ph]]])
                    for ph in range(2):
                        es = ctr["e"] % NEA
                        ctr["e"] += 1
                        S.op("act", ACT(Ev[es], P[banks[ph]][:, :], AF.Exp), reads=[pb[banks[ph]]], writes=[EB[es]])
                        hs.append(es)
                    return hs

                def pv_fn(tl, hs):
                    cc, n, sc, ub = tl
                    for ph in range(2):
                        U = ub + ph
                        S.op("pe", MM(P[U][:, :], vv3[:, sc, :], Ev[hs[ph]], sc == 0, sc == 15),
                             reads=[EB[hs[ph]], vB], writes=[pb[U]])
                    if sc == 15:
                        for ph in range(2):
                            U = ub + ph
                            psl = slice(ph * 64, (ph + 1) * 64)
                            S.op("dve", RCP(rcv[ph][64:128, :], P[U][64:128, :]), reads=[pb[U]], writes=[rcB[ph]])
                            S.op("dve", TT(ov[psl, cc * T + n * TB:cc * T + (n + 1) * TB], P[U][0:64, :], rcv[ph][64:128, :], ALU.mult),
                                 reads=[pb[U], rcB[ph]], writes=[oB[cc][n]])

                flat_pipeline(tiles, s_fn, pv_fn, skew=3)
                if g == 0:
                    dbg("o", ov, [oB[0][0], oB[0][1], oB[0][2], oB[0][3], oB[1][0], oB[1][1], oB[1][2], oB[1][3]], [128, 2 * T], BF16)
                wo_partial(lambda m: ("awo", g, m),
                           [lambda n: o_ap(0, n), lambda n: o_ap(1, n)], [oB[0], oB[1]])

        def mixer_b():
            S.barrier()
            o = A0
            icnt = af(o, 8192); o += 8192
            icB = Buf(const=True)
            rfull = af(o, 2048); o += 2048
            rfB = [Buf() for _ in range(NB)]
            HW = T + 32
            hf = af(o, HW); o += HW
            la = af(o, HW); o += HW
            lb = af(o, HW); o += HW
            assert o <= ARENA_W
            hfB, laB, lbB = Buf(), Buf(), Buf()
            S.dma("sp", "tab", DMA(arena[:, A0:A0 + 8192], d_icnt[:, :]), writes=[icB])
            S.op("dve", MSET(hf[:, 0:16], 0.0), writes=[hfB])
            S.op("dve", MSET(hf[:, 16 + T:HW], 0.0), writes=[hfB])
            for n in range(NB):
                stats_block(n, rfull[:, n * TB:(n + 1) * TB], rfB[n])
            for c in range(NCH):
                w = c // 2
                S.op("dve", STT(hf[:, 16:16 + T], xT[:, c * T:(c + 1) * T], col(C_ATTN + 8 + c), rfull[:, :], ALU.mult, ALU.mult),
                     reads=[xb[c][0], xb[c][1], xb[c][2], xb[c][3], rfB[0], rfB[1], rfB[2], rfB[3], colsB], writes=[hfB])
                S.op("dve", TT(la[:, 1:HW], hf[:, 0:HW - 1], hf[:, 1:HW], ALU.add), reads=[hfB], writes=[laB])
                fin, finB = la, laB
                if w >= 1:
                    S.op("dve", TT(lb[:, 2:HW - 1], la[:, 1:HW - 2], la[:, 3:HW], ALU.add), reads=[laB], writes=[lbB])
                    fin, finB = lb, lbB
                if w >= 2:
                    S.op("dve", TT(la[:, 4:HW - 3], lb[:, 2:HW - 5], lb[:, 6:HW - 1], ALU.add), reads=[lbB], writes=[laB])
                    fin, finB = la, laB
                if w >= 3:
                    S.op("dve", TT(lb[:, 8:HW - 7], la[:, 4:HW - 11], la[:, 12:HW - 3], ALU.add), reads=[laB], writes=[lbB])
                    fin, finB = lb, lbB
                S.op("dve", TT(fin[:, 16:16 + T], fin[:, 16:16 + T], icnt[:, w * T:(w + 1) * T], ALU.mult), reads=[finB, icB], writes=[finB])
                S.op("dve", TT(hT[:, c * T:(c + 1) * T], fin[:, 16:16 + T], hf[:, 16:16 + T], ALU.subtract),
                     reads=[finB, hfB], writes=[hb[c][0], hb[c][1], hb[c][2], hb[c][3]])
            for g in range(4):
                wt, wB = WR.get(("bp", g))
                for e in range(2):
                    base = 0 if e == 0 else 4
                    for kc in range(2):
                        for n in range(NB):
                            S.op("pe", MM(P[base + n][:, :], wt[:, kc * 256 + e * 128:kc * 256 + (e + 1) * 128], h_ap(2 * g + kc, n), kc == 0, kc == 1),
                                 reads=[wB, hb[2 * g + kc][n]], writes=[pb[base + n]])
                    add_to_x(2 * g + e, base, scale_col=col(C_BSCALE + 2 * g + e))

        def mixer_c():
            S.barrier()
            o = A0
            decv = [ab(o + 2048 * i, 2048) for i in range(2)]; o += 4096
            decB = [Buf() for _ in range(2)]
            qv = ab(o, 1024); o += 1024
            kv = ab(o, 1024); o += 1024
            vv = ab(o, 1024); o += 1024
            ov = ab(o, 2048); o += 2048
            NE, NEM = 4, 10
            Ev = [ab(o + 256 * i, 256) for i in range(NE)]; o += 256 * NE
            EB = [Buf() for _ in range(NE)]
            Emv = [ab(o + 256 * i, 256) for i in range(NEM)]; o += 256 * NEM
            EmB = [Buf() for _ in range(NEM)]
            f32t = [af(o + 512 * i, 512) for i in range(8)]; o += 4096
            sq2s = [ab(o, 256), ab(o + 256, 256)]; o += 512
            assert o <= ARENA_W, o
            uA, uB_, rAB, tt_, rr, rX = f32t[0:6]
            ods = f32t[6:8]
            uAB, uBB, rABB, ttB, rrB, rXB = (Buf() for _ in range(6))
            odBs = [Buf(), Buf()]
            sq2Bs = [Buf(), Buf()]
            qB = [Buf() for _ in range(NB)]
            kB = [Buf() for _ in range(NB)]
            vB = Buf()
            oB = [[Buf() for _ in range(NB)] for _ in range(2)]
            norm_to_h(C_ATTN + 16)

            def o_ap(hl, n):
                return ov[:, hl * T + n * TB:hl * T + (n + 1) * TB]

            neglam = dyn[:, 0:1]
            sgc = dyn[:, 1:2]
            for hp in range(4):
                for hl in range(2):
                    h = 2 * hp + hl
                    slope = 2.0 ** (-(h + 1))
                    S.dma("sp", "dec%d" % hl, DMA(decv[hl], d_dec[:, h * 4096:(h + 1) * 4096]), writes=[decB[hl]])
                    for kind in ("cq", "ck"):
                        wt, wB = WR.get((kind, h))
                        for n in range(NB):
                            for kc in range(NCH):
                                S.op("pe", MM(P[n][:, :], wt[:, kc * 128:(kc + 1) * 128], h_ap(kc, n), kc == 0, kc == NCH - 1),
                                     reads=[wB, hb[kc][n]], writes=[pb[n]])
                            if kind == "cq":
                                S.op("act", SMUL(qv[:, n * TB:(n + 1) * TB], P[n][:, :], 0.125), reads=[pb[n]], writes=[qB[n]])
                            else:
                                S.op("dve", CP(kv[:, n * TB:(n + 1) * TB], P[n][:, :]), reads=[pb[n]], writes=[kB[n]])
                    wt, wB = WR.get(("cv", h))
                    for quarter in range(4):
                        bank = 4 + quarter
                        for j in range(4):
                            tc = quarter * 4 + j
                            for kc in range(NCH):
                                S.op("pe", MM(P[bank][:, j * 128:(j + 1) * 128], h_tok(kc, tc), wt[:, kc * 128:(kc + 1) * 128], kc == 0, kc == NCH - 1),
                                     reads=[wB, hb[kc][tc // 4]], writes=[pb[bank]])
                        S.op("dve", CP(vv[:, quarter * TB:(quarter + 1) * TB], P[bank][:, :]),
                             reads=[pb[bank]], writes=[vB])
                    tiles = []
                    for n in range(NB):
                        kept = []
                        for sc in range(16):
                            md = max(0, 128 * sc - (TB * n + TB - 1), TB * n - (128 * sc + 127))
                            if slope * md <= ALIBI_SKIP:
                                kept.append(sc)
                        for i_, sc in enumerate(kept):
                            tiles.append((n, sc, i_ == 0, i_ == len(kept) - 1))

                    def s_fn(tl, hl=hl):
                        n, sc, first, last = tl
                        banks = []
                        for comp in range(2):
                            banks.append(ctr["st"] % 4)
                            ctr["st"] += 1
                        off = n * TB - sc * 128 + 2048
                        for comp in range(2):
                            psl = slice(comp * 64, (comp + 1) * 64)
                            S.op("pe", MM(P[banks[comp]][:, :], kv[psl, sc * 128:(sc + 1) * 128], qv[psl, n * TB:(n + 1) * TB], True, True),
                                 reads=[kB[sc // 4], qB[n]], writes=[pb[banks[comp]]])
                        ems = []
                        for comp in range(2):
                            es = ctr["e"] % NE
                            ctr["e"] += 1
                            em = ctr["sb"] % NEM
                            ctr["sb"] += 1
                            S.op("act", ACT(Ev[es], P[banks[comp]][:, :], AF.Exp), reads=[pb[banks[comp]]], writes=[EB[es]])
                            S.op("dve", TT(Emv[em], Ev[es], decv[hl][:, off:off + TB], ALU.mult),
                                 reads=[EB[es], decB[hl]], writes=[EmB[em]])
                            ems.append(em)
                        return ems

                    def pv_fn(tl, ems, hl=hl):
                        n, sc, first, last = tl
                        for comp in range(2):
                            S.op("pe", MM(P[4 + comp][:, :], vv[:, sc * 128:(sc + 1) * 128], Emv[ems[comp]], first, last),
                                 reads=[EmB[ems[comp]], vB], writes=[pb[4 + comp]])
                        for comp in range(2):
                            S.op("pe", MM(P[6][comp * 64:(comp + 1) * 64, :], ones[:, 0:64], Emv[ems[comp]], first, last),
                                 reads=[EmB[ems[comp]], constB], writes=[pb[6]])
                        if not last:
                            return
                        od, odB, sq2, sq2B = ods[n % 2], odBs[n % 2], sq2s[n % 2], sq2Bs[n % 2]
                        S.op("dve", CP(uA, P[4][:, :]), reads=[pb[4]], writes=[uAB])
                        S.op("act", SCOPY(uB_, P[5][:, :]), reads=[pb[5]], writes=[uBB])
                        S.op("act", ACT(rAB, P[6][:, :], AF.Ln), reads=[pb[6]], writes=[rABB])
                        S.op("act", ACT(rAB, rAB, AF.Exp, scale=-1.0), reads=[rABB], writes=[rABB])

                        def part1():
                            S.op("dve", CP(rX[64:128, :], rAB[0:64, :]), reads=[rABB], writes=[rXB])
                            S.op("dve", CP(rX[0:64, :], rAB[64:128, :]), reads=[rABB], writes=[rXB])
                            S.op("dve", TT(od[0:64, :], uA[0:64, :], rAB[0:64, :], ALU.mult), reads=[uAB, rABB], writes=[odB])
                            S.op("dve", TT(od[64:128, :], uA[64:128, :], rX[64:128, :], ALU.mult), reads=[uAB, rXB], writes=[odB])

                        def part1b():
                            S.op("dve", STT(tt_[0:64, :], uB_[0:64, :], neglam[0:64, :], rX[0:64, :], ALU.mult, ALU.mult),
                                 reads=[uBB, rXB, dynB], writes=[ttB])
                            S.op("dve", STT(tt_[64:128, :], uB_[64:128, :], neglam[64:128, :], rAB[64:128, :], ALU.mult, ALU.mult),
                                 reads=[uBB, rABB, dynB], writes=[ttB])
                            S.op("dve", TT(od, od, tt_, ALU.add), reads=[odB, ttB], writes=[odB])

                        def part2(n=n, hl=hl):
                            S.op("act", ACT(sq2, od, AF.Square), reads=[odB], writes=[sq2B])
                            bank = ctr["st"] % 4
                            ctr["st"] += 1
                            S.op("pe", MM(P[bank][:, :], ones[:, :], sq2, True, True), reads=[sq2B, constB], writes=[pb[bank]])

                            def part3():
                                S.op("act", ACT(rr, P[bank][:, :], AF.Ln, bias=EPS, scale=1.0 / 128), reads=[pb[bank]], writes=[rrB])
                                S.op("act", ACT(rr, rr, AF.Exp, scale=-0.5), reads=[rrB], writes=[rrB])
                                S.op("dve", STT(o_ap(hl, n), od, sgc, rr, ALU.mult, ALU.mult), reads=[odB, rrB, dynB], writes=[oB[hl][n]])

                            deferred.append([1, part3])

                        deferred.append([1, part1])
                        deferred.append([3, part1b])
                        deferred.append([6, part2])

                    flat_pipeline(tiles, s_fn, pv_fn, skew=3)
                wo_partial(lambda m: ("cwo", hp, m),
                           [lambda n: o_ap(0, n), lambda n: o_ap(1, n)], [oB[0], oB[1]])

        def mixer_d():
            S.barrier()
            o = A0
            ABv = ab(o, 16384); o += 16384
            cscv = ab(o, 512); o += 512
            assert o <= ARENA_W
            cscB = Buf(const=True)
            ABB = [Buf() for _ in range(16)]
            S.dma("sp", "tab", DMA(cscv, d_csc[:, :]), writes=[cscB])
            norm_to_h(C_ATTN + 24)

            def AB_ap(tc, g):
                return ABv[:, (tc * 4 + g) * TB:(tc * 4 + g + 1) * TB]

            k = 0
            for tc in range(16):
                for g in range(4):
                    bank = k % 8
                    for kc in range(2):
                        S.op("pe", MM(P[bank][:, :], h_tok(2 * g + kc, tc), cscv[:, kc * TB:(kc + 1) * TB], kc == 0, kc == 1),
                             reads=[hb[2 * g + kc][tc // 4], cscB], writes=[pb[bank]])
                    if k % 2 == 0:
                        S.op("act", SCOPY(AB_ap(tc, g), P[bank][:, :]), reads=[pb[bank]], writes=[ABB[tc]])
                    else:
                        S.op("dve", CP(AB_ap(tc, g), P[bank][:, :]), reads=[pb[bank]], writes=[ABB[tc]])
                    k += 1
            for n in range(NB):
                for tc in range(16):
                    for cs in range(2):
                        tt, tB = TR.get((n, tc, cs))
                        for e in range(NCH):
                            g, eh = e // 2, e % 2
                            lo = (tc * 4 + g) * TB + cs * 256 + eh * 128
                            S.op("pe", MM(P[e][:, :], ABv[:, lo:lo + 128], tt, tc == 0 and cs == 0, tc == 15 and cs == 1),
                                 reads=[ABB[tc], tB], writes=[pb[e]])
                for e in range(NCH):
                    if e % 2 == 0:
                        S.op("act", SCOPY(h_ap(e, n), P[e][:, :]), reads=[pb[e]], writes=[hb[e][n]])
                    else:
                        S.op("dve", CP(h_ap(e, n), P[e][:, :]), reads=[pb[e]], writes=[hb[e][n]])
            srcs = [(lambda n, kc=kc: h_ap(kc, n)) for kc in range(NCH)]
            wo_partial(lambda m: ("dwo", m), srcs, [hb[kc] for kc in range(NCH)])

        store_toks = []
        for s in range(nseq):
            for n in range(NB):
                for c in range(NCH):
                    S.dma("sp", "xl%d_%d" % (c, n), DMA(x_ap(c, n), xin[s, c * 128:(c + 1) * 128, n * TB:(n + 1) * TB]),
                          writes=[xb[c][n]])
            for l in layers:
                (mixer_a, mixer_b, mixer_c, mixer_d)[l]()
                if do_ffn:
                    ffn(l)
            S.barrier()
            if final_norm:
                for n in range(NB):
                    r = n % 2
                    stats_block(n, rstd[r], rstdB[r])
                    for c in range(NCH):
                        S.op("dve", STT(x_ap(c, n), x_ap(c, n), col(C_FINAL + c), rstd[r], ALU.mult, ALU.mult),
                             reads=[xb[c][n], rstdB[r], colsB], writes=[xb[c][n]])
            for n in range(NB):
                for c in range(NCH):
                    tk = S.dma("sp", "xs%d_%d" % (c, n), DMA(yout[s, c * 128:(c + 1) * 128, n * TB:(n + 1) * TB], x_ap(c, n)),
                               reads=[xb[c][n]])
                    store_toks.append(tk)
        S.wait("sp", store_toks[-NCH * NB:])
        assert WR.pos == len(wstream) and TR.pos == len(tstream)
        S.emit(st)
    return nc


_CACHE = {}


def host_inputs(inp, layers=(0, 1, 2, 3)):
    offs, wcols = weight_offsets(layers)
    wts = np.zeros((128, wcols), np.float32)
    for key, o in offs.items():
        wts[:, o:o + tile_cols(key)] = extract_tile(key, inp)
    if "tabs" not in _CACHE:
        _CACHE["tabs"] = const_tables()
    tabs = _CACHE["tabs"]
    shared = {
        "wts": wts,
        "cols": build_cols(inp),
        "lamb": np.ascontiguousarray(np.broadcast_to(np.asarray(inp["c_lambda"][0], np.float32).reshape(1, 256), (128, 256))),
        "rope": tabs["rope"], "dec": tabs["dec"], "icnt": tabs["icnt"], "csc": tabs["csc"], "ttab": tabs["ttab"], "perm": tabs["perm"],
    }
    return shared


def kernel(**inputs):
    inp = {k: np.asarray(v) for k, v in inputs.items()}
    xs = np.concatenate([inp["x_prompt"], inp["x_sample"]], axis=0)
    nseq = SEQ_PER_CORE
    shared = host_inputs(inp)
    nc = build_program(nseq)
    in_maps = []
    for c in range(NCORES):
        xc = np.ascontiguousarray(xs[c * nseq:(c + 1) * nseq].transpose(0, 2, 1))
        m = dict(shared)
        m["xin"] = xc
        in_maps.append(m)
    res = run_bass_kernel_spmd(nc, in_maps, core_ids=list(range(NCORES)))
    ys = np.concatenate([np.asarray(r["yout"]).transpose(0, 2, 1) for r in res.results], axis=0)
    ys = np.ascontiguousarray(ys, dtype=np.float32)
    nb = inp["x_prompt"].shape[0]
    return ys[:nb], ys[nb:]
```

```python
import bisect
import math
from contextlib import ExitStack

import ml_dtypes
import numpy as np

import concourse.bass as bass
import concourse.mybir as mybir
from concourse.bass_utils import run_bass_kernel_spmd

F32 = mybir.dt.float32
BF16 = mybir.dt.bfloat16
ALU = mybir.AluOpType
AF = mybir.ActivationFunctionType
AX = mybir.AxisListType

T = 2048
D = 1024
NCH = 8
TB = 512
NB = 4
DFF = 2816
EPS = 1e-6
NCORES = 8
SEQ_PER_CORE = 5
LAM_INIT = 0.8 - 0.6 * math.exp(-0.3 * 2)
WSLOT = 1408
SKEW = 5
ALIBI_SKIP = 40.0
NWSLOT = 6
NTSLOT = 6
ARENA_W = 20480

C_ATTN = 0
C_FFN = 32
C_FINAL = 64
C_BSCALE = 72
C_GQ, C_GQS, C_GK, C_GKS, C_SUBG = 80, 81, 82, 83, 84
NCOL = 88


class Buf:
    __slots__ = ("lw", "rd", "const", "psum")

    def __init__(self, const=False, psum=False):
        self.lw = None
        self.rd = []
        self.const = const
        self.psum = psum


class Sched:
    ENGS = ("pe", "act", "dve", "pool", "sp")

    def __init__(self, nc):
        self.nc = nc
        self.ops = {e: [] for e in self.ENGS}
        self.seq = {e: 0 for e in self.ENGS}
        self.needed = {e: set() for e in self.ENGS}
        self.known = {e: {} for e in self.ENGS}
        self.dma_cnt = {}

    def _reduce(self, eng, toks, same_toks=()):
        best = {}
        for t in toks:
            kind, key, val = t
            if kind == "e" and key == eng:
                continue
            k = (kind, key)
            if val > best.get(k, -1):
                best[k] = val
        if eng in ("act", "dve", "pool"):
            for t in same_toks:
                if t is not None and t[0] == "e" and t[1] == eng:
                    k = ("e", eng)
                    if t[2] > best.get(k, -1):
                        best[k] = t[2]
        waits = []
        kn = self.known[eng]
        for k, val in best.items():
            if kn.get(k, -1) >= val:
                continue
            kn[k] = val
            waits.append((k[0], k[1], val))
            if k[0] == "e":
                self.needed[k[1]].add(val)
        return waits

    def _collect(self, eng, reads, writes):
        toks = []
        for b in reads:
            if b.lw is not None:
                toks.append(b.lw)
            if b.psum:
                toks.extend(b.rd)
        for b in writes:
            if b.lw is not None:
                toks.append(b.lw)
            toks.extend(b.rd)
        return self._reduce(eng, toks, toks)

    def _mark(self, tok, reads, writes):
        for b in writes:
            b.lw = tok
            b.rd = []
        for b in reads:
            if not b.const:
                b.rd.append(tok)

    def op(self, eng, fn, reads=(), writes=()):
        waits = self._collect(eng, reads, writes)
        self.seq[eng] += 1
        s = self.seq[eng]
        tok = ("e", eng, s)
        self.ops[eng].append((waits, fn, s, None))
        self._mark(tok, reads, writes)
        return tok

    def dma(self, eng, sem, fn, reads=(), writes=()):
        waits = self._collect(eng, reads, writes)
        self.dma_cnt[sem] = self.dma_cnt.get(sem, 0) + 16
        tok = ("d", sem, self.dma_cnt[sem])
        self.seq[eng] += 1
        self.ops[eng].append((waits, fn, self.seq[eng], sem))
        self._mark(tok, reads, writes)
        return tok

    def wait(self, eng, toks):
        waits = self._reduce(eng, [t for t in toks if t is not None])
        if waits:
            self.ops[eng].append((waits, None, None, None))

    def last_tok(self, eng):
        return ("e", eng, self.seq[eng]) if self.seq[eng] > 0 else None

    def barrier(self):
        toks = [self.last_tok(e) for e in ("pe", "act", "dve")]
        for e in ("pe", "act", "dve", "sp"):
            self.wait(e, toks)

    def emit(self, stack):
        nc = self.nc
        esem = {e: stack.enter_context(nc.semaphore("s_" + e)) for e in self.ENGS}
        dsem = {n: stack.enter_context(nc.semaphore("d_" + n)) for n in self.dma_cnt}
        ranks = {e: sorted(self.needed[e]) for e in self.ENGS}
        block = stack.enter_context(nc.Block())

        def run(e, handle):
            needed = self.needed[e]
            for waits, fn, s, dma in self.ops[e]:
                for kind, key, val in waits:
                    if kind == "e":
                        handle.wait_ge(esem[key], bisect.bisect_right(ranks[key], val))
                    else:
                        handle.wait_ge(dsem[key], val)
                if fn is None:
                    continue
                ins = fn(handle)
                if dma is not None:
                    ins.then_inc(dsem[dma], 16)
                elif s in needed:
                    ins.then_inc(esem[e], 1)

        if self.ops["pe"]:
            @block.tensor
            def _(h):
                run("pe", h)
        if self.ops["act"]:
            @block.scalar
            def _(h):
                run("act", h)
        if self.ops["dve"]:
            @block.vector
            def _(h):
                run("dve", h)
        if self.ops["pool"]:
            @block.gpsimd
            def _(h):
                run("pool", h)
        if self.ops["sp"]:
            @block.sync
            def _(h):
                run("sp", h)


def MM(out, lhsT, rhs, start, stop):
    return lambda e: e.matmul(out, lhsT, rhs, start=bool(start), stop=bool(stop))


def ACT(out, in_, func, bias=0.0, scale=1.0):
    return lambda e: e.activation(out, in_, func, bias=bias, scale=scale)


def TT(out, in0, in1, op):
    return lambda e: e.tensor_tensor(out, in0, in1, op)


def STT(out, in0, scalar, in1, op0, op1):
    return lambda e: e.scalar_tensor_tensor(out, in0, scalar, in1, op0, op1)


def TS(out, in0, s1, s2, op0, op1=None):
    if op1 is None:
        return lambda e: e.tensor_scalar(out, in0, s1, None, op0)
    return lambda e: e.tensor_scalar(out, in0, s1, s2, op0, op1)


def CP(out, in_):
    return lambda e: e.tensor_copy(out, in_)


def SMUL(out, in_, c):
    return lambda e: e.mul(out, in_, c)


def SCOPY(out, in_):
    return lambda e: e.copy(out, in_)


def RCP(out, in_):
    return lambda e: e.reciprocal(out, in_)


def MSET(ap, v):
    return lambda e: e.memset(ap, v)


def DMA(out, in_):
    return lambda e: e.dma_start(out=out, in_=in_)


class Ring:
    def __init__(self, S, eng, prefix, slot_ap_fn, nslots, stream):
        self.S, self.eng, self.prefix = S, eng, prefix
        self.slot_ap = slot_ap_fn
        self.n = nslots
        self.stream = stream
        self.pos = 0
        self.loaded = 0
        self.bufs = [Buf() for _ in range(nslots)]

    def get(self, key):
        i = self.pos
        k, _, ncols = self.stream[i]
        assert k == key, (k, key)
        lim = min(len(self.stream), i + self.n - 1)
        while self.loaded < lim:
            j = self.loaded
            _, src, nc_ = self.stream[j]
            slot = j % self.n
            self.S.dma(self.eng, "%s%d" % (self.prefix, slot), DMA(self.slot_ap(slot, nc_), src),
                       writes=[self.bufs[slot]])
            self.loaded += 1
        self.pos += 1
        slot = i % self.n
        return self.slot_ap(slot, ncols), self.bufs[slot]


def ffn_catalog(l):
    L = []
    for G in range(2):
        for j in range(11):
            jj = G * 11 + j
            L += [("gate", l, jj), ("up", l, jj)]
        for m in range(8):
            L.append(("down", l, m, G))
    return L


def catalog(layers=(0, 1, 2, 3)):
    L = []
    for l in layers:
        if l == 0:
            for g in range(4):
                L += [("aq", g, 0, 0), ("aq", g, 1, 0), ("ak", g, 0), ("av", g)]
                if g >= 1:
                    L += [("awo", g - 1, m) for m in range(8)]
            L += [("awo", 3, m) for m in range(8)]
        elif l == 1:
            L += [("bp", g) for g in range(4)]
        elif l == 2:
            for hp in range(4):
                for hl in range(2):
                    h = 2 * hp + hl
                    L += [("cq", h), ("ck", h), ("cv", h)]
                L += [("cwo", hp, m) for m in range(8)]
        else:
            L += [("dwo", m) for m in range(8)]
        L += ffn_catalog(l)
    return L


def tile_cols(key):
    k = key[0]
    if k in ("gate", "up", "aq", "ak", "cq", "ck", "cv", "dwo"):
        return 1024
    if k == "down":
        return 1408
    if k in ("av", "bp"):
        return 512
    if k in ("awo", "cwo"):
        return 256
    raise KeyError(key)


def _kc_tile(w):
    K, M = w.shape
    return np.ascontiguousarray(w.reshape(K // 128, 128, M).transpose(1, 0, 2)).reshape(128, -1)


_SWAP64 = np.arange(64) ^ 1


def extract_tile(key, W):
    k = key[0]
    if k == "gate":
        return _kc_tile(W["w_gate"][key[1]][:, key[2] * 128:(key[2] + 1) * 128])
    if k == "up":
        return _kc_tile(W["w_up"][key[1]][:, key[2] * 128:(key[2] + 1) * 128])
    if k == "down":
        _, l, m, G = key
        return _kc_tile(W["w_down"][l][G * 1408:(G + 1) * 1408, m * 128:(m + 1) * 128])
    if k == "aq":
        _, g, cc, sw = key
        cols = (2 * g + cc) * 128 + np.arange(128)
        if sw:
            cols = cols ^ 1
        return _kc_tile(W["a_w_qkv"][0][:, cols])
    if k == "ak":
        _, g, sw = key
        c64 = np.arange(64)
        if sw:
            c64 = c64 ^ 1
        cols = 1024 + g * 64 + np.concatenate([c64, c64])
        return _kc_tile(W["a_w_qkv"][0][:, cols])
    if k == "av":
        g = key[1]
        cols = 1280 + g * 64 + np.arange(64)
        return _kc_tile(W["a_w_qkv"][0][:, cols])
    if k == "awo":
        _, g, m = key
        return _kc_tile(W["a_w_o"][0][g * 256:(g + 1) * 256, m * 128:(m + 1) * 128])
    if k == "bp":
        g = key[1]
        return _kc_tile(W["b_w_pool"][0][g])
    if k == "cq":
        h = key[1]
        return _kc_tile(W["c_w_qkv"][0][:, h * 128:(h + 1) * 128])
    if k == "ck":
        h = key[1]
        return _kc_tile(W["c_w_qkv"][0][:, 1024 + h * 128:1024 + (h + 1) * 128])
    if k == "cv":
        h = key[1]
        return _kc_tile(W["c_w_qkv"][0][:, 2048 + h * 128:2048 + (h + 1) * 128])
    if k == "cwo":
        _, hp, m = key
        return _kc_tile(W["c_w_o"][0][hp * 256:(hp + 1) * 256, m * 128:(m + 1) * 128])
    if k == "dwo":
        m = key[1]
        return _kc_tile(W["d_w_o"][0][:, m * 128:(m + 1) * 128])
    raise KeyError(key)


def weight_offsets(layers):
    offs = {}
    o = 0
    for key in catalog(layers):
        if key not in offs:
            offs[key] = o
            o += tile_cols(key)
    return offs, o


def const_tables():
    tabs = {}
    p = np.arange(128)
    d = p % 64
    i = d // 2
    t = np.arange(T)
    r = (t // 64).astype(np.float32)
    c = (t % 64).astype(np.float32)
    n = 16
    freqs = (np.float32(10000.0) ** (-np.arange(n, dtype=np.float32) / np.float32(n))).astype(np.float32)
    ang = np.where((i < 16)[:, None], r[None, :] * freqs[np.minimum(i, 15)][:, None],
                   c[None, :] * freqs[np.maximum(i - 16, 0)][:, None]).astype(np.float32)
    cosv = np.cos(ang.astype(np.float64))
    sinv = np.sin(ang.astype(np.float64))
    sgn = np.where(d % 2 == 0, -1.0, 1.0)[:, None]
    tabs["rope"] = np.concatenate([cosv, sinv * sgn], axis=1).astype(np.float32)
    cc = np.arange(4096)
    dist = np.abs(cc[None, :] - 2048 - p[:, None]).astype(np.float64)
    tabs["dec"] = np.concatenate([np.exp(-(2.0 ** (-(h + 1))) * dist) for h in range(8)], axis=1).astype(ml_dtypes.bfloat16)
    ic = np.zeros((4, T), np.float64)
    for gi, win in enumerate((2, 4, 8, 16)):
        lo = np.clip(t - win // 2, 0, T - 1)
        hi = np.clip(t + win // 2 - 1, 0, T - 1)
        ic[gi] = 1.0 / (hi - lo + 1)
    tabs["icnt"] = np.ascontiguousarray(np.broadcast_to(ic.reshape(1, 4 * T), (128, 4 * T))).astype(np.float32)
    cidx = (np.arange(2)[None, :, None] * 128 + p[:, None, None])
    e = np.arange(256)[None, None, :]
    a = 2.0 * np.pi * ((cidx * e) % 256) / 256.0
    csc = np.concatenate([np.cos(a), np.sin(a)], axis=2) / 16.0
    tabs["csc"] = csc.reshape(128, 1024).astype(ml_dtypes.bfloat16)
    pm = np.zeros((128, 128), np.float32)
    pm[np.arange(128) ^ 1, np.arange(128)] = 1.0
    tabs["perm"] = pm.astype(ml_dtypes.bfloat16)
    tt = np.zeros((128, NB, 16, 2, TB), np.float32)
    sc_ = 1.0 / math.sqrt(T)
    for tc in range(16):
        trow = (tc * 128 + p)[:, None]
        ang2 = 2.0 * np.pi * ((trow * t[None, :]) % T) / T
        cm = (np.cos(ang2) * sc_).astype(np.float32).reshape(128, NB, TB)
        sm = (-np.sin(ang2) * sc_).astype(np.float32).reshape(128, NB, TB)
        tt[:, :, tc, 0, :] = cm
        tt[:, :, tc, 1, :] = sm
    tabs["ttab"] = tt.reshape(128, NB * 16 * 2 * TB).astype(ml_dtypes.bfloat16)
    return tabs


def build_cols(inp):
    cols = np.zeros((128, NCOL), np.float32)

    def colify(v):
        return np.asarray(v, np.float32).reshape(8, 128).T

    for l in range(4):
        cols[:, C_ATTN + 8 * l:C_ATTN + 8 * l + 8] = colify(inp["attn_norm"][l])
        cols[:, C_FFN + 8 * l:C_FFN + 8 * l + 8] = colify(inp["ffn_norm"][l])
    cols[:, C_FINAL:C_FINAL + 8] = colify(inp["final_norm"])
    cols[:, C_BSCALE:C_BSCALE + 8] = colify(inp["b_scale"][0])
    d = np.arange(128) % 64
    cols[:, C_GQ] = inp["a_q_gain"][0][d]
    cols[:, C_GQS] = inp["a_q_gain"][0][d ^ 1]
    cols[:, C_GK] = inp["a_k_gain"][0][d]
    cols[:, C_GKS] = inp["a_k_gain"][0][d ^ 1]
    cols[:, C_SUBG] = inp["c_sub_gain"][0]
    return cols


def build_program(nseq, layers=(0, 1, 2, 3), do_ffn=True, final_norm=True, debug=False):
    nc = bass.Bass("TRN2", target_bir_lowering=False)
    dbg_n = [0]
    offs, wcols = weight_offsets(layers)
    xin = nc.dram_tensor("xin", [nseq, D, T], F32, kind="ExternalInput").ap()
    yout = nc.dram_tensor("yout", [nseq, D, T], F32, kind="ExternalOutput").ap()
    wts = nc.dram_tensor("wts", [128, wcols], F32, kind="ExternalInput").ap()
    d_cols = nc.dram_tensor("cols", [128, NCOL], F32, kind="ExternalInput").ap()
    d_lamb = nc.dram_tensor("lamb", [128, 256], F32, kind="ExternalInput").ap()
    d_rope = nc.dram_tensor("rope", [128, 4096], F32, kind="ExternalInput").ap()
    d_dec = nc.dram_tensor("dec", [128, 8 * 4096], BF16, kind="ExternalInput").ap()
    d_icnt = nc.dram_tensor("icnt", [128, 4 * T], F32, kind="ExternalInput").ap()
    d_csc = nc.dram_tensor("csc", [128, 1024], BF16, kind="ExternalInput").ap()
    d_perm = nc.dram_tensor("perm", [128, 128], BF16, kind="ExternalInput").ap()
    d_ttab = nc.dram_tensor("ttab", [128, NB * 16 * 2 * TB], BF16, kind="ExternalInput").ap()

    st = ExitStack()
    with st:
        xT = st.enter_context(nc.sbuf_tensor("xT", [128, NCH * T], F32))
        hT = st.enter_context(nc.sbuf_tensor("hT", [128, NCH * T], BF16))
        wring = st.enter_context(nc.sbuf_tensor("wring", [128, NWSLOT * WSLOT], BF16))
        tring = st.enter_context(nc.sbuf_tensor("tring", [128, NTSLOT * TB], BF16))
        cols = st.enter_context(nc.sbuf_tensor("colst", [128, NCOL], F32))
        dyn = st.enter_context(nc.sbuf_tensor("dyn", [128, 8], F32))
        lamt = st.enter_context(nc.sbuf_tensor("lamt", [128, 256], F32))
        lamp = st.enter_context(nc.sbuf_tensor("lamp", [128, 128], F32))
        ones = st.enter_context(nc.sbuf_tensor("ones", [128, 128], BF16))
        bdiag = st.enter_context(nc.sbuf_tensor("bdiag", [128, 128], BF16))
        permt = st.enter_context(nc.sbuf_tensor("permt", [128, 128], BF16))
        arena = st.enter_context(nc.sbuf_tensor("arena", [128, ARENA_W], F32))
        P = [st.enter_context(nc.psum_tensor("ps%d" % i, [128, TB], F32)) for i in range(8)]

        S = Sched(nc)
        pb = [Buf(psum=True) for _ in range(8)]
        xb = [[Buf() for _ in range(NB)] for _ in range(NCH)]
        hb = [[Buf() for _ in range(NB)] for _ in range(NCH)]
        colsB = Buf(const=True)
        constB = Buf(const=True)
        lamB = Buf()

        def x_ap(c, n):
            return xT[:, c * T + n * TB:c * T + (n + 1) * TB]

        def h_ap(c, n):
            return hT[:, c * T + n * TB:c * T + (n + 1) * TB]

        def h_tok(c, tc):
            return hT[:, c * T + tc * 128:c * T + (tc + 1) * 128]

        def col(i):
            return cols[:, i:i + 1]

        def af(o, n):
            return arena[:, o:o + n]

        def ab(o, n):
            return arena[:, o:o + n].bitcast(BF16)

        sqb = [ab(256 * i, 256) for i in range(4)]
        sqB = [Buf() for _ in range(4)]
        tsq = [af(1024 + 512 * i, 512) for i in range(2)]
        tsqB = [Buf() for _ in range(2)]
        rstd = [af(2048 + 512 * i, 512) for i in range(2)]
        rstdB = [Buf() for _ in range(2)]
        A0 = 3072

        cat = catalog(layers) if do_ffn else [k for k in catalog(layers) if k[0] not in ("gate", "up", "down")]
        wstream = []
        for _ in range(nseq):
            for key in cat:
                ncl = tile_cols(key)
                wstream.append((key, wts[:, offs[key]:offs[key] + ncl], ncl))
        WR = Ring(S, "pool", "w", lambda s, n_: wring[:, s * WSLOT:s * WSLOT + n_], NWSLOT, wstream)
        tstream = []
        if 3 in layers:
            for _ in range(nseq):
                for n in range(NB):
                    for tc in range(16):
                        for cs in range(2):
                            o = ((n * 16 + tc) * 2 + cs) * TB
                            tstream.append(((n, tc, cs), d_ttab[:, o:o + TB], TB))
        TR = Ring(S, "sp", "t", lambda s, n_: tring[:, s * TB:s * TB + n_], NTSLOT, tstream)

        ctr = {"st": 0, "e": 0, "sb": 0, "nb": 0}
        deferred = []

        S.dma("sp", "c0", DMA(cols[:], d_cols[:, :]), writes=[colsB])
        S.dma("sp", "c1", DMA(lamt[:], d_lamb[:, :]), writes=[lamB])
        S.dma("sp", "c2", DMA(permt[:], d_perm[:, :]), writes=[constB])
        S.op("dve", MSET(ones[:], 1.0), writes=[constB])
        S.op("dve", MSET(bdiag[:], 0.0), writes=[constB])
        S.op("dve", MSET(bdiag[0:64, 0:64], 1.0), writes=[constB])
        S.op("dve", MSET(bdiag[64:128, 64:128], 1.0), writes=[constB])
        lpB = Buf()
        S.op("dve", TT(lamp[:, 0:64], lamt[:, 0:64], lamt[:, 64:128], ALU.mult), reads=[lamB], writes=[lpB])
        S.op("dve", TT(lamp[:, 64:128], lamt[:, 128:192], lamt[:, 192:256], ALU.mult), reads=[lamB], writes=[lpB])
        dynB = Buf()
        S.op("dve", lambda e: e.reduce_sum(dyn[:, 2:3], lamp[:, 0:64], AX.X), reads=[lpB], writes=[dynB])
        S.op("dve", lambda e: e.reduce_sum(dyn[:, 3:4], lamp[:, 64:128], AX.X), reads=[lpB], writes=[dynB])
        S.op("act", ACT(dyn[:, 4:6], dyn[:, 2:4], AF.Exp), reads=[dynB], writes=[dynB])
        S.op("dve", TT(dyn[:, 6:7], dyn[:, 4:5], dyn[:, 5:6], ALU.subtract), reads=[dynB], writes=[dynB])
        S.op("dve", TS(dyn[:, 0:1], dyn[:, 6:7], LAM_INIT, -1.0, ALU.add, ALU.mult), reads=[dynB], writes=[dynB])
        S.op("dve", TS(dyn[:, 1:2], col(C_SUBG), 1.0 - LAM_INIT, None, ALU.mult), reads=[dynB, colsB], writes=[dynB])
        dynB.const = True

        def dbg(name, ap, bufs, shape, dt):
            if not debug or dbg_n[0] > 12:
                return
            dbg_n[0] += 1
            dten = nc.dram_tensor("dbg_" + name, shape, dt, kind="ExternalOutput").ap()
            S.dma("sp", "dbg%d" % dbg_n[0], DMA(dten[:, :], ap), reads=bufs)

        def stats_block(n, dst, dstB, inv_n=1.0 / D, src=None):
            bank = 6 + (ctr["nb"] % 2)
            ctr["nb"] += 1
            for c in range(NCH):
                q = c % 4
                S.op("act", ACT(sqb[q], x_ap(c, n), AF.Square), reads=[xb[c][n]], writes=[sqB[q]])
                S.op("pe", MM(P[bank][:, :], ones[:, :], sqb[q], c == 0, c == NCH - 1),
                     reads=[sqB[q], constB], writes=[pb[bank]])
            tq = ctr["nb"] % 2
            S.op("act", ACT(tsq[tq], P[bank][:, :], AF.Ln, bias=EPS, scale=inv_n), reads=[pb[bank]], writes=[tsqB[tq]])
            S.op("act", ACT(dst, tsq[tq], AF.Exp, scale=-0.5), reads=[tsqB[tq]], writes=[dstB])

        def norm_to_h(gcol0):
            for n in range(NB):
                r = n % 2
                stats_block(n, rstd[r], rstdB[r])
                for c in range(NCH):
                    S.op("dve", STT(h_ap(c, n), x_ap(c, n), col(gcol0 + c), rstd[r], ALU.mult, ALU.mult),
                         reads=[xb[c][n], rstdB[r], colsB], writes=[hb[c][n]])

        def add_to_x(m, base, scale_col=None):
            for n in range(NB):
                if scale_col is None:
                    S.op("dve", TT(x_ap(m, n), x_ap(m, n), P[base + n][:, :], ALU.add),
                         reads=[pb[base + n], xb[m][n]], writes=[xb[m][n]])
                else:
                    S.op("dve", STT(x_ap(m, n), P[base + n][:, :], scale_col, x_ap(m, n), ALU.mult, ALU.add),
                         reads=[pb[base + n], xb[m][n], colsB], writes=[xb[m][n]])

        def wo_partial(keyfn, srcs, srcBs):
            nk = len(srcs)
            for m in range(NCH):
                wt, wB = WR.get(keyfn(m))
                base = 0 if m % 2 == 0 else 4
                for kc in range(nk):
                    for n in range(NB):
                        S.op("pe", MM(P[base + n][:, :], wt[:, kc * 128:(kc + 1) * 128], srcs[kc](n), kc == 0, kc == nk - 1),
                             reads=[wB, srcBs[kc][n]], writes=[pb[base + n]])
                add_to_x(m, base)

        def drain_deferred():
            while deferred:
                d_ = deferred.pop(0)
                d_[1]()

        def flat_pipeline(tiles, s_fn, pv_fn, skew=SKEW, flush=True):
            hs = {}
            nt = len(tiles)
            for t in range(nt + skew):
                if t < nt:
                    hs[t] = s_fn(tiles[t])
                if t >= skew:
                    pv_fn(tiles[t - skew], hs.pop(t - skew))
                    for d_ in list(deferred):
                        d_[0] -= 1
                        if d_[0] <= 0:
                            deferred.remove(d_)
                            d_[1]()
            if flush:
                drain_deferred()
            else:
                keep = []
                while deferred:
                    d_ = deferred.pop(0)
                    if len(d_) > 2 and d_[2]:
                        keep.append(d_)
                    else:
                        d_[1]()
                deferred.extend(keep)

        def ffn(l):
            S.barrier()
            norm_to_h(C_FFN + 8 * l)
            act_o = A0
            actv = ab(act_o, 11264)
            actB = [[Buf() for _ in range(NB)] for _ in range(11)]
            sg_o = A0 + 11264
            sg = [af(sg_o + 512 * i, 512) for i in range(4)]
            sgB = [Buf() for _ in range(4)]

            def act_ap(j, n):
                return actv[:, j * T + n * TB:j * T + (n + 1) * TB]

            for G in range(2):
                for j in range(11):
                    jj = G * 11 + j
                    gt, gB = WR.get(("gate", l, jj))
                    ut, uB = WR.get(("up", l, jj))
                    for kc in range(NCH):
                        for n in range(NB):
                            S.op("pe", MM(P[n][:, :], gt[:, kc * 128:(kc + 1) * 128], h_ap(kc, n), kc == 0, kc == NCH - 1),
                                 reads=[gB, hb[kc][n]], writes=[pb[n]])
                    for kc in range(NCH):
                        for n in range(NB):
                            S.op("pe", MM(P[4 + n][:, :], ut[:, kc * 128:(kc + 1) * 128], h_ap(kc, n), kc == 0, kc == NCH - 1),
                                 reads=[uB, hb[kc][n]], writes=[pb[4 + n]])
                    for n in range(NB):
                        S.op("act", ACT(sg[n], P[n][:, :], AF.Silu), reads=[pb[n]], writes=[sgB[n]])
                        S.op("dve", TT(act_ap(j, n), sg[n], P[4 + n][:, :], ALU.mult),
                             reads=[sgB[n], pb[4 + n]], writes=[actB[j][n]])
                for m in range(NCH):
                    dt_, dB = WR.get(("down", l, m, G))
                    base = 0 if m % 2 == 0 else 4
                    for j in range(11):
                        for n in range(NB):
                            S.op("pe", MM(P[base + n][:, :], dt_[:, j * 128:(j + 1) * 128], act_ap(j, n), j == 0, j == 10),
                                 reads=[dB, actB[j][n]], writes=[pb[base + n]])
                    add_to_x(m, base)

        def mixer_a():
            S.barrier()
            o = A0
            ropeC = af(o, 2048); o += 2048
            ropeS = af(o, 2048); o += 2048
            ropeB = Buf(const=True)
            qv = ab(o, 2048); o += 2048
            kv = ab(o, 1024); o += 1024
            vv = ab(o, 1024); o += 1024
            ov = ab(o, 2048); o += 2048
            NEA = 8
            Ev = [ab(o + 256 * i, 256) for i in range(NEA)]; o += 256 * NEA
            EB = [Buf() for _ in range(NEA)]
            sq2 = [ab(o + 256 * i, 256) for i in range(2)]; o += 512
            sq2B = [Buf() for _ in range(2)]
            abf = [ab(o + 256 * i, 256) for i in range(2)]; o += 512
            abfB = [Buf() for _ in range(2)]
            f32t = [af(o + 512 * i, 512) for i in range(7)]; o += 3584
            assert o <= ARENA_W, o
            rs2, t1v, rcv = f32t[0:2], f32t[2:4], f32t[4:6]
            t2v = [f32t[6], f32t[6]]
            rs2B, t1B, rcB = ([Buf() for _ in range(2)] for _ in range(3))
            t2B_ = Buf()
            t2B = [t2B_, t2B_]
            qB = [[Buf() for _ in range(NB)] for _ in range(2)]
            kB = [Buf() for _ in range(NB)]
            vB = Buf()
            oB = [[Buf() for _ in range(NB)] for _ in range(2)]
            S.dma("sp", "tab", DMA(arena[:, A0:A0 + 4096], d_rope[:, :]), writes=[ropeB])
            vv3 = vv.rearrange("p (t e) -> p t e", e=128)
            S.op("dve", MSET(vv3[:, :, 64:128], 1.0), writes=[vB])
            norm_to_h(C_ATTN + 0)

            def q_ap(cc, n):
                return qv[:, cc * T + n * TB:cc * T + (n + 1) * TB]

            def o_ap(cc, n):
                return ov[:, cc * T + n * TB:cc * T + (n + 1) * TB]

            for g in range(4):
                ptiles = [(kind, cc, n) for kind, cc in (("q", 0), ("q", 1), ("k", 0)) for n in range(NB)]
                wcur = {}

                def stage1(i):
                    kind, cc, n = ptiles[i]
                    r = i % 2
                    PA = 0 if r == 0 else 3
                    if n == 0:
                        wcur[0] = WR.get(("aq", g, cc, 0)) if kind == "q" else WR.get(("ak", g, 0))
                    wa, waB = wcur[0]
                    for kc in range(NCH):
                        S.op("pe", MM(P[PA][:, :], wa[:, kc * 128:(kc + 1) * 128], h_ap(kc, n), kc == 0, kc == NCH - 1),
                             reads=[waB, hb[kc][n]], writes=[pb[PA]])
                    S.op("act", ACT(sq2[r], P[PA][:, :], AF.Square), reads=[pb[PA]], writes=[sq2B[r]])
                    S.op("act", SCOPY(abf[r], P[PA][:, :]), reads=[pb[PA]], writes=[abfB[r]])

                def stage2(i):
                    kind, cc, n = ptiles[i]
                    r = i % 2
                    PA, PB_, PR = (0, 1, 2) if r == 0 else (3, 4, 5)
                    if kind == "q":
                        gc, gsc, sc_ = col(C_GQ), col(C_GQS), 0.125
                    else:
                        gc, gsc, sc_ = col(C_GK), col(C_GKS), 1.0
                    S.op("pe", MM(P[PB_][:, :], permt[:, :], abf[r], True, True), reads=[abfB[r], constB], writes=[pb[PB_]])
                    S.op("pe", MM(P[PR][:, :], bdiag[:, :], sq2[r], True, True), reads=[sq2B[r], constB], writes=[pb[PR]])
                    S.op("act", ACT(rs2[r], P[PR][:, :], AF.Ln, bias=EPS, scale=1.0 / 64), reads=[pb[PR]], writes=[rs2B[r]])
                    S.op("act", ACT(rs2[r], rs2[r], AF.Exp, scale=-0.5), reads=[rs2B[r]], writes=[rs2B[r]])
                    S.op("dve", STT(t1v[r], P[PA][:, :], gc, ropeC[:, n * TB:(n + 1) * TB], ALU.mult, ALU.mult),
                         reads=[pb[PA], ropeB, colsB], writes=[t1B[r]])
                    S.op("dve", STT(t2v[r], P[PB_][:, :], gsc, ropeS[:, n * TB:(n + 1) * TB], ALU.mult, ALU.mult),
                         reads=[pb[PB_], ropeB, colsB], writes=[t2B[r]])
                    S.op("dve", TT(t1v[r], t1v[r], t2v[r], ALU.add), reads=[t1B[r], t2B[r]], writes=[t1B[r]])
                    if kind == "q":
                        S.op("dve", STT(q_ap(cc, n), t1v[r], sc_, rs2[r], ALU.mult, ALU.mult), reads=[t1B[r], rs2B[r]], writes=[qB[cc][n]])
                    else:
                        S.op("dve", STT(kv[:, n * TB:(n + 1) * TB], t1v[r], sc_, rs2[r], ALU.mult, ALU.mult),
                             reads=[t1B[r], rs2B[r]], writes=[kB[n]])

                for i in range(len(ptiles) + 1):
                    if i < len(ptiles):
                        stage1(i)
                    if i >= 1:
                        stage2(i - 1)
                wv, wvB = WR.get(("av", g))
                for half in range(2):
                    bank = 6 + half
                    for j in range(8):
                        tc = half * 8 + j
                        for kc in range(NCH):
                            S.op("pe", MM(P[bank][:, j * 64:(j + 1) * 64], h_tok(kc, tc), wv[:, kc * 64:(kc + 1) * 64], kc == 0, kc == NCH - 1),
                                 reads=[wvB, hb[kc][tc // 4]], writes=[pb[bank]])
                    S.op("act", SCOPY(vv3[:, half * 8:(half + 1) * 8, 0:64], P[bank][:, :].rearrange("p (t e) -> p t e", e=64)),
                         reads=[pb[bank]], writes=[vB])
                if g == 0:
                    dbg("q", qv, [qB[0][0], qB[0][1], qB[0][2], qB[0][3], qB[1][0], qB[1][1], qB[1][2], qB[1][3]], [128, 2 * T], BF16)
                    dbg("k", kv, kB, [128, T], BF16)
                    dbg("v", vv, [vB], [128, T], BF16)
                    dbg("h", hT[:, :], [hb[c_][n_] for c_ in range(NCH) for n_ in range(NB)], [128, NCH * T], BF16)
                if g >= 1:
                    wo_partial(lambda m: ("awo", g - 1, m),
                               [lambda n: o_ap(0, n), lambda n: o_ap(1, n)], [oB[0], oB[1]])
                tiles = []
                for cc in range(2):
                    for n in range(NB):
                        ub = 4 + 2 * (ctr["sb"] % 2)
                        ctr["sb"] += 1
                        for sc in range(16):
                            tiles.append((cc, n, sc, ub))

                def s_fn(tl):
                    cc, n, sc, ub = tl
                    hs = []
                    banks = []
                    for ph in range(2):
                        banks.append(ctr["st"] % 4)
                        ctr["st"] += 1
                    for ph in range(2):
                        psl = slice(ph * 64, (ph + 1) * 64)
                        S.op("pe", MM(P[banks[ph]][:, :], kv[psl, sc * 128:(sc + 1) * 128], qv[psl, cc * T + n * TB:cc * T + (n + 1) * TB], True, True),
                             reads=[kB[sc // 4], qB[cc][n]], writes=[pb[banks[ph]]])
                    for ph in range(2):
                        es = ctr["e"] % NEA
                        ctr["e"] += 1
                        S.op("act", ACT(Ev[es], P[banks[ph]][:, :], AF.Exp), reads=[pb[banks[ph]]], writes=[EB[es]])
                        hs.append(es)
                    return hs

                def pv_fn(tl, hs):
                    cc, n, sc, ub = tl
                    for ph in range(2):
                        U = ub + ph
                        S.op("pe", MM(P[U][:, :], vv3[:, sc, :], Ev[hs[ph]], sc == 0, sc == 15),
                             reads=[EB[hs[ph]], vB], writes=[pb[U]])
                    if sc == 15:
                        for ph in range(2):
                            U = ub + ph
                            psl = slice(ph * 64, (ph + 1) * 64)
                            S.op("dve", RCP(rcv[ph][64:128, :], P[U][64:128, :]), reads=[pb[U]], writes=[rcB[ph]])
                            S.op("dve", TT(ov[psl, cc * T + n * TB:cc * T + (n + 1) * TB], P[U][0:64, :], rcv[ph][64:128, :], ALU.mult),
                                 reads=[pb[U], rcB[ph]], writes=[oB[cc][n]])

                flat_pipeline(tiles, s_fn, pv_fn, skew=3)
                if g == 0:
                    dbg("o", ov, [oB[0][0], oB[0][1], oB[0][2], oB[0][3], oB[1][0], oB[1][1], oB[1][2], oB[1][3]], [128, 2 * T], BF16)
                if g == 3:
                    wo_partial(lambda m: ("awo", g, m),
                               [lambda n: o_ap(0, n), lambda n: o_ap(1, n)], [oB[0], oB[1]])

        def mixer_b():
            S.barrier()
            o = A0
            icnt = af(o, 8192); o += 8192
            icB = Buf(const=True)
            rfull = af(o, 2048); o += 2048
            rfB = [Buf() for _ in range(NB)]
            HW = T + 32
            hf = af(o, HW); o += HW
            la = af(o, HW); o += HW
            lb = af(o, HW); o += HW
            assert o <= ARENA_W
            hfB, laB, lbB = Buf(), Buf(), Buf()
            S.dma("sp", "tab", DMA(arena[:, A0:A0 + 8192], d_icnt[:, :]), writes=[icB])
            S.op("dve", MSET(hf[:, 0:16], 0.0), writes=[hfB])
            S.op("dve", MSET(hf[:, 16 + T:HW], 0.0), writes=[hfB])
            for n in range(NB):
                stats_block(n, rfull[:, n * TB:(n + 1) * TB], rfB[n])
            for c in range(NCH):
                w = c // 2
                S.op("dve", STT(hf[:, 16:16 + T], xT[:, c * T:(c + 1) * T], col(C_ATTN + 8 + c), rfull[:, :], ALU.mult, ALU.mult),
                     reads=[xb[c][0], xb[c][1], xb[c][2], xb[c][3], rfB[0], rfB[1], rfB[2], rfB[3], colsB], writes=[hfB])
                S.op("dve", TT(la[:, 1:HW], hf[:, 0:HW - 1], hf[:, 1:HW], ALU.add), reads=[hfB], writes=[laB])
                fin, finB = la, laB
                if w >= 1:
                    S.op("dve", TT(lb[:, 2:HW - 1], la[:, 1:HW - 2], la[:, 3:HW], ALU.add), reads=[laB], writes=[lbB])
                    fin, finB = lb, lbB
                if w >= 2:
                    S.op("dve", TT(la[:, 4:HW - 3], lb[:, 2:HW - 5], lb[:, 6:HW - 1], ALU.add), reads=[lbB], writes=[laB])
                    fin, finB = la, laB
                if w >= 3:
                    S.op("dve", TT(lb[:, 8:HW - 7], la[:, 4:HW - 11], la[:, 12:HW - 3], ALU.add), reads=[laB], writes=[lbB])
                    fin, finB = lb, lbB
                S.op("dve", TT(fin[:, 16:16 + T], fin[:, 16:16 + T], icnt[:, w * T:(w + 1) * T], ALU.mult), reads=[finB, icB], writes=[finB])
                S.op("dve", TT(hT[:, c * T:(c + 1) * T], fin[:, 16:16 + T], hf[:, 16:16 + T], ALU.subtract),
                     reads=[finB, hfB], writes=[hb[c][0], hb[c][1], hb[c][2], hb[c][3]])
            for g in range(4):
                wt, wB = WR.get(("bp", g))
                for e in range(2):
                    base = 0 if e == 0 else 4
                    for kc in range(2):
                        for n in range(NB):
                            S.op("pe", MM(P[base + n][:, :], wt[:, kc * 256 + e * 128:kc * 256 + (e + 1) * 128], h_ap(2 * g + kc, n), kc == 0, kc == 1),
                                 reads=[wB, hb[2 * g + kc][n]], writes=[pb[base + n]])
                    add_to_x(2 * g + e, base, scale_col=col(C_BSCALE + 2 * g + e))

        def mixer_c():
            S.barrier()
            o = A0
            decv = [ab(o + 2048 * i, 2048) for i in range(2)]; o += 4096
            decB = [Buf() for _ in range(2)]
            qv = ab(o, 1024); o += 1024
            kv = ab(o, 1024); o += 1024
            vv = ab(o, 1024); o += 1024
            ov = ab(o, 2048); o += 2048
            NE, NEM = 4, 10
            Ev = [ab(o + 256 * i, 256) for i in range(NE)]; o += 256 * NE
            EB = [Buf() for _ in range(NE)]
            Emv = [ab(o + 256 * i, 256) for i in range(NEM)]; o += 256 * NEM
            EmB = [Buf() for _ in range(NEM)]
            f32t = [af(o + 512 * i, 512) for i in range(8)]; o += 4096
            sq2s = [ab(o, 256), ab(o + 256, 256)]; o += 512
            assert o <= ARENA_W, o
            uA, uB_, rAB, tt_, rr, rX = f32t[0:6]
            ods = f32t[6:8]
            uAB, uBB, rABB, ttB, rrB, rXB = (Buf() for _ in range(6))
            odBs = [Buf(), Buf()]
            sq2Bs = [Buf(), Buf()]
            qB = [Buf() for _ in range(NB)]
            kB = [Buf() for _ in range(NB)]
            vB = Buf()
            oB = [[Buf() for _ in range(NB)] for _ in range(2)]
            norm_to_h(C_ATTN + 16)

            def o_ap(hl, n):
                return ov[:, hl * T + n * TB:hl * T + (n + 1) * TB]

            neglam = dyn[:, 0:1]
            sgc = dyn[:, 1:2]
            for hp in range(4):
                for hl in range(2):
                    h = 2 * hp + hl
                    slope = 2.0 ** (-(h + 1))
                    S.dma("sp", "dec%d" % hl, DMA(decv[hl], d_dec[:, h * 4096:(h + 1) * 4096]), writes=[decB[hl]])
                    for kind in ("cq", "ck"):
                        wt, wB = WR.get((kind, h))
                        for n in range(NB):
                            for kc in range(NCH):
                                S.op("pe", MM(P[n][:, :], wt[:, kc * 128:(kc + 1) * 128], h_ap(kc, n), kc == 0, kc == NCH - 1),
                                     reads=[wB, hb[kc][n]], writes=[pb[n]])
                            if kind == "cq":
                                S.op("act", SMUL(qv[:, n * TB:(n + 1) * TB], P[n][:, :], 0.125), reads=[pb[n]], writes=[qB[n]])
                            else:
                                S.op("dve", CP(kv[:, n * TB:(n + 1) * TB], P[n][:, :]), reads=[pb[n]], writes=[kB[n]])
                        if kind == "cq":
                            drain_deferred()
                    wt, wB = WR.get(("cv", h))
                    for quarter in range(4):
                        bank = 4 + quarter
                        for j in range(4):
                            tc = quarter * 4 + j
                            for kc in range(NCH):
                                S.op("pe", MM(P[bank][:, j * 128:(j + 1) * 128], h_tok(kc, tc), wt[:, kc * 128:(kc + 1) * 128], kc == 0, kc == NCH - 1),
                                     reads=[wB, hb[kc][tc // 4]], writes=[pb[bank]])
                        S.op("dve", CP(vv[:, quarter * TB:(quarter + 1) * TB], P[bank][:, :]),
                             reads=[pb[bank]], writes=[vB])
                    tiles = []
                    for n in range(NB):
                        kept = []
                        for sc in range(16):
                            md = max(0, 128 * sc - (TB * n + TB - 1), TB * n - (128 * sc + 127))
                            if slope * md <= ALIBI_SKIP:
                                kept.append(sc)
                        for i_, sc in enumerate(kept):
                            tiles.append((n, sc, i_ == 0, i_ == len(kept) - 1))

                    def s_fn(tl, hl=hl):
                        n, sc, first, last = tl
                        banks = []
                        for comp in range(2):
                            banks.append(ctr["st"] % 4)
                            ctr["st"] += 1
                        off = n * TB - sc * 128 + 2048
                        for comp in range(2):
                            psl = slice(comp * 64, (comp + 1) * 64)
                            S.op("pe", MM(P[banks[comp]][:, :], kv[psl, sc * 128:(sc + 1) * 128], qv[psl, n * TB:(n + 1) * TB], True, True),
                                 reads=[kB[sc // 4], qB[n]], writes=[pb[banks[comp]]])
                        ems = []
                        for comp in range(2):
                            es = ctr["e"] % NE
                            ctr["e"] += 1
                            em = ctr["sb"] % NEM
                            ctr["sb"] += 1
                            S.op("act", ACT(Ev[es], P[banks[comp]][:, :], AF.Exp), reads=[pb[banks[comp]]], writes=[EB[es]])
                            S.op("dve", TT(Emv[em], Ev[es], decv[hl][:, off:off + TB], ALU.mult),
                                 reads=[EB[es], decB[hl]], writes=[EmB[em]])
                            ems.append(em)
                        return ems

                    def pv_fn(tl, ems, hl=hl):
                        n, sc, first, last = tl
                        for comp in range(2):
                            S.op("pe", MM(P[4 + comp][:, :], vv[:, sc * 128:(sc + 1) * 128], Emv[ems[comp]], first, last),
                                 reads=[EmB[ems[comp]], vB], writes=[pb[4 + comp]])
                        for comp in range(2):
                            S.op("pe", MM(P[6][comp * 64:(comp + 1) * 64, :], ones[:, 0:64], Emv[ems[comp]], first, last),
                                 reads=[EmB[ems[comp]], constB], writes=[pb[6]])
                        if not last:
                            return
                        od, odB, sq2, sq2B = ods[n % 2], odBs[n % 2], sq2s[n % 2], sq2Bs[n % 2]
                        S.op("dve", CP(uA, P[4][:, :]), reads=[pb[4]], writes=[uAB])
                        S.op("act", SCOPY(uB_, P[5][:, :]), reads=[pb[5]], writes=[uBB])
                        S.op("act", ACT(rAB, P[6][:, :], AF.Ln), reads=[pb[6]], writes=[rABB])
                        S.op("act", ACT(rAB, rAB, AF.Exp, scale=-1.0), reads=[rABB], writes=[rABB])

                        def part1():
                            S.op("dve", CP(rX[64:128, :], rAB[0:64, :]), reads=[rABB], writes=[rXB])
                            S.op("dve", CP(rX[0:64, :], rAB[64:128, :]), reads=[rABB], writes=[rXB])
                            S.op("dve", TT(od[0:64, :], uA[0:64, :], rAB[0:64, :], ALU.mult), reads=[uAB, rABB], writes=[odB])
                            S.op("dve", TT(od[64:128, :], uA[64:128, :], rX[64:128, :], ALU.mult), reads=[uAB, rXB], writes=[odB])

                        def part1b():
                            S.op("dve", STT(tt_[0:64, :], uB_[0:64, :], neglam[0:64, :], rX[0:64, :], ALU.mult, ALU.mult),
                                 reads=[uBB, rXB, dynB], writes=[ttB])
                            S.op("dve", STT(tt_[64:128, :], uB_[64:128, :], neglam[64:128, :], rAB[64:128, :], ALU.mult, ALU.mult),
                                 reads=[uBB, rABB, dynB], writes=[ttB])
                            S.op("dve", TT(od, od, tt_, ALU.add), reads=[odB, ttB], writes=[odB])

                        def part2(n=n, hl=hl):
                            S.op("act", ACT(sq2, od, AF.Square), reads=[odB], writes=[sq2B])
                            bank = ctr["st"] % 4
                            ctr["st"] += 1
                            S.op("pe", MM(P[bank][:, :], ones[:, :], sq2, True, True), reads=[sq2B, constB], writes=[pb[bank]])

                            def part3():
                                S.op("act", ACT(rr, P[bank][:, :], AF.Ln, bias=EPS, scale=1.0 / 128), reads=[pb[bank]], writes=[rrB])
                                S.op("act", ACT(rr, rr, AF.Exp, scale=-0.5), reads=[rrB], writes=[rrB])
                                S.op("dve", STT(o_ap(hl, n), od, sgc, rr, ALU.mult, ALU.mult), reads=[odB, rrB, dynB], writes=[oB[hl][n]])

                            deferred.append([1, part3])

                        deferred.append([1, part1])
                        deferred.append([3, part1b])
                        deferred.append([6, part2, True])

                    flat_pipeline(tiles, s_fn, pv_fn, skew=3, flush=False)
                drain_deferred()
                wo_partial(lambda m: ("cwo", hp, m),
                           [lambda n: o_ap(0, n), lambda n: o_ap(1, n)], [oB[0], oB[1]])

        def mixer_d():
            S.barrier()
            o = A0
            ABv = ab(o, 16384); o += 16384
            cscv = ab(o, 512); o += 512
            assert o <= ARENA_W
            cscB = Buf(const=True)
            ABB = [Buf() for _ in range(16)]
            S.dma("sp", "tab", DMA(cscv, d_csc[:, :]), writes=[cscB])
            norm_to_h(C_ATTN + 24)

            def AB_ap(tc, g):
                return ABv[:, (tc * 4 + g) * TB:(tc * 4 + g + 1) * TB]

            k = 0
            for tc in range(16):
                for g in range(4):
                    bank = k % 8
                    for kc in range(2):
                        S.op("pe", MM(P[bank][:, :], h_tok(2 * g + kc, tc), cscv[:, kc * TB:(kc + 1) * TB], kc == 0, kc == 1),
                             reads=[hb[2 * g + kc][tc // 4], cscB], writes=[pb[bank]])
                    if k % 2 == 0:
                        S.op("act", SCOPY(AB_ap(tc, g), P[bank][:, :]), reads=[pb[bank]], writes=[ABB[tc]])
                    else:
                        S.op("dve", CP(AB_ap(tc, g), P[bank][:, :]), reads=[pb[bank]], writes=[ABB[tc]])
                    k += 1
            for n in range(NB):
                for tc in range(16):
                    for cs in range(2):
                        tt, tB = TR.get((n, tc, cs))
                        for e in range(NCH):
                            g, eh = e // 2, e % 2
                            lo = (tc * 4 + g) * TB + cs * 256 + eh * 128
                            S.op("pe", MM(P[e][:, :], ABv[:, lo:lo + 128], tt, tc == 0 and cs == 0, tc == 15 and cs == 1),
                                 reads=[ABB[tc], tB], writes=[pb[e]])
                for e in range(NCH):
                    if e % 2 == 0:
                        S.op("act", SCOPY(h_ap(e, n), P[e][:, :]), reads=[pb[e]], writes=[hb[e][n]])
                    else:
                        S.op("dve", CP(h_ap(e, n), P[e][:, :]), reads=[pb[e]], writes=[hb[e][n]])
            srcs = [(lambda n, kc=kc: h_ap(kc, n)) for kc in range(NCH)]
            wo_partial(lambda m: ("dwo", m), srcs, [hb[kc] for kc in range(NCH)])

        store_toks = []
        for s in range(nseq):
            for n in range(NB):
                for c in range(NCH):
                    S.dma("sp", "xl%d_%d" % (c, n), DMA(x_ap(c, n), xin[s, c * 128:(c + 1) * 128, n * TB:(n + 1) * TB]),
                          writes=[xb[c][n]])
            for l in layers:
                (mixer_a, mixer_b, mixer_c, mixer_d)[l]()
                if do_ffn:
                    ffn(l)
            S.barrier()
            if final_norm:
                for n in range(NB):
                    r = n % 2
                    stats_block(n, rstd[r], rstdB[r])
                    for c in range(NCH):
                        S.op("dve", STT(x_ap(c, n), x_ap(c, n), col(C_FINAL + c), rstd[r], ALU.mult, ALU.mult),
                             reads=[xb[c][n], rstdB[r], colsB], writes=[xb[c][n]])
            for n in range(NB):
                for c in range(NCH):
                    tk = S.dma("sp", "xs%d_%d" % (c, n), DMA(yout[s, c * 128:(c + 1) * 128, n * TB:(n + 1) * TB], x_ap(c, n)),
                               reads=[xb[c][n]])
                    store_toks.append(tk)
        S.wait("sp", store_toks[-NCH * NB:])
        assert WR.pos == len(wstream) and TR.pos == len(tstream)
        S.emit(st)
    return nc


_CACHE = {}


def host_inputs(inp, layers=(0, 1, 2, 3)):
    offs, wcols = weight_offsets(layers)
    wts = np.zeros((128, wcols), np.float32)
    for key, o in offs.items():
        wts[:, o:o + tile_cols(key)] = extract_tile(key, inp)
    if "tabs" not in _CACHE:
        _CACHE["tabs"] = const_tables()
    tabs = _CACHE["tabs"]
    shared = {
        "wts": wts,
        "cols": build_cols(inp),
        "lamb": np.ascontiguousarray(np.broadcast_to(np.asarray(inp["c_lambda"][0], np.float32).reshape(1, 256), (128, 256))),
        "rope": tabs["rope"], "dec": tabs["dec"], "icnt": tabs["icnt"], "csc": tabs["csc"], "ttab": tabs["ttab"], "perm": tabs["perm"],
    }
    return shared


def kernel(**inputs):
    inp = {k: np.asarray(v) for k, v in inputs.items()}
    xs = np.concatenate([inp["x_prompt"], inp["x_sample"]], axis=0)
    nseq = SEQ_PER_CORE
    shared = host_inputs(inp)
    nc = build_program(nseq)
    in_maps = []
    for c in range(NCORES):
        xc = np.ascontiguousarray(xs[c * nseq:(c + 1) * nseq].transpose(0, 2, 1))
        m = dict(shared)
        m["xin"] = xc
        in_maps.append(m)
    res = run_bass_kernel_spmd(nc, in_maps, core_ids=list(range(NCORES)))
    ys = np.concatenate([np.asarray(r["yout"]).transpose(0, 2, 1) for r in res.results], axis=0)
    ys = np.ascontiguousarray(ys, dtype=np.float32)
    nb = inp["x_prompt"].shape[0]
    return ys[:nb], ys[nb:]
```

```python
import bisect
import math
from contextlib import ExitStack

import ml_dtypes
import numpy as np

import concourse.bass as bass
import concourse.mybir as mybir
from concourse.bass_utils import run_bass_kernel_spmd

F32 = mybir.dt.float32
BF16 = mybir.dt.bfloat16
ALU = mybir.AluOpType
AF = mybir.ActivationFunctionType
AX = mybir.AxisListType

T = 2048
D = 1024
NCH = 8
TB = 512
NB = 4
DFF = 2816
EPS = 1e-6
NCORES = 8
SEQ_PER_CORE = 5
LAM_INIT = 0.8 - 0.6 * math.exp(-0.3 * 2)
WSLOT = 1408
SKEW = 5
ALIBI_SKIP = 40.0
NWSLOT = 6
NTSLOT = 6
ARENA_W = 20480

C_ATTN = 0
C_FFN = 32
C_FINAL = 64
C_BSCALE = 72
C_GQ, C_GQS, C_GK, C_GKS, C_SUBG = 80, 81, 82, 83, 84
NCOL = 88


class Buf:
    __slots__ = ("lw", "rd", "const", "psum")

    def __init__(self, const=False, psum=False):
        self.lw = None
        self.rd = []
        self.const = const
        self.psum = psum


class Sched:
    ENGS = ("pe", "act", "dve", "pool", "sp")

    def __init__(self, nc):
        self.nc = nc
        self.ops = {e: [] for e in self.ENGS}
        self.seq = {e: 0 for e in self.ENGS}
        self.needed = {e: set() for e in self.ENGS}
        self.known = {e: {} for e in self.ENGS}
        self.dma_cnt = {}

    def _reduce(self, eng, toks, same_toks=()):
        best = {}
        for t in toks:
            kind, key, val = t
            if kind == "e" and key == eng:
                continue
            k = (kind, key)
            if val > best.get(k, -1):
                best[k] = val
        if eng in ("act", "dve", "pool"):
            for t in same_toks:
                if t is not None and t[0] == "e" and t[1] == eng:
                    k = ("e", eng)
                    if t[2] > best.get(k, -1):
                        best[k] = t[2]
        waits = []
        kn = self.known[eng]
        for k, val in best.items():
            if kn.get(k, -1) >= val:
                continue
            kn[k] = val
            waits.append((k[0], k[1], val))
            if k[0] == "e":
                self.needed[k[1]].add(val)
        return waits

    def _collect(self, eng, reads, writes):
        toks = []
        for b in reads:
            if b.lw is not None:
                toks.append(b.lw)
            if b.psum:
                toks.extend(b.rd)
        for b in writes:
            if b.lw is not None:
                toks.append(b.lw)
            toks.extend(b.rd)
        return self._reduce(eng, toks, toks)

    def _mark(self, tok, reads, writes):
        for b in writes:
            b.lw = tok
            b.rd = []
        for b in reads:
            if not b.const:
                b.rd.append(tok)

    def op(self, eng, fn, reads=(), writes=()):
        waits = self._collect(eng, reads, writes)
        self.seq[eng] += 1
        s = self.seq[eng]
        tok = ("e", eng, s)
        self.ops[eng].append((waits, fn, s, None))
        self._mark(tok, reads, writes)
        return tok

    def dma(self, eng, sem, fn, reads=(), writes=()):
        waits = self._collect(eng, reads, writes)
        self.dma_cnt[sem] = self.dma_cnt.get(sem, 0) + 16
        tok = ("d", sem, self.dma_cnt[sem])
        self.seq[eng] += 1
        self.ops[eng].append((waits, fn, self.seq[eng], sem))
        self._mark(tok, reads, writes)
        return tok

    def wait(self, eng, toks):
        waits = self._reduce(eng, [t for t in toks if t is not None])
        if waits:
            self.ops[eng].append((waits, None, None, None))

    def last_tok(self, eng):
        return ("e", eng, self.seq[eng]) if self.seq[eng] > 0 else None

    def barrier(self):
        toks = [self.last_tok(e) for e in ("pe", "act", "dve")]
        for e in ("pe", "act", "dve", "sp"):
            self.wait(e, toks)

    def emit(self, stack):
        nc = self.nc
        esem = {e: stack.enter_context(nc.semaphore("s_" + e)) for e in self.ENGS}
        dsem = {n: stack.enter_context(nc.semaphore("d_" + n)) for n in self.dma_cnt}
        ranks = {e: sorted(self.needed[e]) for e in self.ENGS}
        block = stack.enter_context(nc.Block())

        def run(e, handle):
            needed = self.needed[e]
            for waits, fn, s, dma in self.ops[e]:
                for kind, key, val in waits:
                    if kind == "e":
                        handle.wait_ge(esem[key], bisect.bisect_right(ranks[key], val))
                    else:
                        handle.wait_ge(dsem[key], val)
                if fn is None:
                    continue
                ins = fn(handle)
                if dma is not None:
                    ins.then_inc(dsem[dma], 16)
                elif s in needed:
                    ins.then_inc(esem[e], 1)

        if self.ops["pe"]:
            @block.tensor
            def _(h):
                run("pe", h)
        if self.ops["act"]:
            @block.scalar
            def _(h):
                run("act", h)
        if self.ops["dve"]:
            @block.vector
            def _(h):
                run("dve", h)
        if self.ops["pool"]:
            @block.gpsimd
            def _(h):
                run("pool", h)
        if self.ops["sp"]:
            @block.sync
            def _(h):
                run("sp", h)


def MM(out, lhsT, rhs, start, stop):
    return lambda e: e.matmul(out, lhsT, rhs, start=bool(start), stop=bool(stop))


def ACT(out, in_, func, bias=0.0, scale=1.0):
    return lambda e: e.activation(out, in_, func, bias=bias, scale=scale)


def TT(out, in0, in1, op):
    return lambda e: e.tensor_tensor(out, in0, in1, op)


def STT(out, in0, scalar, in1, op0, op1):
    return lambda e: e.scalar_tensor_tensor(out, in0, scalar, in1, op0, op1)


def TS(out, in0, s1, s2, op0, op1=None):
    if op1 is None:
        return lambda e: e.tensor_scalar(out, in0, s1, None, op0)
    return lambda e: e.tensor_scalar(out, in0, s1, s2, op0, op1)


def CP(out, in_):
    return lambda e: e.tensor_copy(out, in_)


def SMUL(out, in_, c):
    return lambda e: e.mul(out, in_, c)


def SCOPY(out, in_):
    return lambda e: e.copy(out, in_)


def RCP(out, in_):
    return lambda e: e.reciprocal(out, in_)


def MSET(ap, v):
    return lambda e: e.memset(ap, v)


def DMA(out, in_):
    return lambda e: e.dma_start(out=out, in_=in_)


class Ring:
    def __init__(self, S, eng, prefix, slot_ap_fn, nslots, stream):
        self.S, self.eng, self.prefix = S, eng, prefix
        self.slot_ap = slot_ap_fn
        self.n = nslots
        self.stream = stream
        self.pos = 0
        self.loaded = 0
        self.bufs = [Buf() for _ in range(nslots)]

    def get(self, key):
        i = self.pos
        k, _, ncols = self.stream[i]
        assert k == key, (k, key)
        lim = min(len(self.stream), i + self.n - 1)
        while self.loaded < lim:
            j = self.loaded
            _, src, nc_ = self.stream[j]
            slot = j % self.n
            self.S.dma(self.eng, "%s%d" % (self.prefix, slot), DMA(self.slot_ap(slot, nc_), src),
                       writes=[self.bufs[slot]])
            self.loaded += 1
        self.pos += 1
        slot = i % self.n
        return self.slot_ap(slot, ncols), self.bufs[slot]


def ffn_catalog(l):
    L = []
    for G in range(2):
        for j in range(11):
            jj = G * 11 + j
            L += [("gate", l, jj), ("up", l, jj)]
        for m in range(8):
            L.append(("down", l, m, G))
    return L


def catalog(layers=(0, 1, 2, 3)):
    L = []
    for l in layers:
        if l == 0:
            for g in range(4):
                L += [("aq", g, 0, 0), ("aq", g, 1, 0), ("ak", g, 0), ("av", g)]
                if g >= 1:
                    L += [("awo", g - 1, m) for m in range(8)]
            L += [("awo", 3, m) for m in range(8)]
        elif l == 1:
            L += [("bp", g) for g in range(4)]
        elif l == 2:
            for hp in range(4):
                for hl in range(2):
                    h = 2 * hp + hl
                    L += [("cq", h), ("ck", h), ("cv", h)]
                L += [("cwo", hp, m) for m in range(8)]
        else:
            L += [("dwo", m) for m in range(8)]
        L += ffn_catalog(l)
    return L


def tile_cols(key):
    k = key[0]
    if k in ("gate", "up", "aq", "ak", "cq", "ck", "cv", "dwo"):
        return 1024
    if k == "down":
        return 1408
    if k in ("av", "bp"):
        return 512
    if k in ("awo", "cwo"):
        return 256
    raise KeyError(key)


def _kc_tile(w):
    K, M = w.shape
    return np.ascontiguousarray(w.reshape(K // 128, 128, M).transpose(1, 0, 2)).reshape(128, -1)


_SWAP64 = np.arange(64) ^ 1


def extract_tile(key, W):
    k = key[0]
    if k == "gate":
        return _kc_tile(W["w_gate"][key[1]][:, key[2] * 128:(key[2] + 1) * 128])
    if k == "up":
        return _kc_tile(W["w_up"][key[1]][:, key[2] * 128:(key[2] + 1) * 128])
    if k == "down":
        _, l, m, G = key
        return _kc_tile(W["w_down"][l][G * 1408:(G + 1) * 1408, m * 128:(m + 1) * 128])
    if k == "aq":
        _, g, cc, sw = key
        cols = (2 * g + cc) * 128 + np.arange(128)
        if sw:
            cols = cols ^ 1
        return _kc_tile(W["a_w_qkv"][0][:, cols])
    if k == "ak":
        _, g, sw = key
        c64 = np.arange(64)
        if sw:
            c64 = c64 ^ 1
        cols = 1024 + g * 64 + np.concatenate([c64, c64])
        return _kc_tile(W["a_w_qkv"][0][:, cols])
    if k == "av":
        g = key[1]
        cols = 1280 + g * 64 + np.arange(64)
        return _kc_tile(W["a_w_qkv"][0][:, cols])
    if k == "awo":
        _, g, m = key
        return _kc_tile(W["a_w_o"][0][g * 256:(g + 1) * 256, m * 128:(m + 1) * 128])
    if k == "bp":
        g = key[1]
        return _kc_tile(W["b_w_pool"][0][g])
    if k == "cq":
        h = key[1]
        return _kc_tile(W["c_w_qkv"][0][:, h * 128:(h + 1) * 128])
    if k == "ck":
        h = key[1]
        return _kc_tile(W["c_w_qkv"][0][:, 1024 + h * 128:1024 + (h + 1) * 128])
    if k == "cv":
        h = key[1]
        return _kc_tile(W["c_w_qkv"][0][:, 2048 + h * 128:2048 + (h + 1) * 128])
    if k == "cwo":
        _, hp, m = key
        return _kc_tile(W["c_w_o"][0][hp * 256:(hp + 1) * 256, m * 128:(m + 1) * 128])
    if k == "dwo":
        m = key[1]
        return _kc_tile(W["d_w_o"][0][:, m * 128:(m + 1) * 128])
    raise KeyError(key)


def weight_offsets(layers):
    offs = {}
    o = 0
    for key in catalog(layers):
        if key not in offs:
            offs[key] = o
            o += tile_cols(key)
    return offs, o


def const_tables():
    tabs = {}
    p = np.arange(128)
    d = p % 64
    i = d // 2
    t = np.arange(T)
    r = (t // 64).astype(np.float32)
    c = (t % 64).astype(np.float32)
    n = 16
    freqs = (np.float32(10000.0) ** (-np.arange(n, dtype=np.float32) / np.float32(n))).astype(np.float32)
    ang = np.where((i < 16)[:, None], r[None, :] * freqs[np.minimum(i, 15)][:, None],
                   c[None, :] * freqs[np.maximum(i - 16, 0)][:, None]).astype(np.float32)
    cosv = np.cos(ang.astype(np.float64))
    sinv = np.sin(ang.astype(np.float64))
    sgn = np.where(d % 2 == 0, -1.0, 1.0)[:, None]
    tabs["rope"] = np.concatenate([cosv, sinv * sgn], axis=1).astype(np.float32)
    cc = np.arange(4096)
    dist = np.abs(cc[None, :] - 2048 - p[:, None]).astype(np.float64)
    tabs["dec"] = np.concatenate([np.exp(-(2.0 ** (-(h + 1))) * dist) for h in range(8)], axis=1).astype(ml_dtypes.bfloat16)
    ic = np.zeros((4, T), np.float64)
    for gi, win in enumerate((2, 4, 8, 16)):
        lo = np.clip(t - win // 2, 0, T - 1)
        hi = np.clip(t + win // 2 - 1, 0, T - 1)
        ic[gi] = 1.0 / (hi - lo + 1)
    tabs["icnt"] = np.ascontiguousarray(np.broadcast_to(ic.reshape(1, 4 * T), (128, 4 * T))).astype(np.float32)
    cidx = (np.arange(2)[None, :, None] * 128 + p[:, None, None])
    e = np.arange(256)[None, None, :]
    a = 2.0 * np.pi * ((cidx * e) % 256) / 256.0
    csc = np.concatenate([np.cos(a), np.sin(a)], axis=2) / 16.0
    tabs["csc"] = csc.reshape(128, 1024).astype(ml_dtypes.bfloat16)
    pm = np.zeros((128, 128), np.float32)
    pm[np.arange(128) ^ 1, np.arange(128)] = 1.0
    tabs["perm"] = pm.astype(ml_dtypes.bfloat16)
    tt = np.zeros((128, NB, 16, 2, TB), np.float32)
    sc_ = 1.0 / math.sqrt(T)
    for tc in range(16):
        trow = (tc * 128 + p)[:, None]
        ang2 = 2.0 * np.pi * ((trow * t[None, :]) % T) / T
        cm = (np.cos(ang2) * sc_).astype(np.float32).reshape(128, NB, TB)
        sm = (-np.sin(ang2) * sc_).astype(np.float32).reshape(128, NB, TB)
        tt[:, :, tc, 0, :] = cm
        tt[:, :, tc, 1, :] = sm
    tabs["ttab"] = tt.reshape(128, NB * 16 * 2 * TB).astype(ml_dtypes.bfloat16)
    return tabs


def build_cols(inp):
    cols = np.zeros((128, NCOL), np.float32)

    def colify(v):
        return np.asarray(v, np.float32).reshape(8, 128).T

    for l in range(4):
        cols[:, C_ATTN + 8 * l:C_ATTN + 8 * l + 8] = colify(inp["attn_norm"][l])
        cols[:, C_FFN + 8 * l:C_FFN + 8 * l + 8] = colify(inp["ffn_norm"][l])
    cols[:, C_FINAL:C_FINAL + 8] = colify(inp["final_norm"])
    cols[:, C_BSCALE:C_BSCALE + 8] = colify(inp["b_scale"][0])
    d = np.arange(128) % 64
    cols[:, C_GQ] = inp["a_q_gain"][0][d]
    cols[:, C_GQS] = inp["a_q_gain"][0][d ^ 1]
    cols[:, C_GK] = inp["a_k_gain"][0][d]
    cols[:, C_GKS] = inp["a_k_gain"][0][d ^ 1]
    cols[:, C_SUBG] = inp["c_sub_gain"][0]
    return cols


def build_program(nseq, layers=(0, 1, 2, 3), do_ffn=True, final_norm=True, debug=False):
    nc = bass.Bass("TRN2", target_bir_lowering=False)
    dbg_n = [0]
    offs, wcols = weight_offsets(layers)
    xin = nc.dram_tensor("xin", [nseq, D, T], F32, kind="ExternalInput").ap()
    yout = nc.dram_tensor("yout", [nseq, D, T], F32, kind="ExternalOutput").ap()
    wts = nc.dram_tensor("wts", [128, wcols], F32, kind="ExternalInput").ap()
    d_cols = nc.dram_tensor("cols", [128, NCOL], F32, kind="ExternalInput").ap()
    d_lamb = nc.dram_tensor("lamb", [128, 256], F32, kind="ExternalInput").ap()
    d_rope = nc.dram_tensor("rope", [128, 4096], F32, kind="ExternalInput").ap()
    d_dec = nc.dram_tensor("dec", [128, 8 * 4096], BF16, kind="ExternalInput").ap()
    d_icnt = nc.dram_tensor("icnt", [128, 4 * T], F32, kind="ExternalInput").ap()
    d_csc = nc.dram_tensor("csc", [128, 1024], BF16, kind="ExternalInput").ap()
    d_perm = nc.dram_tensor("perm", [128, 128], BF16, kind="ExternalInput").ap()
    d_ttab = nc.dram_tensor("ttab", [128, NB * 16 * 2 * TB], BF16, kind="ExternalInput").ap()

    st = ExitStack()
    with st:
        xT = st.enter_context(nc.sbuf_tensor("xT", [128, NCH * T], F32))
        hT = st.enter_context(nc.sbuf_tensor("hT", [128, NCH * T], BF16))
        wring = st.enter_context(nc.sbuf_tensor("wring", [128, NWSLOT * WSLOT], BF16))
        tring = st.enter_context(nc.sbuf_tensor("tring", [128, NTSLOT * TB], BF16))
        cols = st.enter_context(nc.sbuf_tensor("colst", [128, NCOL], F32))
        dyn = st.enter_context(nc.sbuf_tensor("dyn", [128, 8], F32))
        lamt = st.enter_context(nc.sbuf_tensor("lamt", [128, 256], F32))
        lamp = st.enter_context(nc.sbuf_tensor("lamp", [128, 128], F32))
        ones = st.enter_context(nc.sbuf_tensor("ones", [128, 128], BF16))
        bdiag = st.enter_context(nc.sbuf_tensor("bdiag", [128, 128], BF16))
        permt = st.enter_context(nc.sbuf_tensor("permt", [128, 128], BF16))
        arena = st.enter_context(nc.sbuf_tensor("arena", [128, ARENA_W], F32))
        P = [st.enter_context(nc.psum_tensor("ps%d" % i, [128, TB], F32)) for i in range(8)]

        S = Sched(nc)
        pb = [Buf(psum=True) for _ in range(8)]
        xb = [[Buf() for _ in range(NB)] for _ in range(NCH)]
        hb = [[Buf() for _ in range(NB)] for _ in range(NCH)]
        colsB = Buf(const=True)
        constB = Buf(const=True)
        lamB = Buf()

        def x_ap(c, n):
            return xT[:, c * T + n * TB:c * T + (n + 1) * TB]

        def h_ap(c, n):
            return hT[:, c * T + n * TB:c * T + (n + 1) * TB]

        def h_tok(c, tc):
            return hT[:, c * T + tc * 128:c * T + (tc + 1) * 128]

        def col(i):
            return cols[:, i:i + 1]

        def af(o, n):
            return arena[:, o:o + n]

        def ab(o, n):
            return arena[:, o:o + n].bitcast(BF16)

        sqb = [ab(256 * i, 256) for i in range(4)]
        sqB = [Buf() for _ in range(4)]
        tsq = [af(1024 + 512 * i, 512) for i in range(2)]
        tsqB = [Buf() for _ in range(2)]
        rstd = [af(2048 + 512 * i, 512) for i in range(2)]
        rstdB = [Buf() for _ in range(2)]
        A0 = 3072

        cat = catalog(layers) if do_ffn else [k for k in catalog(layers) if k[0] not in ("gate", "up", "down")]
        wstream = []
        for _ in range(nseq):
            for key in cat:
                ncl = tile_cols(key)
                wstream.append((key, wts[:, offs[key]:offs[key] + ncl], ncl))
        WR = Ring(S, "pool", "w", lambda s, n_: wring[:, s * WSLOT:s * WSLOT + n_], NWSLOT, wstream)
        tstream = []
        if 3 in layers:
            for _ in range(nseq):
                for n in range(NB):
                    for tc in range(16):
                        for cs in range(2):
                            o = ((n * 16 + tc) * 2 + cs) * TB
                            tstream.append(((n, tc, cs), d_ttab[:, o:o + TB], TB))
        TR = Ring(S, "sp", "t", lambda s, n_: tring[:, s * TB:s * TB + n_], NTSLOT, tstream)

        ctr = {"st": 0, "e": 0, "sb": 0, "nb": 0}
        deferred = []
        IC01 = 16384
        ic01B = Buf(const=True)
        pre = {"ic01": False}

        S.dma("sp", "c0", DMA(cols[:], d_cols[:, :]), writes=[colsB])
        S.dma("sp", "c1", DMA(lamt[:], d_lamb[:, :]), writes=[lamB])
        S.dma("sp", "c2", DMA(permt[:], d_perm[:, :]), writes=[constB])
        S.op("dve", MSET(ones[:], 1.0), writes=[constB])
        S.op("dve", MSET(bdiag[:], 0.0), writes=[constB])
        S.op("dve", MSET(bdiag[0:64, 0:64], 1.0), writes=[constB])
        S.op("dve", MSET(bdiag[64:128, 64:128], 1.0), writes=[constB])
        lpB = Buf()
        S.op("dve", TT(lamp[:, 0:64], lamt[:, 0:64], lamt[:, 64:128], ALU.mult), reads=[lamB], writes=[lpB])
        S.op("dve", TT(lamp[:, 64:128], lamt[:, 128:192], lamt[:, 192:256], ALU.mult), reads=[lamB], writes=[lpB])
        dynB = Buf()
        S.op("dve", lambda e: e.reduce_sum(dyn[:, 2:3], lamp[:, 0:64], AX.X), reads=[lpB], writes=[dynB])
        S.op("dve", lambda e: e.reduce_sum(dyn[:, 3:4], lamp[:, 64:128], AX.X), reads=[lpB], writes=[dynB])
        S.op("act", ACT(dyn[:, 4:6], dyn[:, 2:4], AF.Exp), reads=[dynB], writes=[dynB])
        S.op("dve", TT(dyn[:, 6:7], dyn[:, 4:5], dyn[:, 5:6], ALU.subtract), reads=[dynB], writes=[dynB])
        S.op("dve", TS(dyn[:, 0:1], dyn[:, 6:7], LAM_INIT, -1.0, ALU.add, ALU.mult), reads=[dynB], writes=[dynB])
        S.op("dve", TS(dyn[:, 1:2], col(C_SUBG), 1.0 - LAM_INIT, None, ALU.mult), reads=[dynB, colsB], writes=[dynB])
        dynB.const = True

        def dbg(name, ap, bufs, shape, dt):
            if not debug or dbg_n[0] > 12:
                return
            dbg_n[0] += 1
            dten = nc.dram_tensor("dbg_" + name, shape, dt, kind="ExternalOutput").ap()
            S.dma("sp", "dbg%d" % dbg_n[0], DMA(dten[:, :], ap), reads=bufs)

        def stats_block(n, dst, dstB, inv_n=1.0 / D, src=None):
            bank = 6 + (ctr["nb"] % 2)
            ctr["nb"] += 1
            for c in range(NCH):
                q = c % 4
                S.op("act", ACT(sqb[q], x_ap(c, n), AF.Square), reads=[xb[c][n]], writes=[sqB[q]])
                S.op("pe", MM(P[bank][:, :], ones[:, :], sqb[q], c == 0, c == NCH - 1),
                     reads=[sqB[q], constB], writes=[pb[bank]])
            tq = ctr["nb"] % 2
            S.op("act", ACT(tsq[tq], P[bank][:, :], AF.Ln, bias=EPS, scale=inv_n), reads=[pb[bank]], writes=[tsqB[tq]])
            S.op("act", ACT(dst, tsq[tq], AF.Exp, scale=-0.5), reads=[tsqB[tq]], writes=[dstB])

        def norm_to_h(gcol0):
            for n in range(NB):
                r = n % 2
                stats_block(n, rstd[r], rstdB[r])
                for c in range(NCH):
                    S.op("dve", STT(h_ap(c, n), x_ap(c, n), col(gcol0 + c), rstd[r], ALU.mult, ALU.mult),
                         reads=[xb[c][n], rstdB[r], colsB], writes=[hb[c][n]])

        def add_to_x(m, base, scale_col=None):
            for n in range(NB):
                if scale_col is None:
                    S.op("dve", TT(x_ap(m, n), x_ap(m, n), P[base + n][:, :], ALU.add),
                         reads=[pb[base + n], xb[m][n]], writes=[xb[m][n]])
                else:
                    S.op("dve", STT(x_ap(m, n), P[base + n][:, :], scale_col, x_ap(m, n), ALU.mult, ALU.add),
                         reads=[pb[base + n], xb[m][n], colsB], writes=[xb[m][n]])

        def wo_partial(keyfn, srcs, srcBs):
            nk = len(srcs)
            for m in range(NCH):
                wt, wB = WR.get(keyfn(m))
                base = 0 if m % 2 == 0 else 4
                for kc in range(nk):
                    for n in range(NB):
                        S.op("pe", MM(P[base + n][:, :], wt[:, kc * 128:(kc + 1) * 128], srcs[kc](n), kc == 0, kc == nk - 1),
                             reads=[wB, srcBs[kc][n]], writes=[pb[base + n]])
                add_to_x(m, base)

        def drain_deferred():
            while deferred:
                d_ = deferred.pop(0)
                d_[1]()

        def flat_pipeline(tiles, s_fn, pv_fn, skew=SKEW, flush=True):
            hs = {}
            nt = len(tiles)
            for t in range(nt + skew):
                if t < nt:
                    hs[t] = s_fn(tiles[t])
                if t >= skew:
                    pv_fn(tiles[t - skew], hs.pop(t - skew))
                    for d_ in list(deferred):
                        d_[0] -= 1
                        if d_[0] <= 0:
                            deferred.remove(d_)
                            d_[1]()
            if flush:
                drain_deferred()
            else:
                keep = []
                while deferred:
                    d_ = deferred.pop(0)
                    if len(d_) > 2 and d_[2]:
                        keep.append(d_)
                    else:
                        d_[1]()
                deferred.extend(keep)

        def ffn(l):
            S.barrier()
            if l == 0 and 1 in layers:
                S.dma("sp", "tab2", DMA(arena[:, IC01:IC01 + 4096], d_icnt[:, 0:4096]), writes=[ic01B])
                pre["ic01"] = True
            norm_to_h(C_FFN + 8 * l)
            act_o = A0
            actv = ab(act_o, 11264)
            actB = [[Buf() for _ in range(NB)] for _ in range(11)]
            sg_o = A0 + 11264
            sg = [af(sg_o + 512 * i, 512) for i in range(4)]
            sgB = [Buf() for _ in range(4)]

            def act_ap(j, n):
                return actv[:, j * T + n * TB:j * T + (n + 1) * TB]

            for G in range(2):
                for j in range(11):
                    jj = G * 11 + j
                    gt, gB = WR.get(("gate", l, jj))
                    ut, uB = WR.get(("up", l, jj))
                    for kc in range(NCH):
                        for n in range(NB):
                            S.op("pe", MM(P[n][:, :], gt[:, kc * 128:(kc + 1) * 128], h_ap(kc, n), kc == 0, kc == NCH - 1),
                                 reads=[gB, hb[kc][n]], writes=[pb[n]])
                    for kc in range(NCH):
                        for n in range(NB):
                            S.op("pe", MM(P[4 + n][:, :], ut[:, kc * 128:(kc + 1) * 128], h_ap(kc, n), kc == 0, kc == NCH - 1),
                                 reads=[uB, hb[kc][n]], writes=[pb[4 + n]])
                    for n in range(NB):
                        S.op("act", ACT(sg[n], P[n][:, :], AF.Silu), reads=[pb[n]], writes=[sgB[n]])
                        S.op("dve", TT(act_ap(j, n), sg[n], P[4 + n][:, :], ALU.mult),
                             reads=[sgB[n], pb[4 + n]], writes=[actB[j][n]])
                for m in range(NCH):
                    dt_, dB = WR.get(("down", l, m, G))
                    base = 0 if m % 2 == 0 else 4
                    for j in range(11):
                        for n in range(NB):
                            S.op("pe", MM(P[base + n][:, :], dt_[:, j * 128:(j + 1) * 128], act_ap(j, n), j == 0, j == 10),
                                 reads=[dB, actB[j][n]], writes=[pb[base + n]])
                    add_to_x(m, base)

        def mixer_a():
            S.barrier()
            o = A0
            ropeC = af(o, 2048); o += 2048
            ropeS = af(o, 2048); o += 2048
            ropeB = Buf(const=True)
            qv = ab(o, 2048); o += 2048
            kv = ab(o, 1024); o += 1024
            vv = ab(o, 1024); o += 1024
            ov = ab(o, 2048); o += 2048
            NEA = 8
            Ev = [ab(o + 256 * i, 256) for i in range(NEA)]; o += 256 * NEA
            EB = [Buf() for _ in range(NEA)]
            sq2 = [ab(o + 256 * i, 256) for i in range(2)]; o += 512
            sq2B = [Buf() for _ in range(2)]
            abf = [ab(o + 256 * i, 256) for i in range(2)]; o += 512
            abfB = [Buf() for _ in range(2)]
            f32t = [af(o + 512 * i, 512) for i in range(7)]; o += 3584
            assert o <= ARENA_W, o
            rs2, t1v, rcv = f32t[0:2], f32t[2:4], f32t[4:6]
            t2v = [f32t[6], f32t[6]]
            rs2B, t1B, rcB = ([Buf() for _ in range(2)] for _ in range(3))
            t2B_ = Buf()
            t2B = [t2B_, t2B_]
            qB = [[Buf() for _ in range(NB)] for _ in range(2)]
            kB = [Buf() for _ in range(NB)]
            vB = Buf()
            oB = [[Buf() for _ in range(NB)] for _ in range(2)]
            S.dma("sp", "tab", DMA(arena[:, A0:A0 + 4096], d_rope[:, :]), writes=[ropeB])
            vv3 = vv.rearrange("p (t e) -> p t e", e=128)
            S.op("dve", MSET(vv3[:, :, 64:128], 1.0), writes=[vB])
            norm_to_h(C_ATTN + 0)

            def q_ap(cc, n):
                return qv[:, cc * T + n * TB:cc * T + (n + 1) * TB]

            def o_ap(cc, n):
                return ov[:, cc * T + n * TB:cc * T + (n + 1) * TB]

            for g in range(4):
                ptiles = [(kind, cc, n) for kind, cc in (("q", 0), ("q", 1), ("k", 0)) for n in range(NB)]
                wcur = {}

                def stage1(i):
                    kind, cc, n = ptiles[i]
                    r = i % 2
                    PA = 0 if r == 0 else 3
                    if n == 0:
                        wcur[0] = WR.get(("aq", g, cc, 0)) if kind == "q" else WR.get(("ak", g, 0))
                    wa, waB = wcur[0]
                    for kc in range(NCH):
                        S.op("pe", MM(P[PA][:, :], wa[:, kc * 128:(kc + 1) * 128], h_ap(kc, n), kc == 0, kc == NCH - 1),
                             reads=[waB, hb[kc][n]], writes=[pb[PA]])
                    S.op("act", ACT(sq2[r], P[PA][:, :], AF.Square), reads=[pb[PA]], writes=[sq2B[r]])
                    S.op("act", SCOPY(abf[r], P[PA][:, :]), reads=[pb[PA]], writes=[abfB[r]])

                def stage2(i):
                    kind, cc, n = ptiles[i]
                    r = i % 2
                    PA, PB_, PR = (0, 1, 2) if r == 0 else (3, 4, 5)
                    if kind == "q":
                        gc, gsc, sc_ = col(C_GQ), col(C_GQS), 0.125
                    else:
                        gc, gsc, sc_ = col(C_GK), col(C_GKS), 1.0
                    S.op("pe", MM(P[PB_][:, :], permt[:, :], abf[r], True, True), reads=[abfB[r], constB], writes=[pb[PB_]])
                    S.op("pe", MM(P[PR][:, :], bdiag[:, :], sq2[r], True, True), reads=[sq2B[r], constB], writes=[pb[PR]])
                    S.op("act", ACT(rs2[r], P[PR][:, :], AF.Ln, bias=EPS, scale=1.0 / 64), reads=[pb[PR]], writes=[rs2B[r]])
                    S.op("act", ACT(rs2[r], rs2[r], AF.Exp, scale=-0.5), reads=[rs2B[r]], writes=[rs2B[r]])
                    S.op("dve", STT(t1v[r], P[PA][:, :], gc, ropeC[:, n * TB:(n + 1) * TB], ALU.mult, ALU.mult),
                         reads=[pb[PA], ropeB, colsB], writes=[t1B[r]])
                    S.op("dve", STT(t2v[r], P[PB_][:, :], gsc, ropeS[:, n * TB:(n + 1) * TB], ALU.mult, ALU.mult),
                         reads=[pb[PB_], ropeB, colsB], writes=[t2B[r]])
                    S.op("dve", TT(t1v[r], t1v[r], t2v[r], ALU.add), reads=[t1B[r], t2B[r]], writes=[t1B[r]])
                    if kind == "q":
                        S.op("dve", STT(q_ap(cc, n), t1v[r], sc_, rs2[r], ALU.mult, ALU.mult), reads=[t1B[r], rs2B[r]], writes=[qB[cc][n]])
                    else:
                        S.op("dve", STT(kv[:, n * TB:(n + 1) * TB], t1v[r], sc_, rs2[r], ALU.mult, ALU.mult),
                             reads=[t1B[r], rs2B[r]], writes=[kB[n]])

                for i in range(len(ptiles) + 1):
                    if i < len(ptiles):
                        stage1(i)
                    if i >= 1:
                        stage2(i - 1)
                wv, wvB = WR.get(("av", g))
                for half in range(2):
                    bank = 6 + half
                    for j in range(8):
                        tc = half * 8 + j
                        for kc in range(NCH):
                            S.op("pe", MM(P[bank][:, j * 64:(j + 1) * 64], h_tok(kc, tc), wv[:, kc * 64:(kc + 1) * 64], kc == 0, kc == NCH - 1),
                                 reads=[wvB, hb[kc][tc // 4]], writes=[pb[bank]])
                    S.op("act", SCOPY(vv3[:, half * 8:(half + 1) * 8, 0:64], P[bank][:, :].rearrange("p (t e) -> p t e", e=64)),
                         reads=[pb[bank]], writes=[vB])
                if g == 0:
                    dbg("q", qv, [qB[0][0], qB[0][1], qB[0][2], qB[0][3], qB[1][0], qB[1][1], qB[1][2], qB[1][3]], [128, 2 * T], BF16)
                    dbg("k", kv, kB, [128, T], BF16)
                    dbg("v", vv, [vB], [128, T], BF16)
                    dbg("h", hT[:, :], [hb[c_][n_] for c_ in range(NCH) for n_ in range(NB)], [128, NCH * T], BF16)
                if g >= 1:
                    wo_partial(lambda m: ("awo", g - 1, m),
                               [lambda n: o_ap(0, n), lambda n: o_ap(1, n)], [oB[0], oB[1]])
                tiles = []
                for cc in range(2):
                    for n in range(NB):
                        ub = 4 + 2 * (ctr["sb"] % 2)
                        ctr["sb"] += 1
                        for sc in range(16):
                            tiles.append((cc, n, sc, ub))

                def s_fn(tl):
                    cc, n, sc, ub = tl
                    hs = []
                    banks = []
                    for ph in range(2):
                        banks.append(ctr["st"] % 4)
                        ctr["st"] += 1
                    for ph in range(2):
                        psl = slice(ph * 64, (ph + 1) * 64)
                        S.op("pe", MM(P[banks[ph]][:, :], kv[psl, sc * 128:(sc + 1) * 128], qv[psl, cc * T + n * TB:cc * T + (n + 1) * TB], True, True),
                             reads=[kB[sc // 4], qB[cc][n]], writes=[pb[banks[ph]]])
                    for ph in range(2):
                        es = ctr["e"] % NEA
                        ctr["e"] += 1
                        S.op("act", ACT(Ev[es], P[banks[ph]][:, :], AF.Exp), reads=[pb[banks[ph]]], writes=[EB[es]])
                        hs.append(es)
                    return hs

                def pv_fn(tl, hs):
                    cc, n, sc, ub = tl
                    for ph in range(2):
                        U = ub + ph
                        S.op("pe", MM(P[U][:, :], vv3[:, sc, :], Ev[hs[ph]], sc == 0, sc == 15),
                             reads=[EB[hs[ph]], vB], writes=[pb[U]])
                    if sc == 15:
                        for ph in range(2):
                            U = ub + ph
                            psl = slice(ph * 64, (ph + 1) * 64)
                            S.op("dve", RCP(rcv[ph][64:128, :], P[U][64:128, :]), reads=[pb[U]], writes=[rcB[ph]])
                            S.op("dve", TT(ov[psl, cc * T + n * TB:cc * T + (n + 1) * TB], P[U][0:64, :], rcv[ph][64:128, :], ALU.mult),
                                 reads=[pb[U], rcB[ph]], writes=[oB[cc][n]])

                flat_pipeline(tiles, s_fn, pv_fn, skew=3)
                if g == 0:
                    dbg("o", ov, [oB[0][0], oB[0][1], oB[0][2], oB[0][3], oB[1][0], oB[1][1], oB[1][2], oB[1][3]], [128, 2 * T], BF16)
                if g == 3:
                    wo_partial(lambda m: ("awo", g, m),
                               [lambda n: o_ap(0, n), lambda n: o_ap(1, n)], [oB[0], oB[1]])

        def mixer_b():
            S.barrier()
            o = A0
            icnt23 = af(o, 4096); o += 4096
            icB = Buf(const=True)
            rfull = af(o, 2048); o += 2048
            rfB = [Buf() for _ in range(NB)]
            HW = T + 32
            hf = af(o, HW); o += HW
            la = af(o, HW); o += HW
            lb = af(o, HW); o += HW
            assert o <= IC01
            hfB, laB, lbB = Buf(), Buf(), Buf()
            if not pre["ic01"]:
                S.dma("sp", "tab2", DMA(arena[:, IC01:IC01 + 4096], d_icnt[:, 0:4096]), writes=[ic01B])
            pre["ic01"] = False
            S.dma("sp", "tab", DMA(arena[:, A0:A0 + 4096], d_icnt[:, 4096:8192]), writes=[icB])

            def icnt_w(w):
                if w < 2:
                    return arena[:, IC01 + w * T:IC01 + (w + 1) * T], ic01B
                return icnt23[:, (w - 2) * T:(w - 1) * T], icB
            S.op("dve", MSET(hf[:, 0:16], 0.0), writes=[hfB])
            S.op("dve", MSET(hf[:, 16 + T:HW], 0.0), writes=[hfB])
            for n in range(NB):
                stats_block(n, rfull[:, n * TB:(n + 1) * TB], rfB[n])
            for c in range(NCH):
                w = c // 2
                S.op("dve", STT(hf[:, 16:16 + T], xT[:, c * T:(c + 1) * T], col(C_ATTN + 8 + c), rfull[:, :], ALU.mult, ALU.mult),
                     reads=[xb[c][0], xb[c][1], xb[c][2], xb[c][3], rfB[0], rfB[1], rfB[2], rfB[3], colsB], writes=[hfB])
                S.op("dve", TT(la[:, 1:HW], hf[:, 0:HW - 1], hf[:, 1:HW], ALU.add), reads=[hfB], writes=[laB])
                fin, finB = la, laB
                if w >= 1:
                    S.op("dve", TT(lb[:, 2:HW - 1], la[:, 1:HW - 2], la[:, 3:HW], ALU.add), reads=[laB], writes=[lbB])
                    fin, finB = lb, lbB
                if w >= 2:
                    S.op("dve", TT(la[:, 4:HW - 3], lb[:, 2:HW - 5], lb[:, 6:HW - 1], ALU.add), reads=[lbB], writes=[laB])
                    fin, finB = la, laB
                if w >= 3:
                    S.op("dve", TT(lb[:, 8:HW - 7], la[:, 4:HW - 11], la[:, 12:HW - 3], ALU.add), reads=[laB], writes=[lbB])
                    fin, finB = lb, lbB
                ic_ap, ic_b = icnt_w(w)
                S.op("dve", TT(fin[:, 16:16 + T], fin[:, 16:16 + T], ic_ap, ALU.mult), reads=[finB, ic_b], writes=[finB])
                S.op("dve", TT(hT[:, c * T:(c + 1) * T], fin[:, 16:16 + T], hf[:, 16:16 + T], ALU.subtract),
                     reads=[finB, hfB], writes=[hb[c][0], hb[c][1], hb[c][2], hb[c][3]])
            for g in range(4):
                wt, wB = WR.get(("bp", g))
                for e in range(2):
                    base = 0 if e == 0 else 4
                    for kc in range(2):
                        for n in range(NB):
                            S.op("pe", MM(P[base + n][:, :], wt[:, kc * 256 + e * 128:kc * 256 + (e + 1) * 128], h_ap(2 * g + kc, n), kc == 0, kc == 1),
                                 reads=[wB, hb[2 * g + kc][n]], writes=[pb[base + n]])
                    add_to_x(2 * g + e, base, scale_col=col(C_BSCALE + 2 * g + e))

        def mixer_c():
            S.barrier()
            o = A0
            decv = [ab(o + 2048 * i, 2048) for i in range(2)]; o += 4096
            decB = [Buf() for _ in range(2)]
            qv = ab(o, 1024); o += 1024
            kv = ab(o, 1024); o += 1024
            vv = ab(o, 1024); o += 1024
            ov = ab(o, 2048); o += 2048
            NE, NEM = 4, 10
            Ev = [ab(o + 256 * i, 256) for i in range(NE)]; o += 256 * NE
            EB = [Buf() for _ in range(NE)]
            Emv = [ab(o + 256 * i, 256) for i in range(NEM)]; o += 256 * NEM
            EmB = [Buf() for _ in range(NEM)]
            f32t = [af(o + 512 * i, 512) for i in range(8)]; o += 4096
            sq2s = [ab(o, 256), ab(o + 256, 256)]; o += 512
            assert o <= ARENA_W, o
            uA, uB_, rAB, tt_, rr, rX = f32t[0:6]
            ods = f32t[6:8]
            uAB, uBB, rABB, ttB, rrB, rXB = (Buf() for _ in range(6))
            odBs = [Buf(), Buf()]
            sq2Bs = [Buf(), Buf()]
            qB = [Buf() for _ in range(NB)]
            kB = [Buf() for _ in range(NB)]
            vB = Buf()
            oB = [[Buf() for _ in range(NB)] for _ in range(2)]
            norm_to_h(C_ATTN + 16)

            def o_ap(hl, n):
                return ov[:, hl * T + n * TB:hl * T + (n + 1) * TB]

            neglam = dyn[:, 0:1]
            sgc = dyn[:, 1:2]
            for hp in range(4):
                for hl in range(2):
                    h = 2 * hp + hl
                    slope = 2.0 ** (-(h + 1))
                    S.dma("sp", "dec%d" % hl, DMA(decv[hl], d_dec[:, h * 4096:(h + 1) * 4096]), writes=[decB[hl]])
                    for kind in ("cq", "ck"):
                        wt, wB = WR.get((kind, h))
                        for n in range(NB):
                            for kc in range(NCH):
                                S.op("pe", MM(P[n][:, :], wt[:, kc * 128:(kc + 1) * 128], h_ap(kc, n), kc == 0, kc == NCH - 1),
                                     reads=[wB, hb[kc][n]], writes=[pb[n]])
                            if kind == "cq":
                                S.op("act", SMUL(qv[:, n * TB:(n + 1) * TB], P[n][:, :], 0.125), reads=[pb[n]], writes=[qB[n]])
                            else:
                                S.op("dve", CP(kv[:, n * TB:(n + 1) * TB], P[n][:, :]), reads=[pb[n]], writes=[kB[n]])
                        if kind == "cq":
                            drain_deferred()
                    wt, wB = WR.get(("cv", h))
                    for quarter in range(4):
                        bank = 4 + quarter
                        for j in range(4):
                            tc = quarter * 4 + j
                            for kc in range(NCH):
                                S.op("pe", MM(P[bank][:, j * 128:(j + 1) * 128], h_tok(kc, tc), wt[:, kc * 128:(kc + 1) * 128], kc == 0, kc == NCH - 1),
                                     reads=[wB, hb[kc][tc // 4]], writes=[pb[bank]])
                        S.op("dve", CP(vv[:, quarter * TB:(quarter + 1) * TB], P[bank][:, :]),
                             reads=[pb[bank]], writes=[vB])
                    tiles = []
                    for n in range(NB):
                        kept = []
                        for sc in range(16):
                            md = max(0, 128 * sc - (TB * n + TB - 1), TB * n - (128 * sc + 127))
                            if slope * md <= ALIBI_SKIP:
                                kept.append(sc)
                        for i_, sc in enumerate(kept):
                            tiles.append((n, sc, i_ == 0, i_ == len(kept) - 1))

                    def s_fn(tl, hl=hl):
                        n, sc, first, last = tl
                        banks = []
                        for comp in range(2):
                            banks.append(ctr["st"] % 4)
                            ctr["st"] += 1
                        off = n * TB - sc * 128 + 2048
                        for comp in range(2):
                            psl = slice(comp * 64, (comp + 1) * 64)
                            S.op("pe", MM(P[banks[comp]][:, :], kv[psl, sc * 128:(sc + 1) * 128], qv[psl, n * TB:(n + 1) * TB], True, True),
                                 reads=[kB[sc // 4], qB[n]], writes=[pb[banks[comp]]])
                        ems = []
                        for comp in range(2):
                            es = ctr["e"] % NE
                            ctr["e"] += 1
                            em = ctr["sb"] % NEM
                            ctr["sb"] += 1
                            S.op("act", ACT(Ev[es], P[banks[comp]][:, :], AF.Exp), reads=[pb[banks[comp]]], writes=[EB[es]])
                            S.op("dve", TT(Emv[em], Ev[es], decv[hl][:, off:off + TB], ALU.mult),
                                 reads=[EB[es], decB[hl]], writes=[EmB[em]])
                            ems.append(em)
                        return ems

                    def pv_fn(tl, ems, hl=hl):
                        n, sc, first, last = tl
                        for comp in range(2):
                            S.op("pe", MM(P[4 + comp][:, :], vv[:, sc * 128:(sc + 1) * 128], Emv[ems[comp]], first, last),
                                 reads=[EmB[ems[comp]], vB], writes=[pb[4 + comp]])
                        for comp in range(2):
                            S.op("pe", MM(P[6][comp * 64:(comp + 1) * 64, :], ones[:, 0:64], Emv[ems[comp]], first, last),
                                 reads=[EmB[ems[comp]], constB], writes=[pb[6]])
                        if not last:
                            return
                        od, odB, sq2, sq2B = ods[n % 2], odBs[n % 2], sq2s[n % 2], sq2Bs[n % 2]
                        S.op("dve", CP(uA, P[4][:, :]), reads=[pb[4]], writes=[uAB])
                        S.op("act", SCOPY(uB_, P[5][:, :]), reads=[pb[5]], writes=[uBB])
                        S.op("act", ACT(rAB, P[6][:, :], AF.Ln), reads=[pb[6]], writes=[rABB])
                        S.op("act", ACT(rAB, rAB, AF.Exp, scale=-1.0), reads=[rABB], writes=[rABB])

                        def part1():
                            S.op("dve", CP(rX[64:128, :], rAB[0:64, :]), reads=[rABB], writes=[rXB])
                            S.op("dve", CP(rX[0:64, :], rAB[64:128, :]), reads=[rABB], writes=[rXB])
                            S.op("dve", TT(od[0:64, :], uA[0:64, :], rAB[0:64, :], ALU.mult), reads=[uAB, rABB], writes=[odB])
                            S.op("dve", TT(od[64:128, :], uA[64:128, :], rX[64:128, :], ALU.mult), reads=[uAB, rXB], writes=[odB])

                        def part1b():
                            S.op("dve", STT(tt_[0:64, :], uB_[0:64, :], neglam[0:64, :], rX[0:64, :], ALU.mult, ALU.mult),
                                 reads=[uBB, rXB, dynB], writes=[ttB])
                            S.op("dve", STT(tt_[64:128, :], uB_[64:128, :], neglam[64:128, :], rAB[64:128, :], ALU.mult, ALU.mult),
                                 reads=[uBB, rABB, dynB], writes=[ttB])
                            S.op("dve", TT(od, od, tt_, ALU.add), reads=[odB, ttB], writes=[odB])

                        def part2(n=n, hl=hl):
                            S.op("act", ACT(sq2, od, AF.Square), reads=[odB], writes=[sq2B])
                            bank = ctr["st"] % 4
                            ctr["st"] += 1
                            S.op("pe", MM(P[bank][:, :], ones[:, :], sq2, True, True), reads=[sq2B, constB], writes=[pb[bank]])

                            def part3():
                                S.op("act", ACT(rr, P[bank][:, :], AF.Ln, bias=EPS, scale=1.0 / 128), reads=[pb[bank]], writes=[rrB])
                                S.op("act", ACT(rr, rr, AF.Exp, scale=-0.5), reads=[rrB], writes=[rrB])
                                S.op("dve", STT(o_ap(hl, n), od, sgc, rr, ALU.mult, ALU.mult), reads=[odB, rrB, dynB], writes=[oB[hl][n]])

                            deferred.append([1, part3])

                        deferred.append([1, part1])
                        deferred.append([3, part1b])
                        deferred.append([6, part2, True])

                    flat_pipeline(tiles, s_fn, pv_fn, skew=3, flush=False)
                drain_deferred()
                wo_partial(lambda m: ("cwo", hp, m),
                           [lambda n: o_ap(0, n), lambda n: o_ap(1, n)], [oB[0], oB[1]])

        def mixer_d():
            S.barrier()
            o = A0
            ABv = ab(o, 16384); o += 16384
            cscv = ab(o, 512); o += 512
            assert o <= ARENA_W
            cscB = Buf(const=True)
            ABB = [Buf() for _ in range(16)]
            S.dma("sp", "tab", DMA(cscv, d_csc[:, :]), writes=[cscB])
            norm_to_h(C_ATTN + 24)

            def AB_ap(tc, g):
                return ABv[:, (tc * 4 + g) * TB:(tc * 4 + g + 1) * TB]

            k = 0
            for tc in range(16):
                for g in range(4):
                    bank = k % 8
                    for kc in range(2):
                        S.op("pe", MM(P[bank][:, :], h_tok(2 * g + kc, tc), cscv[:, kc * TB:(kc + 1) * TB], kc == 0, kc == 1),
                             reads=[hb[2 * g + kc][tc // 4], cscB], writes=[pb[bank]])
                    if k % 2 == 0:
                        S.op("act", SCOPY(AB_ap(tc, g), P[bank][:, :]), reads=[pb[bank]], writes=[ABB[tc]])
                    else:
                        S.op("dve", CP(AB_ap(tc, g), P[bank][:, :]), reads=[pb[bank]], writes=[ABB[tc]])
                    k += 1
            for n in range(NB):
                for tc in range(16):
                    for cs in range(2):
                        tt, tB = TR.get((n, tc, cs))
                        for e in range(NCH):
                            g, eh = e // 2, e % 2
                            lo = (tc * 4 + g) * TB + cs * 256 + eh * 128
                            S.op("pe", MM(P[e][:, :], ABv[:, lo:lo + 128], tt, tc == 0 and cs == 0, tc == 15 and cs == 1),
                                 reads=[ABB[tc], tB], writes=[pb[e]])
                for e in range(NCH):
                    if e % 2 == 0:
                        S.op("act", SCOPY(h_ap(e, n), P[e][:, :]), reads=[pb[e]], writes=[hb[e][n]])
                    else:
                        S.op("dve", CP(h_ap(e, n), P[e][:, :]), reads=[pb[e]], writes=[hb[e][n]])
            srcs = [(lambda n, kc=kc: h_ap(kc, n)) for kc in range(NCH)]
            wo_partial(lambda m: ("dwo", m), srcs, [hb[kc] for kc in range(NCH)])

        store_toks = []
        for s in range(nseq):
            for n in range(NB):
                for c in range(NCH):
                    S.dma("sp", "xl%d_%d" % (c, n), DMA(x_ap(c, n), xin[s, c * 128:(c + 1) * 128, n * TB:(n + 1) * TB]),
                          writes=[xb[c][n]])
            for l in layers:
                (mixer_a, mixer_b, mixer_c, mixer_d)[l]()
                if do_ffn:
                    ffn(l)
            S.barrier()
            if final_norm:
                for n in range(NB):
                    r = n % 2
                    stats_block(n, rstd[r], rstdB[r])
                    for c in range(NCH):
                        S.op("dve", STT(x_ap(c, n), x_ap(c, n), col(C_FINAL + c), rstd[r], ALU.mult, ALU.mult),
                             reads=[xb[c][n], rstdB[r], colsB], writes=[xb[c][n]])
            for n in range(NB):
                for c in range(NCH):
                    tk = S.dma("sp", "xs%d_%d" % (c, n), DMA(yout[s, c * 128:(c + 1) * 128, n * TB:(n + 1) * TB], x_ap(c, n)),
                               reads=[xb[c][n]])
                    store_toks.append(tk)
        S.wait("sp", store_toks[-NCH * NB:])
        assert WR.pos == len(wstream) and TR.pos == len(tstream)
        S.emit(st)
    return nc


_CACHE = {}


def host_inputs(inp, layers=(0, 1, 2, 3)):
    offs, wcols = weight_offsets(layers)
    wts = np.zeros((128, wcols), np.float32)
    for key, o in offs.items():
        wts[:, o:o + tile_cols(key)] = extract_tile(key, inp)
    if "tabs" not in _CACHE:
        _CACHE["tabs"] = const_tables()
    tabs = _CACHE["tabs"]
    shared = {
        "wts": wts,
        "cols": build_cols(inp),
        "lamb": np.ascontiguousarray(np.broadcast_to(np.asarray(inp["c_lambda"][0], np.float32).reshape(1, 256), (128, 256))),
        "rope": tabs["rope"], "dec": tabs["dec"], "icnt": tabs["icnt"], "csc": tabs["csc"], "ttab": tabs["ttab"], "perm": tabs["perm"],
    }
    return shared


def kernel(**inputs):
    inp = {k: np.asarray(v) for k, v in inputs.items()}
    xs = np.concatenate([inp["x_prompt"], inp["x_sample"]], axis=0)
    nseq = SEQ_PER_CORE
    shared = host_inputs(inp)
    nc = build_program(nseq)
    in_maps = []
    for c in range(NCORES):
        xc = np.ascontiguousarray(xs[c * nseq:(c + 1) * nseq].transpose(0, 2, 1))
        m = dict(shared)
        m["xin"] = xc
        in_maps.append(m)
    res = run_bass_kernel_spmd(nc, in_maps, core_ids=list(range(NCORES)))
    ys = np.concatenate([np.asarray(r["yout"]).transpose(0, 2, 1) for r in res.results], axis=0)
    ys = np.ascontiguousarray(ys, dtype=np.float32)
    nb = inp["x_prompt"].shape[0]
    return ys[:nb], ys[nb:]
```

```python
import bisect
import math
from contextlib import ExitStack

import ml_dtypes
import numpy as np

import concourse.bass as bass
import concourse.mybir as mybir
from concourse.bass_utils import run_bass_kernel_spmd

F32 = mybir.dt.float32
BF16 = mybir.dt.bfloat16
ALU = mybir.AluOpType
AF = mybir.ActivationFunctionType
AX = mybir.AxisListType

T = 2048
D = 1024
NCH = 8
TB = 512
NB = 4
DFF = 2816
EPS = 1e-6
NCORES = 8
SEQ_PER_CORE = 5
LAM_INIT = 0.8 - 0.6 * math.exp(-0.3 * 2)
WSLOT = 1408
SKEW = 5
ALIBI_SKIP = 30.0
NWSLOT = 6
NTSLOT = 6
ARENA_W = 20480

C_ATTN = 0
C_FFN = 32
C_FINAL = 64
C_BSCALE = 72
C_GQ, C_GQS, C_GK, C_GKS, C_SUBG = 80, 81, 82, 83, 84
NCOL = 88


class Buf:
    __slots__ = ("lw", "rd", "const", "psum")

    def __init__(self, const=False, psum=False):
        self.lw = None
        self.rd = []
        self.const = const
        self.psum = psum


class Sched:
    ENGS = ("pe", "act", "dve", "pool", "sp")

    def __init__(self, nc):
        self.nc = nc
        self.ops = {e: [] for e in self.ENGS}
        self.seq = {e: 0 for e in self.ENGS}
        self.needed = {e: set() for e in self.ENGS}
        self.known = {e: {} for e in self.ENGS}
        self.dma_cnt = {}

    def _reduce(self, eng, toks, same_toks=()):
        best = {}
        for t in toks:
            kind, key, val = t
            if kind == "e" and key == eng:
                continue
            k = (kind, key)
            if val > best.get(k, -1):
                best[k] = val
        if eng in ("act", "dve", "pool"):
            for t in same_toks:
                if t is not None and t[0] == "e" and t[1] == eng:
                    k = ("e", eng)
                    if t[2] > best.get(k, -1):
                        best[k] = t[2]
        waits = []
        kn = self.known[eng]
        for k, val in best.items():
            if kn.get(k, -1) >= val:
                continue
            kn[k] = val
            waits.append((k[0], k[1], val))
            if k[0] == "e":
                self.needed[k[1]].add(val)
        return waits

    def _collect(self, eng, reads, writes):
        toks = []
        for b in reads:
            if b.lw is not None:
                toks.append(b.lw)
            if b.psum:
                toks.extend(b.rd)
        for b in writes:
            if b.lw is not None:
                toks.append(b.lw)
            toks.extend(b.rd)
        return self._reduce(eng, toks, toks)

    def _mark(self, tok, reads, writes):
        for b in writes:
            b.lw = tok
            b.rd = []
        for b in reads:
            if not b.const:
                b.rd.append(tok)

    def op(self, eng, fn, reads=(), writes=()):
        waits = self._collect(eng, reads, writes)
        self.seq[eng] += 1
        s = self.seq[eng]
        tok = ("e", eng, s)
        self.ops[eng].append((waits, fn, s, None))
        self._mark(tok, reads, writes)
        return tok

    def dma(self, eng, sem, fn, reads=(), writes=()):
        waits = self._collect(eng, reads, writes)
        self.dma_cnt[sem] = self.dma_cnt.get(sem, 0) + 16
        tok = ("d", sem, self.dma_cnt[sem])
        self.seq[eng] += 1
        self.ops[eng].append((waits, fn, self.seq[eng], sem))
        self._mark(tok, reads, writes)
        return tok

    def wait(self, eng, toks):
        waits = self._reduce(eng, [t for t in toks if t is not None])
        if waits:
            self.ops[eng].append((waits, None, None, None))

    def last_tok(self, eng):
        return ("e", eng, self.seq[eng]) if self.seq[eng] > 0 else None

    def barrier(self):
        toks = [self.last_tok(e) for e in ("pe", "act", "dve")]
        for e in ("pe", "act", "dve", "sp"):
            self.wait(e, toks)

    def emit(self, stack):
        nc = self.nc
        esem = {e: stack.enter_context(nc.semaphore("s_" + e)) for e in self.ENGS}
        dsem = {n: stack.enter_context(nc.semaphore("d_" + n)) for n in self.dma_cnt}
        ranks = {e: sorted(self.needed[e]) for e in self.ENGS}
        block = stack.enter_context(nc.Block())

        def run(e, handle):
            needed = self.needed[e]
            for waits, fn, s, dma in self.ops[e]:
                for kind, key, val in waits:
                    if kind == "e":
                        handle.wait_ge(esem[key], bisect.bisect_right(ranks[key], val))
                    else:
                        handle.wait_ge(dsem[key], val)
                if fn is None:
                    continue
                ins = fn(handle)
                if dma is not None:
                    ins.then_inc(dsem[dma], 16)
                elif s in needed:
                    ins.then_inc(esem[e], 1)

        if self.ops["pe"]:
            @block.tensor
            def _(h):
                run("pe", h)
        if self.ops["act"]:
            @block.scalar
            def _(h):
                run("act", h)
        if self.ops["dve"]:
            @block.vector
            def _(h):
                run("dve", h)
        if self.ops["pool"]:
            @block.gpsimd
            def _(h):
                run("pool", h)
        if self.ops["sp"]:
            @block.sync
            def _(h):
                run("sp", h)


def MM(out, lhsT, rhs, start, stop):
    return lambda e: e.matmul(out, lhsT, rhs, start=bool(start), stop=bool(stop))


def ACT(out, in_, func, bias=0.0, scale=1.0):
    return lambda e: e.activation(out, in_, func, bias=bias, scale=scale)


def TT(out, in0, in1, op):
    return lambda e: e.tensor_tensor(out, in0, in1, op)


def STT(out, in0, scalar, in1, op0, op1):
    return lambda e: e.scalar_tensor_tensor(out, in0, scalar, in1, op0, op1)


def TS(out, in0, s1, s2, op0, op1=None):
    if op1 is None:
        return lambda e: e.tensor_scalar(out, in0, s1, None, op0)
    return lambda e: e.tensor_scalar(out, in0, s1, s2, op0, op1)


def CP(out, in_):
    return lambda e: e.tensor_copy(out, in_)


def SMUL(out, in_, c):
    return lambda e: e.mul(out, in_, c)


def SCOPY(out, in_):
    return lambda e: e.copy(out, in_)


def RCP(out, in_):
    return lambda e: e.reciprocal(out, in_)


def MSET(ap, v):
    return lambda e: e.memset(ap, v)


def DMA(out, in_):
    return lambda e: e.dma_start(out=out, in_=in_)


class Ring:
    def __init__(self, S, eng, prefix, slot_ap_fn, nslots, stream):
        self.S, self.eng, self.prefix = S, eng, prefix
        self.slot_ap = slot_ap_fn
        self.n = nslots
        self.stream = stream
        self.pos = 0
        self.loaded = 0
        self.bufs = [Buf() for _ in range(nslots)]

    def get(self, key):
        i = self.pos
        k, _, ncols = self.stream[i]
        assert k == key, (k, key)
        lim = min(len(self.stream), i + self.n - 1)
        while self.loaded < lim:
            j = self.loaded
            _, src, nc_ = self.stream[j]
            slot = j % self.n
            self.S.dma(self.eng, "%s%d" % (self.prefix, slot), DMA(self.slot_ap(slot, nc_), src),
                       writes=[self.bufs[slot]])
            self.loaded += 1
        self.pos += 1
        slot = i % self.n
        return self.slot_ap(slot, ncols), self.bufs[slot]


def ffn_catalog(l):
    L = []
    for G in range(2):
        for j in range(11):
            jj = G * 11 + j
            L += [("gate", l, jj), ("up", l, jj)]
        for m in range(8):
            L.append(("down", l, m, G))
    return L


def catalog(layers=(0, 1, 2, 3)):
    L = []
    for l in layers:
        if l == 0:
            for g in range(4):
                L += [("aq", g, 0, 0), ("aq", g, 1, 0), ("ak", g, 0), ("av", g)]
                if g >= 1:
                    L += [("awo", g - 1, m) for m in range(8)]
            L += [("awo", 3, m) for m in range(8)]
        elif l == 1:
            L += [("bp", g) for g in range(4)]
        elif l == 2:
            for hp in range(4):
                for hl in range(2):
                    h = 2 * hp + hl
                    L += [("cq", h), ("ck", h), ("cv", h)]
                L += [("cwo", hp, m) for m in range(8)]
        else:
            L += [("dwo", m) for m in range(8)]
        L += ffn_catalog(l)
    return L


def tile_cols(key):
    k = key[0]
    if k in ("gate", "up", "aq", "ak", "cq", "ck", "cv", "dwo"):
        return 1024
    if k == "down":
        return 1408
    if k in ("av", "bp"):
        return 512
    if k in ("awo", "cwo"):
        return 256
    raise KeyError(key)


def _kc_tile(w):
    K, M = w.shape
    return np.ascontiguousarray(w.reshape(K // 128, 128, M).transpose(1, 0, 2)).reshape(128, -1)


_SWAP64 = np.arange(64) ^ 1


def extract_tile(key, W):
    k = key[0]
    if k == "gate":
        return _kc_tile(W["w_gate"][key[1]][:, key[2] * 128:(key[2] + 1) * 128])
    if k == "up":
        return _kc_tile(W["w_up"][key[1]][:, key[2] * 128:(key[2] + 1) * 128])
    if k == "down":
        _, l, m, G = key
        return _kc_tile(W["w_down"][l][G * 1408:(G + 1) * 1408, m * 128:(m + 1) * 128])
    if k == "aq":
        _, g, cc, sw = key
        cols = (2 * g + cc) * 128 + np.arange(128)
        if sw:
            cols = cols ^ 1
        return _kc_tile(W["a_w_qkv"][0][:, cols])
    if k == "ak":
        _, g, sw = key
        c64 = np.arange(64)
        if sw:
            c64 = c64 ^ 1
        cols = 1024 + g * 64 + np.concatenate([c64, c64])
        return _kc_tile(W["a_w_qkv"][0][:, cols])
    if k == "av":
        g = key[1]
        cols = 1280 + g * 64 + np.arange(64)
        return _kc_tile(W["a_w_qkv"][0][:, cols])
    if k == "awo":
        _, g, m = key
        return _kc_tile(W["a_w_o"][0][g * 256:(g + 1) * 256, m * 128:(m + 1) * 128])
    if k == "bp":
        g = key[1]
        return _kc_tile(W["b_w_pool"][0][g])
    if k == "cq":
        h = key[1]
        return _kc_tile(W["c_w_qkv"][0][:, h * 128:(h + 1) * 128])
    if k == "ck":
        h = key[1]
        return _kc_tile(W["c_w_qkv"][0][:, 1024 + h * 128:1024 + (h + 1) * 128])
    if k == "cv":
        h = key[1]
        return _kc_tile(W["c_w_qkv"][0][:, 2048 + h * 128:2048 + (h + 1) * 128])
    if k == "cwo":
        _, hp, m = key
        return _kc_tile(W["c_w_o"][0][hp * 256:(hp + 1) * 256, m * 128:(m + 1) * 128])
    if k == "dwo":
        m = key[1]
        return _kc_tile(W["d_w_o"][0][:, m * 128:(m + 1) * 128])
    raise KeyError(key)


def weight_offsets(layers):
    offs = {}
    o = 0
    for key in catalog(layers):
        if key not in offs:
            offs[key] = o
            o += tile_cols(key)
    return offs, o


def const_tables():
    tabs = {}
    p = np.arange(128)
    d = p % 64
    i = d // 2
    t = np.arange(T)
    r = (t // 64).astype(np.float32)
    c = (t % 64).astype(np.float32)
    n = 16
    freqs = (np.float32(10000.0) ** (-np.arange(n, dtype=np.float32) / np.float32(n))).astype(np.float32)
    ang = np.where((i < 16)[:, None], r[None, :] * freqs[np.minimum(i, 15)][:, None],
                   c[None, :] * freqs[np.maximum(i - 16, 0)][:, None]).astype(np.float32)
    cosv = np.cos(ang.astype(np.float64))
    sinv = np.sin(ang.astype(np.float64))
    sgn = np.where(d % 2 == 0, -1.0, 1.0)[:, None]
    tabs["rope"] = np.concatenate([cosv, sinv * sgn], axis=1).astype(np.float32)
    cc = np.arange(4096)
    dist = np.abs(cc[None, :] - 2048 - p[:, None]).astype(np.float64)
    tabs["dec"] = np.concatenate([np.exp(-(2.0 ** (-(h + 1))) * dist) for h in range(8)], axis=1).astype(ml_dtypes.bfloat16)
    ic = np.zeros((4, T), np.float64)
    for gi, win in enumerate((2, 4, 8, 16)):
        lo = np.clip(t - win // 2, 0, T - 1)
        hi = np.clip(t + win // 2 - 1, 0, T - 1)
        ic[gi] = 1.0 / (hi - lo + 1)
    tabs["icnt"] = np.ascontiguousarray(np.broadcast_to(ic.reshape(1, 4 * T), (128, 4 * T))).astype(np.float32)
    cidx = (np.arange(2)[None, :, None] * 128 + p[:, None, None])
    e = np.arange(256)[None, None, :]
    a = 2.0 * np.pi * ((cidx * e) % 256) / 256.0
    csc = np.concatenate([np.cos(a), np.sin(a)], axis=2) / 16.0
    tabs["csc"] = csc.reshape(128, 1024).astype(ml_dtypes.bfloat16)
    pm = np.zeros((128, 128), np.float32)
    pm[np.arange(128) ^ 1, np.arange(128)] = 1.0
    tabs["perm"] = pm.astype(ml_dtypes.bfloat16)
    tt = np.zeros((128, NB, 16, 2, TB), np.float32)
    sc_ = 1.0 / math.sqrt(T)
    for tc in range(16):
        trow = (tc * 128 + p)[:, None]
        ang2 = 2.0 * np.pi * ((trow * t[None, :]) % T) / T
        cm = (np.cos(ang2) * sc_).astype(np.float32).reshape(128, NB, TB)
        sm = (-np.sin(ang2) * sc_).astype(np.float32).reshape(128, NB, TB)
        tt[:, :, tc, 0, :] = cm
        tt[:, :, tc, 1, :] = sm
    tabs["ttab"] = tt.reshape(128, NB * 16 * 2 * TB).astype(ml_dtypes.bfloat16)
    return tabs


def build_cols(inp):
    cols = np.zeros((128, NCOL), np.float32)

    def colify(v):
        return np.asarray(v, np.float32).reshape(8, 128).T

    for l in range(4):
        cols[:, C_ATTN + 8 * l:C_ATTN + 8 * l + 8] = colify(inp["attn_norm"][l])
        cols[:, C_FFN + 8 * l:C_FFN + 8 * l + 8] = colify(inp["ffn_norm"][l])
    cols[:, C_FINAL:C_FINAL + 8] = colify(inp["final_norm"])
    cols[:, C_BSCALE:C_BSCALE + 8] = colify(inp["b_scale"][0])
    d = np.arange(128) % 64
    cols[:, C_GQ] = inp["a_q_gain"][0][d]
    cols[:, C_GQS] = inp["a_q_gain"][0][d ^ 1]
    cols[:, C_GK] = inp["a_k_gain"][0][d]
    cols[:, C_GKS] = inp["a_k_gain"][0][d ^ 1]
    cols[:, C_SUBG] = inp["c_sub_gain"][0]
    return cols


def build_program(nseq, layers=(0, 1, 2, 3), do_ffn=True, final_norm=True, debug=False):
    nc = bass.Bass("TRN2", target_bir_lowering=False)
    dbg_n = [0]
    offs, wcols = weight_offsets(layers)
    xin = nc.dram_tensor("xin", [nseq, D, T], F32, kind="ExternalInput").ap()
    yout = nc.dram_tensor("yout", [nseq, D, T], F32, kind="ExternalOutput").ap()
    wts = nc.dram_tensor("wts", [128, wcols], F32, kind="ExternalInput").ap()
    d_cols = nc.dram_tensor("cols", [128, NCOL], F32, kind="ExternalInput").ap()
    d_lamb = nc.dram_tensor("lamb", [128, 256], F32, kind="ExternalInput").ap()
    d_rope = nc.dram_tensor("rope", [128, 4096], F32, kind="ExternalInput").ap()
    d_dec = nc.dram_tensor("dec", [128, 8 * 4096], BF16, kind="ExternalInput").ap()
    d_icnt = nc.dram_tensor("icnt", [128, 4 * T], F32, kind="ExternalInput").ap()
    d_csc = nc.dram_tensor("csc", [128, 1024], BF16, kind="ExternalInput").ap()
    d_perm = nc.dram_tensor("perm", [128, 128], BF16, kind="ExternalInput").ap()
    d_ttab = nc.dram_tensor("ttab", [128, NB * 16 * 2 * TB], BF16, kind="ExternalInput").ap()

    st = ExitStack()
    with st:
        xT = st.enter_context(nc.sbuf_tensor("xT", [128, NCH * T], F32))
        hT = st.enter_context(nc.sbuf_tensor("hT", [128, NCH * T], BF16))
        wring = st.enter_context(nc.sbuf_tensor("wring", [128, NWSLOT * WSLOT], BF16))
        tring = st.enter_context(nc.sbuf_tensor("tring", [128, NTSLOT * TB], BF16))
        cols = st.enter_context(nc.sbuf_tensor("colst", [128, NCOL], F32))
        dyn = st.enter_context(nc.sbuf_tensor("dyn", [128, 8], F32))
        lamt = st.enter_context(nc.sbuf_tensor("lamt", [128, 256], F32))
        lamp = st.enter_context(nc.sbuf_tensor("lamp", [128, 128], F32))
        ones = st.enter_context(nc.sbuf_tensor("ones", [128, 128], BF16))
        bdiag = st.enter_context(nc.sbuf_tensor("bdiag", [128, 128], BF16))
        permt = st.enter_context(nc.sbuf_tensor("permt", [128, 128], BF16))
        arena = st.enter_context(nc.sbuf_tensor("arena", [128, ARENA_W], F32))
        P = [st.enter_context(nc.psum_tensor("ps%d" % i, [128, TB], F32)) for i in range(8)]

        S = Sched(nc)
        pb = [Buf(psum=True) for _ in range(8)]
        xb = [[Buf() for _ in range(NB)] for _ in range(NCH)]
        hb = [[Buf() for _ in range(NB)] for _ in range(NCH)]
        colsB = Buf(const=True)
        constB = Buf(const=True)
        lamB = Buf()

        def x_ap(c, n):
            return xT[:, c * T + n * TB:c * T + (n + 1) * TB]

        def h_ap(c, n):
            return hT[:, c * T + n * TB:c * T + (n + 1) * TB]

        def h_tok(c, tc):
            return hT[:, c * T + tc * 128:c * T + (tc + 1) * 128]

        def col(i):
            return cols[:, i:i + 1]

        def af(o, n):
            return arena[:, o:o + n]

        def ab(o, n):
            return arena[:, o:o + n].bitcast(BF16)

        sqb = [ab(256 * i, 256) for i in range(4)]
        sqB = [Buf() for _ in range(4)]
        tsq = [af(1024 + 512 * i, 512) for i in range(2)]
        tsqB = [Buf() for _ in range(2)]
        rstd = [af(2048 + 512 * i, 512) for i in range(2)]
        rstdB = [Buf() for _ in range(2)]
        A0 = 3072

        cat = catalog(layers) if do_ffn else [k for k in catalog(layers) if k[0] not in ("gate", "up", "down")]
        wstream = []
        for _ in range(nseq):
            for key in cat:
                ncl = tile_cols(key)
                wstream.append((key, wts[:, offs[key]:offs[key] + ncl], ncl))
        WR = Ring(S, "pool", "w", lambda s, n_: wring[:, s * WSLOT:s * WSLOT + n_], NWSLOT, wstream)
        tstream = []
        if 3 in layers:
            for _ in range(nseq):
                for n in range(NB):
                    for tc in range(16):
                        for cs in range(2):
                            o = ((n * 16 + tc) * 2 + cs) * TB
                            tstream.append(((n, tc, cs), d_ttab[:, o:o + TB], TB))
        TR = Ring(S, "sp", "t", lambda s, n_: tring[:, s * TB:s * TB + n_], NTSLOT, tstream)

        ctr = {"st": 0, "e": 0, "sb": 0, "nb": 0}
        deferred = []
        IC01 = 16384
        ic01B = Buf(const=True)
        pre = {"ic01": False}

        S.dma("sp", "c0", DMA(cols[:], d_cols[:, :]), writes=[colsB])
        S.dma("sp", "c1", DMA(lamt[:], d_lamb[:, :]), writes=[lamB])
        S.dma("sp", "c2", DMA(permt[:], d_perm[:, :]), writes=[constB])
        S.op("dve", MSET(ones[:], 1.0), writes=[constB])
        S.op("dve", MSET(bdiag[:], 0.0), writes=[constB])
        S.op("dve", MSET(bdiag[0:64, 0:64], 1.0), writes=[constB])
        S.op("dve", MSET(bdiag[64:128, 64:128], 1.0), writes=[constB])
        lpB = Buf()
        S.op("dve", TT(lamp[:, 0:64], lamt[:, 0:64], lamt[:, 64:128], ALU.mult), reads=[lamB], writes=[lpB])
        S.op("dve", TT(lamp[:, 64:128], lamt[:, 128:192], lamt[:, 192:256], ALU.mult), reads=[lamB], writes=[lpB])
        dynB = Buf()
        S.op("dve", lambda e: e.reduce_sum(dyn[:, 2:3], lamp[:, 0:64], AX.X), reads=[lpB], writes=[dynB])
        S.op("dve", lambda e: e.reduce_sum(dyn[:, 3:4], lamp[:, 64:128], AX.X), reads=[lpB], writes=[dynB])
        S.op("act", ACT(dyn[:, 4:6], dyn[:, 2:4], AF.Exp), reads=[dynB], writes=[dynB])
        S.op("dve", TT(dyn[:, 6:7], dyn[:, 4:5], dyn[:, 5:6], ALU.subtract), reads=[dynB], writes=[dynB])
        S.op("dve", TS(dyn[:, 0:1], dyn[:, 6:7], LAM_INIT, -1.0, ALU.add, ALU.mult), reads=[dynB], writes=[dynB])
        S.op("dve", TS(dyn[:, 1:2], col(C_SUBG), 1.0 - LAM_INIT, None, ALU.mult), reads=[dynB, colsB], writes=[dynB])
        dynB.const = True

        def dbg(name, ap, bufs, shape, dt):
            if not debug or dbg_n[0] > 12:
                return
            dbg_n[0] += 1
            dten = nc.dram_tensor("dbg_" + name, shape, dt, kind="ExternalOutput").ap()
            S.dma("sp", "dbg%d" % dbg_n[0], DMA(dten[:, :], ap), reads=bufs)

        def stats_block(n, dst, dstB, inv_n=1.0 / D, src=None):
            bank = 6 + (ctr["nb"] % 2)
            ctr["nb"] += 1
            for c in range(NCH):
                q = c % 4
                S.op("act", ACT(sqb[q], x_ap(c, n), AF.Square), reads=[xb[c][n]], writes=[sqB[q]])
                S.op("pe", MM(P[bank][:, :], ones[:, :], sqb[q], c == 0, c == NCH - 1),
                     reads=[sqB[q], constB], writes=[pb[bank]])
            tq = ctr["nb"] % 2
            S.op("act", ACT(tsq[tq], P[bank][:, :], AF.Ln, bias=EPS, scale=inv_n), reads=[pb[bank]], writes=[tsqB[tq]])
            S.op("act", ACT(dst, tsq[tq], AF.Exp, scale=-0.5), reads=[tsqB[tq]], writes=[dstB])

        def norm_to_h(gcol0):
            for n in range(NB):
                r = n % 2
                stats_block(n, rstd[r], rstdB[r])
                for c in range(NCH):
                    S.op("dve", STT(h_ap(c, n), x_ap(c, n), col(gcol0 + c), rstd[r], ALU.mult, ALU.mult),
                         reads=[xb[c][n], rstdB[r], colsB], writes=[hb[c][n]])

        def add_to_x(m, base, scale_col=None):
            for n in range(NB):
                if scale_col is None:
                    S.op("dve", TT(x_ap(m, n), x_ap(m, n), P[base + n][:, :], ALU.add),
                         reads=[pb[base + n], xb[m][n]], writes=[xb[m][n]])
                else:
                    S.op("dve", STT(x_ap(m, n), P[base + n][:, :], scale_col, x_ap(m, n), ALU.mult, ALU.add),
                         reads=[pb[base + n], xb[m][n], colsB], writes=[xb[m][n]])

        def wo_partial(keyfn, srcs, srcBs):
            nk = len(srcs)
            for m in range(NCH):
                wt, wB = WR.get(keyfn(m))
                base = 0 if m % 2 == 0 else 4
                for kc in range(nk):
                    for n in range(NB):
                        S.op("pe", MM(P[base + n][:, :], wt[:, kc * 128:(kc + 1) * 128], srcs[kc](n), kc == 0, kc == nk - 1),
                             reads=[wB, srcBs[kc][n]], writes=[pb[base + n]])
                add_to_x(m, base)

        def drain_deferred():
            while deferred:
                d_ = deferred.pop(0)
                d_[1]()

        def flat_pipeline(tiles, s_fn, pv_fn, skew=SKEW, flush=True):
            hs = {}
            nt = len(tiles)
            for t in range(nt + skew):
                if t < nt:
                    hs[t] = s_fn(tiles[t])
                if t >= skew:
                    pv_fn(tiles[t - skew], hs.pop(t - skew))
                    for d_ in list(deferred):
                        d_[0] -= 1
                        if d_[0] <= 0:
                            deferred.remove(d_)
                            d_[1]()
            if flush:
                drain_deferred()
            else:
                keep = []
                while deferred:
                    d_ = deferred.pop(0)
                    if len(d_) > 2 and d_[2]:
                        keep.append(d_)
                    else:
                        d_[1]()
                deferred.extend(keep)

        def ffn(l):
            S.barrier()
            if l == 0 and 1 in layers:
                S.dma("sp", "tab2", DMA(arena[:, IC01:IC01 + 4096], d_icnt[:, 0:4096]), writes=[ic01B])
                pre["ic01"] = True
            norm_to_h(C_FFN + 8 * l)
            act_o = A0
            actv = ab(act_o, 11264)
            actB = [[Buf() for _ in range(NB)] for _ in range(11)]
            sg_o = A0 + 11264
            sg = [af(sg_o + 512 * i, 512) for i in range(4)]
            sgB = [Buf() for _ in range(4)]

            def act_ap(j, n):
                return actv[:, j * T + n * TB:j * T + (n + 1) * TB]

            for G in range(2):
                for j in range(11):
                    jj = G * 11 + j
                    gt, gB = WR.get(("gate", l, jj))
                    ut, uB = WR.get(("up", l, jj))
                    for kc in range(NCH):
                        for n in range(NB):
                            S.op("pe", MM(P[n][:, :], gt[:, kc * 128:(kc + 1) * 128], h_ap(kc, n), kc == 0, kc == NCH - 1),
                                 reads=[gB, hb[kc][n]], writes=[pb[n]])
                    for kc in range(NCH):
                        for n in range(NB):
                            S.op("pe", MM(P[4 + n][:, :], ut[:, kc * 128:(kc + 1) * 128], h_ap(kc, n), kc == 0, kc == NCH - 1),
                                 reads=[uB, hb[kc][n]], writes=[pb[4 + n]])
                    for n in range(NB):
                        S.op("act", ACT(sg[n], P[n][:, :], AF.Silu), reads=[pb[n]], writes=[sgB[n]])
                        S.op("dve", TT(act_ap(j, n), sg[n], P[4 + n][:, :], ALU.mult),
                             reads=[sgB[n], pb[4 + n]], writes=[actB[j][n]])
                for m in range(NCH):
                    dt_, dB = WR.get(("down", l, m, G))
                    base = 0 if m % 2 == 0 else 4
                    for j in range(11):
                        for n in range(NB):
                            S.op("pe", MM(P[base + n][:, :], dt_[:, j * 128:(j + 1) * 128], act_ap(j, n), j == 0, j == 10),
                                 reads=[dB, actB[j][n]], writes=[pb[base + n]])
                    add_to_x(m, base)

        def mixer_a():
            S.barrier()
            o = A0
            ropeC = af(o, 2048); o += 2048
            ropeS = af(o, 2048); o += 2048
            ropeB = Buf(const=True)
            qv = ab(o, 2048); o += 2048
            kv = ab(o, 1024); o += 1024
            vv = ab(o, 1024); o += 1024
            ov = ab(o, 2048); o += 2048
            NEA = 8
            Ev = [ab(o + 256 * i, 256) for i in range(NEA)]; o += 256 * NEA
            EB = [Buf() for _ in range(NEA)]
            sq2 = [ab(o + 256 * i, 256) for i in range(2)]; o += 512
            sq2B = [Buf() for _ in range(2)]
            abf = [ab(o + 256 * i, 256) for i in range(2)]; o += 512
            abfB = [Buf() for _ in range(2)]
            f32t = [af(o + 512 * i, 512) for i in range(7)]; o += 3584
            assert o <= ARENA_W, o
            rs2, t1v, rcv = f32t[0:2], f32t[2:4], f32t[4:6]
            t2v = [f32t[6], f32t[6]]
            rs2B, t1B, rcB = ([Buf() for _ in range(2)] for _ in range(3))
            t2B_ = Buf()
            t2B = [t2B_, t2B_]
            qB = [[Buf() for _ in range(NB)] for _ in range(2)]
            kB = [Buf() for _ in range(NB)]
            vB = Buf()
            oB = [[Buf() for _ in range(NB)] for _ in range(2)]
            S.dma("sp", "tab", DMA(arena[:, A0:A0 + 4096], d_rope[:, :]), writes=[ropeB])
            vv3 = vv.rearrange("p (t e) -> p t e", e=128)
            S.op("dve", MSET(vv3[:, :, 64:128], 1.0), writes=[vB])
            norm_to_h(C_ATTN + 0)

            def q_ap(cc, n):
                return qv[:, cc * T + n * TB:cc * T + (n + 1) * TB]

            def o_ap(cc, n):
                return ov[:, cc * T + n * TB:cc * T + (n + 1) * TB]

            for g in range(4):
                ptiles = [(kind, cc, n) for kind, cc in (("q", 0), ("q", 1), ("k", 0)) for n in range(NB)]
                wcur = {}

                def stage1(i):
                    kind, cc, n = ptiles[i]
                    r = i % 2
                    PA = 0 if r == 0 else 3
                    if n == 0:
                        wcur[0] = WR.get(("aq", g, cc, 0)) if kind == "q" else WR.get(("ak", g, 0))
                    wa, waB = wcur[0]
                    for kc in range(NCH):
                        S.op("pe", MM(P[PA][:, :], wa[:, kc * 128:(kc + 1) * 128], h_ap(kc, n), kc == 0, kc == NCH - 1),
                             reads=[waB, hb[kc][n]], writes=[pb[PA]])
                    S.op("act", ACT(sq2[r], P[PA][:, :], AF.Square), reads=[pb[PA]], writes=[sq2B[r]])
                    S.op("act", SCOPY(abf[r], P[PA][:, :]), reads=[pb[PA]], writes=[abfB[r]])

                def stage2(i):
                    kind, cc, n = ptiles[i]
                    r = i % 2
                    PA, PB_, PR = (0, 1, 2) if r == 0 else (3, 4, 5)
                    if kind == "q":
                        gc, gsc, sc_ = col(C_GQ), col(C_GQS), 0.125
                    else:
                        gc, gsc, sc_ = col(C_GK), col(C_GKS), 1.0
                    S.op("pe", MM(P[PB_][:, :], permt[:, :], abf[r], True, True), reads=[abfB[r], constB], writes=[pb[PB_]])
                    S.op("pe", MM(P[PR][:, :], bdiag[:, :], sq2[r], True, True), reads=[sq2B[r], constB], writes=[pb[PR]])
                    S.op("act", ACT(rs2[r], P[PR][:, :], AF.Ln, bias=EPS, scale=1.0 / 64), reads=[pb[PR]], writes=[rs2B[r]])
                    S.op("act", ACT(rs2[r], rs2[r], AF.Exp, scale=-0.5), reads=[rs2B[r]], writes=[rs2B[r]])
                    S.op("dve", STT(t1v[r], P[PA][:, :], gc, ropeC[:, n * TB:(n + 1) * TB], ALU.mult, ALU.mult),
                         reads=[pb[PA], ropeB, colsB], writes=[t1B[r]])
                    S.op("dve", STT(t2v[r], P[PB_][:, :], gsc, ropeS[:, n * TB:(n + 1) * TB], ALU.mult, ALU.mult),
                         reads=[pb[PB_], ropeB, colsB], writes=[t2B[r]])
                    S.op("dve", TT(t1v[r], t1v[r], t2v[r], ALU.add), reads=[t1B[r], t2B[r]], writes=[t1B[r]])
                    if kind == "q":
                        S.op("dve", STT(q_ap(cc, n), t1v[r], sc_, rs2[r], ALU.mult, ALU.mult), reads=[t1B[r], rs2B[r]], writes=[qB[cc][n]])
                    else:
                        S.op("dve", STT(kv[:, n * TB:(n + 1) * TB], t1v[r], sc_, rs2[r], ALU.mult, ALU.mult),
                             reads=[t1B[r], rs2B[r]], writes=[kB[n]])

                for i in range(len(ptiles) + 1):
                    if i < len(ptiles):
                        stage1(i)
                    if i >= 1:
                        stage2(i - 1)
                wv, wvB = WR.get(("av", g))
                for half in range(2):
                    bank = 6 + half
                    for j in range(8):
                        tc = half * 8 + j
                        for kc in range(NCH):
                            S.op("pe", MM(P[bank][:, j * 64:(j + 1) * 64], h_tok(kc, tc), wv[:, kc * 64:(kc + 1) * 64], kc == 0, kc == NCH - 1),
                                 reads=[wvB, hb[kc][tc // 4]], writes=[pb[bank]])
                    S.op("act", SCOPY(vv3[:, half * 8:(half + 1) * 8, 0:64], P[bank][:, :].rearrange("p (t e) -> p t e", e=64)),
                         reads=[pb[bank]], writes=[vB])
                if g == 0:
                    dbg("q", qv, [qB[0][0], qB[0][1], qB[0][2], qB[0][3], qB[1][0], qB[1][1], qB[1][2], qB[1][3]], [128, 2 * T], BF16)
                    dbg("k", kv, kB, [128, T], BF16)
                    dbg("v", vv, [vB], [128, T], BF16)
                    dbg("h", hT[:, :], [hb[c_][n_] for c_ in range(NCH) for n_ in range(NB)], [128, NCH * T], BF16)
                if g >= 1:
                    wo_partial(lambda m: ("awo", g - 1, m),
                               [lambda n: o_ap(0, n), lambda n: o_ap(1, n)], [oB[0], oB[1]])
                tiles = []
                for cc in range(2):
                    for n in range(NB):
                        ub = 4 + 2 * (ctr["sb"] % 2)
                        ctr["sb"] += 1
                        for sc in range(16):
                            tiles.append((cc, n, sc, ub))

                def s_fn(tl):
                    cc, n, sc, ub = tl
                    hs = []
                    banks = []
                    for ph in range(2):
                        banks.append(ctr["st"] % 4)
                        ctr["st"] += 1
                    for ph in range(2):
                        psl = slice(ph * 64, (ph + 1) * 64)
                        S.op("pe", MM(P[banks[ph]][:, :], kv[psl, sc * 128:(sc + 1) * 128], qv[psl, cc * T + n * TB:cc * T + (n + 1) * TB], True, True),
                             reads=[kB[sc // 4], qB[cc][n]], writes=[pb[banks[ph]]])
                    for ph in range(2):
                        es = ctr["e"] % NEA
                        ctr["e"] += 1
                        S.op("act", ACT(Ev[es], P[banks[ph]][:, :], AF.Exp), reads=[pb[banks[ph]]], writes=[EB[es]])
                        hs.append(es)
                    return hs

                def pv_fn(tl, hs):
                    cc, n, sc, ub = tl
                    for ph in range(2):
                        U = ub + ph
                        S.op("pe", MM(P[U][:, :], vv3[:, sc, :], Ev[hs[ph]], sc == 0, sc == 15),
                             reads=[EB[hs[ph]], vB], writes=[pb[U]])
                    if sc == 15:
                        for ph in range(2):
                            U = ub + ph
                            psl = slice(ph * 64, (ph + 1) * 64)
                            S.op("dve", RCP(rcv[ph][64:128, :], P[U][64:128, :]), reads=[pb[U]], writes=[rcB[ph]])
                            S.op("dve", TT(ov[psl, cc * T + n * TB:cc * T + (n + 1) * TB], P[U][0:64, :], rcv[ph][64:128, :], ALU.mult),
                                 reads=[pb[U], rcB[ph]], writes=[oB[cc][n]])

                flat_pipeline(tiles, s_fn, pv_fn, skew=3)
                if g == 0:
                    dbg("o", ov, [oB[0][0], oB[0][1], oB[0][2], oB[0][3], oB[1][0], oB[1][1], oB[1][2], oB[1][3]], [128, 2 * T], BF16)
                if g == 3:
                    wo_partial(lambda m: ("awo", g, m),
                               [lambda n: o_ap(0, n), lambda n: o_ap(1, n)], [oB[0], oB[1]])

        def mixer_b():
            S.barrier()
            o = A0
            icnt23 = af(o, 4096); o += 4096
            icB = Buf(const=True)
            rfull = af(o, 2048); o += 2048
            rfB = [Buf() for _ in range(NB)]
            HW = T + 32
            hf = af(o, HW); o += HW
            la = af(o, HW); o += HW
            lb = af(o, HW); o += HW
            assert o <= IC01
            hfB, laB, lbB = Buf(), Buf(), Buf()
            if not pre["ic01"]:
                S.dma("sp", "tab2", DMA(arena[:, IC01:IC01 + 4096], d_icnt[:, 0:4096]), writes=[ic01B])
            pre["ic01"] = False
            S.dma("sp", "tab", DMA(arena[:, A0:A0 + 4096], d_icnt[:, 4096:8192]), writes=[icB])

            def icnt_w(w):
                if w < 2:
                    return arena[:, IC01 + w * T:IC01 + (w + 1) * T], ic01B
                return icnt23[:, (w - 2) * T:(w - 1) * T], icB
            S.op("dve", MSET(hf[:, 0:16], 0.0), writes=[hfB])
            S.op("dve", MSET(hf[:, 16 + T:HW], 0.0), writes=[hfB])
            for n in range(NB):
                stats_block(n, rfull[:, n * TB:(n + 1) * TB], rfB[n])
            for c in range(NCH):
                w = c // 2
                S.op("dve", STT(hf[:, 16:16 + T], xT[:, c * T:(c + 1) * T], col(C_ATTN + 8 + c), rfull[:, :], ALU.mult, ALU.mult),
                     reads=[xb[c][0], xb[c][1], xb[c][2], xb[c][3], rfB[0], rfB[1], rfB[2], rfB[3], colsB], writes=[hfB])
                S.op("dve", TT(la[:, 1:HW], hf[:, 0:HW - 1], hf[:, 1:HW], ALU.add), reads=[hfB], writes=[laB])
                fin, finB = la, laB
                if w >= 1:
                    S.op("dve", TT(lb[:, 2:HW - 1], la[:, 1:HW - 2], la[:, 3:HW], ALU.add), reads=[laB], writes=[lbB])
                    fin, finB = lb, lbB
                if w >= 2:
                    S.op("dve", TT(la[:, 4:HW - 3], lb[:, 2:HW - 5], lb[:, 6:HW - 1], ALU.add), reads=[lbB], writes=[laB])
                    fin, finB = la, laB
                if w >= 3:
                    S.op("dve", TT(lb[:, 8:HW - 7], la[:, 4:HW - 11], la[:, 12:HW - 3], ALU.add), reads=[laB], writes=[lbB])
                    fin, finB = lb, lbB
                ic_ap, ic_b = icnt_w(w)
                S.op("dve", TT(fin[:, 16:16 + T], fin[:, 16:16 + T], ic_ap, ALU.mult), reads=[finB, ic_b], writes=[finB])
                S.op("dve", TT(hT[:, c * T:(c + 1) * T], fin[:, 16:16 + T], hf[:, 16:16 + T], ALU.subtract),
                     reads=[finB, hfB], writes=[hb[c][0], hb[c][1], hb[c][2], hb[c][3]])
            for g in range(4):
                wt, wB = WR.get(("bp", g))
                for e in range(2):
                    base = 0 if e == 0 else 4
                    for kc in range(2):
                        for n in range(NB):
                            S.op("pe", MM(P[base + n][:, :], wt[:, kc * 256 + e * 128:kc * 256 + (e + 1) * 128], h_ap(2 * g + kc, n), kc == 0, kc == 1),
                                 reads=[wB, hb[2 * g + kc][n]], writes=[pb[base + n]])
                    add_to_x(2 * g + e, base, scale_col=col(C_BSCALE + 2 * g + e))

        def mixer_c():
            S.barrier()
            o = A0
            decv = [ab(o + 2048 * i, 2048) for i in range(2)]; o += 4096
            decB = [Buf() for _ in range(2)]
            qv = ab(o, 1024); o += 1024
            kv = ab(o, 1024); o += 1024
            vv = ab(o, 1024); o += 1024
            ov = ab(o, 2048); o += 2048
            NE, NEM = 4, 10
            Ev = [ab(o + 256 * i, 256) for i in range(NE)]; o += 256 * NE
            EB = [Buf() for _ in range(NE)]
            Emv = [ab(o + 256 * i, 256) for i in range(NEM)]; o += 256 * NEM
            EmB = [Buf() for _ in range(NEM)]
            f32t = [af(o + 512 * i, 512) for i in range(8)]; o += 4096
            sq2s = [ab(o, 256), ab(o + 256, 256)]; o += 512
            assert o <= ARENA_W, o
            uA, uB_, rAB, tt_, rr, rX = f32t[0:6]
            ods = f32t[6:8]
            uAB, uBB, rABB, ttB, rrB, rXB = (Buf() for _ in range(6))
            odBs = [Buf(), Buf()]
            sq2Bs = [Buf(), Buf()]
            qB = [Buf() for _ in range(NB)]
            kB = [Buf() for _ in range(NB)]
            vB = Buf()
            oB = [[Buf() for _ in range(NB)] for _ in range(2)]
            norm_to_h(C_ATTN + 16)

            def o_ap(hl, n):
                return ov[:, hl * T + n * TB:hl * T + (n + 1) * TB]

            neglam = dyn[:, 0:1]
            sgc = dyn[:, 1:2]
            for hp in range(4):
                for hl in range(2):
                    h = 2 * hp + hl
                    slope = 2.0 ** (-(h + 1))
                    S.dma("sp", "dec%d" % hl, DMA(decv[hl], d_dec[:, h * 4096:(h + 1) * 4096]), writes=[decB[hl]])
                    for kind in ("cq", "ck"):
                        wt, wB = WR.get((kind, h))
                        for n in range(NB):
                            for kc in range(NCH):
                                S.op("pe", MM(P[n][:, :], wt[:, kc * 128:(kc + 1) * 128], h_ap(kc, n), kc == 0, kc == NCH - 1),
                                     reads=[wB, hb[kc][n]], writes=[pb[n]])
                            if kind == "cq":
                                S.op("act", SMUL(qv[:, n * TB:(n + 1) * TB], P[n][:, :], 0.125), reads=[pb[n]], writes=[qB[n]])
                            else:
                                S.op("dve", CP(kv[:, n * TB:(n + 1) * TB], P[n][:, :]), reads=[pb[n]], writes=[kB[n]])
                        if kind == "cq":
                            drain_deferred()
                    wt, wB = WR.get(("cv", h))
                    for quarter in range(4):
                        bank = 4 + quarter
                        for j in range(4):
                            tc = quarter * 4 + j
                            for kc in range(NCH):
                                S.op("pe", MM(P[bank][:, j * 128:(j + 1) * 128], h_tok(kc, tc), wt[:, kc * 128:(kc + 1) * 128], kc == 0, kc == NCH - 1),
                                     reads=[wB, hb[kc][tc // 4]], writes=[pb[bank]])
                        S.op("dve", CP(vv[:, quarter * TB:(quarter + 1) * TB], P[bank][:, :]),
                             reads=[pb[bank]], writes=[vB])
                    tiles = []
                    for n in range(NB):
                        kept = []
                        for sc in range(16):
                            md = max(0, 128 * sc - (TB * n + TB - 1), TB * n - (128 * sc + 127))
                            if slope * md <= ALIBI_SKIP:
                                kept.append(sc)
                        for i_, sc in enumerate(kept):
                            tiles.append((n, sc, i_ == 0, i_ == len(kept) - 1))

                    def s_fn(tl, hl=hl):
                        n, sc, first, last = tl
                        banks = []
                        for comp in range(2):
                            banks.append(ctr["st"] % 4)
                            ctr["st"] += 1
                        off = n * TB - sc * 128 + 2048
                        for comp in range(2):
                            psl = slice(comp * 64, (comp + 1) * 64)
                            S.op("pe", MM(P[banks[comp]][:, :], kv[psl, sc * 128:(sc + 1) * 128], qv[psl, n * TB:(n + 1) * TB], True, True),
                                 reads=[kB[sc // 4], qB[n]], writes=[pb[banks[comp]]])
                        ems = []
                        for comp in range(2):
                            es = ctr["e"] % NE
                            ctr["e"] += 1
                            em = ctr["sb"] % NEM
                            ctr["sb"] += 1
                            S.op("act", ACT(Ev[es], P[banks[comp]][:, :], AF.Exp), reads=[pb[banks[comp]]], writes=[EB[es]])
                            S.op("dve", TT(Emv[em], Ev[es], decv[hl][:, off:off + TB], ALU.mult),
                                 reads=[EB[es], decB[hl]], writes=[EmB[em]])
                            ems.append(em)
                        return ems

                    def pv_fn(tl, ems, hl=hl):
                        n, sc, first, last = tl
                        for comp in range(2):
                            S.op("pe", MM(P[4 + comp][:, :], vv[:, sc * 128:(sc + 1) * 128], Emv[ems[comp]], first, last),
                                 reads=[EmB[ems[comp]], vB], writes=[pb[4 + comp]])
                        for comp in range(2):
                            S.op("pe", MM(P[6][comp * 64:(comp + 1) * 64, :], ones[:, 0:64], Emv[ems[comp]], first, last),
                                 reads=[EmB[ems[comp]], constB], writes=[pb[6]])
                        if not last:
                            return
                        od, odB, sq2, sq2B = ods[n % 2], odBs[n % 2], sq2s[n % 2], sq2Bs[n % 2]
                        S.op("dve", CP(uA, P[4][:, :]), reads=[pb[4]], writes=[uAB])
                        S.op("act", SCOPY(uB_, P[5][:, :]), reads=[pb[5]], writes=[uBB])
                        S.op("act", ACT(rAB, P[6][:, :], AF.Ln), reads=[pb[6]], writes=[rABB])
                        S.op("act", ACT(rAB, rAB, AF.Exp, scale=-1.0), reads=[rABB], writes=[rABB])

                        def part1():
                            S.op("dve", CP(rX[64:128, :], rAB[0:64, :]), reads=[rABB], writes=[rXB])
                            S.op("dve", CP(rX[0:64, :], rAB[64:128, :]), reads=[rABB], writes=[rXB])
                            S.op("dve", TT(od[0:64, :], uA[0:64, :], rAB[0:64, :], ALU.mult), reads=[uAB, rABB], writes=[odB])
                            S.op("dve", TT(od[64:128, :], uA[64:128, :], rX[64:128, :], ALU.mult), reads=[uAB, rXB], writes=[odB])

                        def part1b():
                            S.op("dve", STT(tt_[0:64, :], uB_[0:64, :], neglam[0:64, :], rX[0:64, :], ALU.mult, ALU.mult),
                                 reads=[uBB, rXB, dynB], writes=[ttB])
                            S.op("dve", STT(tt_[64:128, :], uB_[64:128, :], neglam[64:128, :], rAB[64:128, :], ALU.mult, ALU.mult),
                                 reads=[uBB, rABB, dynB], writes=[ttB])
                            S.op("dve", TT(od, od, tt_, ALU.add), reads=[odB, ttB], writes=[odB])

                        def part2(n=n, hl=hl):
                            S.op("act", ACT(sq2, od, AF.Square), reads=[odB], writes=[sq2B])
                            bank = ctr["st"] % 4
                            ctr["st"] += 1
                            S.op("pe", MM(P[bank][:, :], ones[:, :], sq2, True, True), reads=[sq2B, constB], writes=[pb[bank]])

                            def part3():
                                S.op("act", ACT(rr, P[bank][:, :], AF.Ln, bias=EPS, scale=1.0 / 128), reads=[pb[bank]], writes=[rrB])
                                S.op("act", ACT(rr, rr, AF.Exp, scale=-0.5), reads=[rrB], writes=[rrB])
                                S.op("dve", STT(o_ap(hl, n), od, sgc, rr, ALU.mult, ALU.mult), reads=[odB, rrB, dynB], writes=[oB[hl][n]])

                            deferred.append([1, part3])

                        deferred.append([1, part1])
                        deferred.append([3, part1b])
                        deferred.append([6, part2, True])

                    flat_pipeline(tiles, s_fn, pv_fn, skew=3, flush=False)
                drain_deferred()
                wo_partial(lambda m: ("cwo", hp, m),
                           [lambda n: o_ap(0, n), lambda n: o_ap(1, n)], [oB[0], oB[1]])

        def mixer_d():
            S.barrier()
            o = A0
            ABv = ab(o, 16384); o += 16384
            cscv = ab(o, 512); o += 512
            assert o <= ARENA_W
            cscB = Buf(const=True)
            ABB = [Buf() for _ in range(16)]
            S.dma("sp", "tab", DMA(cscv, d_csc[:, :]), writes=[cscB])
            norm_to_h(C_ATTN + 24)

            def AB_ap(tc, g):
                return ABv[:, (tc * 4 + g) * TB:(tc * 4 + g + 1) * TB]

            k = 0
            for tc in range(16):
                for g in range(4):
                    bank = k % 8
                    for kc in range(2):
                        S.op("pe", MM(P[bank][:, :], h_tok(2 * g + kc, tc), cscv[:, kc * TB:(kc + 1) * TB], kc == 0, kc == 1),
                             reads=[hb[2 * g + kc][tc // 4], cscB], writes=[pb[bank]])
                    if k % 2 == 0:
                        S.op("act", SCOPY(AB_ap(tc, g), P[bank][:, :]), reads=[pb[bank]], writes=[ABB[tc]])
                    else:
                        S.op("dve", CP(AB_ap(tc, g), P[bank][:, :]), reads=[pb[bank]], writes=[ABB[tc]])
                    k += 1
            for n in range(NB):
                for tc in range(16):
                    for cs in range(2):
                        tt, tB = TR.get((n, tc, cs))
                        for e in range(NCH):
                            g, eh = e // 2, e % 2
                            lo = (tc * 4 + g) * TB + cs * 256 + eh * 128
                            S.op("pe", MM(P[e][:, :], ABv[:, lo:lo + 128], tt, tc == 0 and cs == 0, tc == 15 and cs == 1),
                                 reads=[ABB[tc], tB], writes=[pb[e]])
                for e in range(NCH):
                    if e % 2 == 0:
                        S.op("act", SCOPY(h_ap(e, n), P[e][:, :]), reads=[pb[e]], writes=[hb[e][n]])
                    else:
                        S.op("dve", CP(h_ap(e, n), P[e][:, :]), reads=[pb[e]], writes=[hb[e][n]])
            srcs = [(lambda n, kc=kc: h_ap(kc, n)) for kc in range(NCH)]
            wo_partial(lambda m: ("dwo", m), srcs, [hb[kc] for kc in range(NCH)])

        store_toks = []
        for s in range(nseq):
            for n in range(NB):
                for c in range(NCH):
                    S.dma("sp", "xl%d_%d" % (c, n), DMA(x_ap(c, n), xin[s, c * 128:(c + 1) * 128, n * TB:(n + 1) * TB]),
                          writes=[xb[c][n]])
            for l in layers:
                (mixer_a, mixer_b, mixer_c, mixer_d)[l]()
                if do_ffn:
                    ffn(l)
            S.barrier()
            if final_norm:
                for n in range(NB):
                    r = n % 2
                    stats_block(n, rstd[r], rstdB[r])
                    for c in range(NCH):
                        S.op("dve", STT(x_ap(c, n), x_ap(c, n), col(C_FINAL + c), rstd[r], ALU.mult, ALU.mult),
                             reads=[xb[c][n], rstdB[r], colsB], writes=[xb[c][n]])
            for n in range(NB):
                for c in range(NCH):
                    tk = S.dma("sp", "xs%d_%d" % (c, n), DMA(yout[s, c * 128:(c + 1) * 128, n * TB:(n + 1) * TB], x_ap(c, n)),
                               reads=[xb[c][n]])
                    store_toks.append(tk)
        S.wait("sp", store_toks[-NCH * NB:])
        assert WR.pos == len(wstream) and TR.pos == len(tstream)
        S.emit(st)
    return nc


_CACHE = {}


def host_inputs(inp, layers=(0, 1, 2, 3)):
    offs, wcols = weight_offsets(layers)
    wts = np.zeros((128, wcols), np.float32)
    for key, o in offs.items():
        wts[:, o:o + tile_cols(key)] = extract_tile(key, inp)
    if "tabs" not in _CACHE:
        _CACHE["tabs"] = const_tables()
    tabs = _CACHE["tabs"]
    shared = {
        "wts": wts,
        "cols": build_cols(inp),
        "lamb": np.ascontiguousarray(np.broadcast_to(np.asarray(inp["c_lambda"][0], np.float32).reshape(1, 256), (128, 256))),
        "rope": tabs["rope"], "dec": tabs["dec"], "icnt": tabs["icnt"], "csc": tabs["csc"], "ttab": tabs["ttab"], "perm": tabs["perm"],
    }
    return shared


def kernel(**inputs):
    inp = {k: np.asarray(v) for k, v in inputs.items()}
    xs = np.concatenate([inp["x_prompt"], inp["x_sample"]], axis=0)
    nseq = SEQ_PER_CORE
    shared = host_inputs(inp)
    nc = build_program(nseq)
    in_maps = []
    for c in range(NCORES):
        xc = np.ascontiguousarray(xs[c * nseq:(c + 1) * nseq].transpose(0, 2, 1))
        m = dict(shared)
        m["xin"] = xc
        in_maps.append(m)
    res = run_bass_kernel_spmd(nc, in_maps, core_ids=list(range(NCORES)))
    ys = np.concatenate([np.asarray(r["yout"]).transpose(0, 2, 1) for r in res.results], axis=0)
    ys = np.ascontiguousarray(ys, dtype=np.float32)
    nb = inp["x_prompt"].shape[0]
    return ys[:nb], ys[nb:]
```
